# Optimizing a Trainium2 kernel written in Bass

```python
import math
import jax, jax.numpy as jnp
from jax import lax
import numpy as np

D_MODEL = 1024
BATCH = 16
SEQ = 4096
DEPTH = 4

CHUNK = 64
D_MIX = D_MODEL
D_HGRN = D_MIX // 2
HGRN_HEADS = 4
HGRN_HEAD_DIM = D_HGRN // HGRN_HEADS
D_SB = D_MIX - D_HGRN
SB_HEADS = 8
SB_HEAD_DIM = D_SB // SB_HEADS
Q_BLOCK = 128
IN_SPLITS = (D_HGRN, 2 * D_HGRN, 3 * D_HGRN, 4 * D_HGRN, 4 * D_HGRN + D_SB, 4 * D_HGRN + 2 * D_SB)
D_IN_PROJ = 4 * D_HGRN + 3 * D_SB
D_FF_DENSE = 2816
N_EXPERTS = 8
TOP_K = 2
D_FF_EXPERT = 3584
N_DENSE = (DEPTH + 1) // 2
N_MOE = DEPTH // 2
DEEPNORM_ALPHA = (2 * DEPTH) ** 0.25
DEEPNORM_BETA = (8 * DEPTH) ** -0.25
LN_EPS = 1e-5
RMS_EPS = 1e-6

kernel_name = "hybrid_hgrn2_stickbreaking_moe_deepnorm"


def layer_norm(x, gain, bias):
    xf = x.astype(jnp.float32)
    mu = jnp.mean(xf, axis=-1, keepdims=True)
    var = jnp.mean(jnp.square(xf - mu), axis=-1, keepdims=True)
    return ((xf - mu) * lax.rsqrt(var + LN_EPS)).astype(x.dtype) * gain + bias


def ada_modulation(c, w, b):
    m = (jax.nn.silu(c) @ w + b)[:, None, :]
    return jnp.split(m, 3, axis=-1)


def hgrn_lower_bounds(lb_logits):
    p = jax.nn.softmax(lb_logits.astype(jnp.float32), axis=0)
    return (jnp.cumsum(p, axis=0) - p[0:1]).astype(lb_logits.dtype)


def hgrn2_mixer(q, f_logit, i, g, lb, norm_gain):
    B, S, _ = q.shape
    n_chunks = S // CHUNK
    f = lb + (1.0 - lb) * jax.nn.sigmoid(f_logit)
    log_f = jnp.log(f)
    k = 1.0 - f
    q = jax.nn.silu(q)

    def to_chunks(t):
        return t.reshape(B, n_chunks, CHUNK, HGRN_HEADS, HGRN_HEAD_DIM).transpose(1, 0, 3, 2, 4)

    causal = jnp.tril(jnp.ones((CHUNK, CHUNK), dtype=bool))[:, :, None]

    def step(state, inp):
        qc, gc, kc, vc = inp
        b = jnp.cumsum(gc, axis=2)
        o_inter = jnp.einsum('bhck,bhkv->bhcv', qc * jnp.exp(b), state)
        diff = b[:, :, :, None, :] - b[:, :, None, :, :]
        decay = jnp.exp(jnp.where(causal, diff, -jnp.inf))
        scores = jnp.sum(qc[:, :, :, None, :] * decay * kc[:, :, None, :, :], axis=-1)
        o_intra = jnp.einsum('bhts,bhsv->bhtv', scores, vc)
        b_last = b[:, :, -1:, :]
        k_dec = kc * jnp.exp(b_last - b)
        state = jnp.exp(b_last)[:, :, 0, :, None] * state + jnp.einsum('bhsk,bhsv->bhkv', k_dec, vc)
        return state, o_inter + o_intra

    state0 = jnp.zeros((B, HGRN_HEADS, HGRN_HEAD_DIM, HGRN_HEAD_DIM), dtype=q.dtype)
    _, o = lax.scan(step, state0, (to_chunks(q), to_chunks(log_f), to_chunks(k), to_chunks(i)))
    o = o.transpose(1, 0, 3, 2, 4).reshape(B, S, HGRN_HEADS, HGRN_HEAD_DIM)
    of = o.astype(jnp.float32)
    o = (of * lax.rsqrt(jnp.mean(jnp.square(of), axis=-1, keepdims=True) + RMS_EPS)).astype(q.dtype)
    return o.reshape(B, S, D_HGRN) * norm_gain * jax.nn.silu(g)


def stick_breaking_mixer(q, k, v):
    B, S, _ = q.shape

    def heads(t):
        return t.reshape(B, S, SB_HEADS, SB_HEAD_DIM).transpose(0, 2, 1, 3)

    q, k, v = heads(q), heads(k), heads(v)
    scale = SB_HEAD_DIM ** -0.5
    outs = []
    for start in range(0, S, Q_BLOCK):
        end = start + Q_BLOCK
        qb, kb, vb = q[:, :, start:end], k[:, :, :end], v[:, :, :end]
        z = jnp.einsum('bhqd,bhkd->bhqk', qb, kb).astype(jnp.float32) * scale
        t_pos = start + jnp.arange(Q_BLOCK)[:, None]
        s_pos = jnp.arange(end)[None, :]
        strict = s_pos < t_pos
        log_not = jnp.where(strict, jax.nn.log_sigmoid(-z), 0.0)
        after = lax.cumsum(log_not, axis=3, reverse=True) - log_not
        a = jnp.where(strict, jnp.exp(jax.nn.log_sigmoid(z) + after), 0.0)
        outs.append(jnp.einsum('bhqk,bhkd->bhqd', a.astype(vb.dtype), vb))
    o = jnp.concatenate(outs, axis=2)
    return o.transpose(0, 2, 1, 3).reshape(B, S, D_SB)


def token_mixer(h, w_in, w_out, lb, hgrn_norm_gain):
    proj = h @ w_in
    hq, hf, hi, hg, sq, sk, sv = jnp.split(proj, IN_SPLITS, axis=-1)
    o_hgrn = hgrn2_mixer(hq, hf, hi, hg, lb, hgrn_norm_gain)
    o_sb = stick_breaking_mixer(sq, sk, sv)
    return jnp.concatenate([o_hgrn, o_sb], axis=-1) @ w_out


def swiglu(h, wg, wu, wd):
    return (jax.nn.silu(h @ wg) * (h @ wu)) @ wd


def moe_swiglu(h, w_router, wg, wu, wd):
    logits = (h @ w_router).astype(jnp.float32)
    top_v, top_i = lax.top_k(logits, TOP_K)
    top_w = jax.nn.softmax(top_v, axis=-1).astype(h.dtype)
    combine = jnp.sum(jax.nn.one_hot(top_i, N_EXPERTS, dtype=h.dtype) * top_w[..., None], axis=-2)
    y = jnp.zeros_like(h)
    for e in range(N_EXPERTS):
        y = y + combine[..., e:e + 1] * swiglu(h, wg[e], wu[e], wd[e])
    return y


def setup_inputs(seed: int = 0) -> dict:
    key = jax.random.key(seed)
    ks = jax.random.split(key, 18)
    f32 = jnp.float32
    beta = DEEPNORM_BETA
    x = jax.random.normal(ks[0], (BATCH, SEQ, D_MODEL), f32)
    c = jax.random.normal(ks[1], (BATCH, D_MODEL), f32)
    w_ada = jax.random.normal(ks[2], (DEPTH, 2, D_MODEL, 3 * D_MODEL), f32) * (0.5 * D_MODEL ** -0.5)
    b_ada = jax.random.normal(ks[3], (DEPTH, 2, 3 * D_MODEL), f32) * 0.02
    col_scale = jnp.concatenate([
        jnp.ones((2 * D_HGRN,), f32), jnp.full((D_HGRN,), beta, f32),
        jnp.ones((D_HGRN + 2 * D_SB,), f32), jnp.full((D_SB,), beta, f32)])
    w_in = jax.random.normal(ks[4], (DEPTH, D_MODEL, D_IN_PROJ), f32) * (D_MODEL ** -0.5) * col_scale
    w_out = jax.random.normal(ks[5], (DEPTH, D_MIX, D_MODEL), f32) * (D_MIX ** -0.5) * beta
    hgrn_lb_logits = jax.random.normal(ks[6], (DEPTH, D_HGRN), f32)
    hgrn_norm_gain = 1.0 + 0.02 * jax.random.normal(ks[7], (DEPTH, D_HGRN), f32)
    w_dense_gate = jax.random.normal(ks[8], (N_DENSE, D_MODEL, D_FF_DENSE), f32) * (D_MODEL ** -0.5) * beta
    w_dense_up = jax.random.normal(ks[9], (N_DENSE, D_MODEL, D_FF_DENSE), f32) * (D_MODEL ** -0.5) * beta
    w_dense_down = jax.random.normal(ks[10], (N_DENSE, D_FF_DENSE, D_MODEL), f32) * (D_FF_DENSE ** -0.5) * beta
    w_router = jax.random.normal(ks[11], (N_MOE, D_MODEL, N_EXPERTS), f32) * (D_MODEL ** -0.5)
    w_moe_gate = jax.random.normal(ks[12], (N_MOE, N_EXPERTS, D_MODEL, D_FF_EXPERT), f32) * (D_MODEL ** -0.5) * beta
    w_moe_up = jax.random.normal(ks[13], (N_MOE, N_EXPERTS, D_MODEL, D_FF_EXPERT), f32) * (D_MODEL ** -0.5) * beta
    w_moe_down = jax.random.normal(ks[14], (N_MOE, N_EXPERTS, D_FF_EXPERT, D_MODEL), f32) * (D_FF_EXPERT ** -0.5) * beta
    ln_gain = 1.0 + 0.02 * jax.random.normal(ks[15], (DEPTH, 2, D_MODEL), f32)
    ln_bias = 0.02 * jax.random.normal(ks[16], (DEPTH, 2, D_MODEL), f32)
    return {"x": x, "c": c, "w_ada": w_ada, "b_ada": b_ada, "w_in": w_in, "w_out": w_out,
            "hgrn_lb_logits": hgrn_lb_logits, "hgrn_norm_gain": hgrn_norm_gain,
            "w_dense_gate": w_dense_gate, "w_dense_up": w_dense_up, "w_dense_down": w_dense_down,
            "w_router": w_router, "w_moe_gate": w_moe_gate, "w_moe_up": w_moe_up, "w_moe_down": w_moe_down,
            "ln_gain": ln_gain, "ln_bias": ln_bias}


def reference(x, c, w_ada, b_ada, w_in, w_out, hgrn_lb_logits, hgrn_norm_gain,
              w_dense_gate, w_dense_up, w_dense_down, w_router, w_moe_gate, w_moe_up, w_moe_down,
              ln_gain, ln_bias):
    lb_all = hgrn_lower_bounds(hgrn_lb_logits)
    for layer in range(DEPTH):
        shift, scale, gate = ada_modulation(c, w_ada[layer, 0], b_ada[layer, 0])
        h = x * (1.0 + scale) + shift
        y = token_mixer(h, w_in[layer], w_out[layer], lb_all[layer], hgrn_norm_gain[layer])
        x = layer_norm(DEEPNORM_ALPHA * x + gate * y, ln_gain[layer, 0], ln_bias[layer, 0])
        shift, scale, gate = ada_modulation(c, w_ada[layer, 1], b_ada[layer, 1])
        h = x * (1.0 + scale) + shift
        if layer % 2 == 0:
            j = layer // 2
            y = swiglu(h, w_dense_gate[j], w_dense_up[j], w_dense_down[j])
        else:
            j = layer // 2
            y = moe_swiglu(h, w_router[j], w_moe_gate[j], w_moe_up[j], w_moe_down[j])
        x = layer_norm(DEEPNORM_ALPHA * x + gate * y, ln_gain[layer, 1], ln_bias[layer, 1])
    return x
```

```python
import contextlib
import numpy as np
import ml_dtypes
import concourse.bass as bass
import concourse.mybir as mybir
from concourse.bass_utils import run_bass_kernel_spmd

F32 = mybir.dt.float32
BF16 = mybir.dt.bfloat16
AF = mybir.ActivationFunctionType
ALU = mybir.AluOpType
AX = mybir.AxisListType

D = 1024
DEPTH = 4
DH = 512
NH_H = 4
DSB = 512
NH_S = 8
DIN = 3584
FF_DENSE = 2816
FF_MOE = 3584
NE = 8
ALPHA = float((2 * DEPTH) ** 0.25)
LN_EPS = 1e-5
RMS_EPS = 1e-6
CHUNK = 64
NCORES = 8
DBG = {}


class Buf:
    __slots__ = ("name", "w", "r", "dsem", "dcnt", "psum")

    def __init__(self, name="", psum=False):
        self.name = name
        self.psum = psum
        self.w = None
        self.r = []
        self.dsem = None
        self.dcnt = 0


class _Eng:
    def __init__(self, name, sem):
        self.name = name
        self.sem = sem
        self.count = 0
        self.ops = []
        self.waited = {}


class Prog:
    ENG = ("pe", "act", "dve", "pool", "sp")

    def __init__(self, nc, name):
        self.nc = nc
        self.name = name
        self.stack = contextlib.ExitStack()
        self.eng = {}
        self.sems = []
        for e in self.ENG:
            sem = nc.alloc_semaphore(name=f"{name}_{e}")
            self.sems.append(sem)
            self.eng[e] = _Eng(e, sem)
        self.dma_toks = []
        self.nsem = 5

    def sbuf(self, name, shape, dt):
        return self.stack.enter_context(self.nc.sbuf_tensor(f"{self.name}_{name}", list(shape), dt))

    def psum(self, name, shape, dt=F32):
        return self.stack.enter_context(self.nc.psum_tensor(f"{self.name}_{name}", list(shape), dt))

    def _wait(self, e, tok):
        if tok is None:
            return
        sem, val, owner = tok
        if owner == "pe" and e.name == "pe":
            return
        key = id(sem)
        if e.waited.get(key, 0) >= val:
            return
        e.waited[key] = val
        e.ops.append(lambda eng, s=sem, v=val: eng.wait_ge(s, v))

    def _deps(self, e, reads, writes, extra):
        for b in reads:
            self._wait(e, b.w)
            if b.psum:
                for t in b.r:
                    if t[2] != e.name:
                        self._wait(e, t)
        for b in writes:
            self._wait(e, b.w)
            for t in b.r:
                self._wait(e, t)
        for t in extra:
            self._wait(e, t)

    def op(self, engine, fn, reads=(), writes=(), extra=(), signal=True):
        e = self.eng[engine]
        self._deps(e, reads, writes, extra)
        if signal:
            e.count += 1
            tok = (e.sem, e.count, engine)
            e.ops.append(lambda eng, f=fn, s=e.sem: f(eng).then_inc(s, 1))
        else:
            tok = (e.sem, e.count + 1, engine)
            e.ops.append(lambda eng, f=fn: f(eng))
        for b in writes:
            b.w = tok
            b.r = []
        for b in reads:
            b.r.append(tok)
        return tok

    def dma(self, queue, out, in_, owner, reads=(), writes=(), extra=(), **kw):
        e = self.eng[queue]
        self._deps(e, reads, writes, extra)
        if owner.dsem is None:
            owner.dsem = self.nc.alloc_semaphore(name=f"{self.name}_d{self.nsem}")
            self.sems.append(owner.dsem)
            self.nsem += 1
            owner.dcnt = 0
        owner.dcnt += 16
        tok = (owner.dsem, owner.dcnt, "dma")
        e.ops.append(lambda eng, o=out, i=in_, s=owner.dsem, k=kw: eng.dma_start(out=o, in_=i, **k).then_inc(s, 16))
        for b in writes:
            b.w = tok
            b.r = []
        for b in reads:
            b.r.append(tok)
        self.dma_toks.append(tok)
        return tok

    def run(self):
        nc = self.nc
        sp = self.eng["sp"]
        for t in self.dma_toks:
            self._wait(sp, t)
        with nc.Block() as block:
            @block.tensor
            def _(t):
                for f in self.eng["pe"].ops:
                    f(t)

            @block.scalar
            def _(a):
                for f in self.eng["act"].ops:
                    f(a)

            @block.vector
            def _(v):
                for f in self.eng["dve"].ops:
                    f(v)

            @block.gpsimd
            def _(g):
                for f in self.eng["pool"].ops:
                    f(g)

            @block.sync
            def _(s):
                for f in self.eng["sp"].ops:
                    f(s)
        nc.clear_and_free_semaphores(self.sems)
        nc.all_engine_barrier()
        self.stack.close()


class Cfg:
    def __init__(self, nseq=2, S=4096, layers=(0, 1, 2, 3), debug=()):
        self.nseq = nseq
        self.S = S
        self.ntok = nseq * S
        self.layers = tuple(layers)
        self.debug = tuple(debug)


def bcast_rows(ap1d, nparts):
    return ap1d.partition_broadcast(nparts)


class Builder:
    def __init__(self, cfg):
        self.cfg = cfg
        nc = self.nc = bass.Bass("TRN2", target_bir_lowering=False)
        NT = cfg.ntok
        ns = cfg.nseq

        def din(name, shape, dt=F32):
            return nc.dram_tensor(name, list(shape), dt, kind="ExternalInput").ap()

        def scratch(name, shape, dt=F32):
            kind = "ExternalOutput" if name in cfg.debug else "Internal"
            return nc.dram_tensor(name, list(shape), dt, kind=kind).ap()

        self.x_in = din("x", [NT, D])
        self.cT = din("cT", [128, 8, ns])
        self.w_ada = din("w_ada", [DEPTH, 2, D, 3 * D])
        self.b_ada = din("b_ada", [DEPTH, 2, 3 * D])
        self.w_in = din("w_in", [DEPTH, D, DIN])
        self.w_out = din("w_out", [DEPTH, D, D])
        self.lbT = din("lbT", [128, NH_H, DEPTH])
        self.hgain = din("hgain", [DEPTH, DH])
        self.wdg = din("w_dense_gate", [2, D, FF_DENSE])
        self.wdu = din("w_dense_up", [2, D, FF_DENSE])
        self.wdd = din("w_dense_down", [2, FF_DENSE, D])
        self.w_router = din("w_router", [2, 128, 8, NE])
        self.wmg = din("w_moe_gate", [2, NE, D, FF_MOE])
        self.wmu = din("w_moe_up", [2, NE, D, FF_MOE])
        self.wmd = din("w_moe_down", [2, NE, FF_MOE, D])
        self.ln_gain = din("ln_gain", [DEPTH, 2, D])
        self.ln_bias = din("ln_bias", [DEPTH, 2, D])
        self.consts = din("consts", [128, 5, 128])

        self.out = nc.dram_tensor("out", [NT, D], F32, kind="ExternalOutput").ap()
        self.xa = scratch("xa", [NT, D])
        self.xb = scratch("xb", [NT, D])
        self.mod = scratch("mod", [ns, 8, 3 * D])
        self.hqT = scratch("hqT", [DH, NT])
        self.hfT = scratch("hfT", [DH, NT])
        self.hi_tm = scratch("hi_tm", [NT, DH], BF16)
        self.hg_tm = scratch("hg_tm", [NT, DH])
        self.sqT = scratch("sqT", [DSB, NT], BF16)
        self.skT = scratch("skT", [DSB, NT], BF16)
        self.sv_tm = scratch("sv_tm", [NT, DSB], BF16)
        self.oT = scratch("oT", [D, NT], BF16)

    def dump(self, P, name, ap, buf, shape, dt=F32):
        if name not in self.cfg.debug:
            return
        t = self.nc.dram_tensor(name, list(shape), dt, kind="ExternalOutput").ap()
        P.dma("sp", t, ap, buf, reads=[buf])

    def build(self):
        nc = self.nc
        cfg = self.cfg
        with contextlib.ExitStack() as top:
            def pt(name, shape, dt):
                return top.enter_context(nc.sbuf_tensor(name, list(shape), dt))
            self.identf = pt("identf", [128, 128], F32)
            self.identb = pt("identb", [128, 128], BF16)
            self.mask_bd = pt("mask_bd", [128, 128], F32)
            self.mask_st = pt("mask_st", [128, 128], F32)
            self.tri_neg = pt("tri_neg", [128, 128], BF16)
            self.neg_col = pt("neg_col", [128, 128], BF16)
            self.lbA = pt("lbA", [128, NH_H * DEPTH], F32)
            self.lbB = pt("lbB", [128, NH_H * DEPTH], F32)
            self.ones_f = pt("ones_f", [128, 128], F32)
            self.zeros_b = pt("zeros_b", [128, 512], BF16)
            self.ones_b = pt("ones_b", [128, 128], BF16)

            self.phase_setup()
            xcur = self.x_in
            for li, l in enumerate(cfg.layers):
                last = li == len(cfg.layers) - 1
                self.phase_inproj(l, xcur)
                self.phase_hgrn(l)
                self.phase_sb(l)
                self.phase_outproj(l, xcur, self.xa)
                self.phase_ffn(l, self.xa, self.out if last else self.xb)
                xcur = self.xb
        return nc

    def phase_setup(self):
        nc, cfg = self.nc, self.cfg
        P = Prog(nc, "p0")
        ns = cfg.nseq
        cst = P.sbuf("cst", [128, 5, 128], F32)
        b_cst = Buf("cst")
        P.dma("sp", cst[:], self.consts, b_cst, writes=[b_cst])
        bp = Buf("persist")
        P.op("dve", lambda e: e.tensor_copy(out=self.identf[:], in_=cst[:, 0, :]), reads=[b_cst], writes=[bp])
        P.op("dve", lambda e: e.tensor_copy(out=self.identb[:], in_=cst[:, 0, :]), reads=[b_cst], writes=[bp])
        P.op("dve", lambda e: e.tensor_copy(out=self.mask_bd[:], in_=cst[:, 1, :]), reads=[b_cst], writes=[bp])
        P.op("dve", lambda e: e.tensor_copy(out=self.mask_st[:], in_=cst[:, 2, :]), reads=[b_cst], writes=[bp])
        P.op("dve", lambda e: e.tensor_copy(out=self.tri_neg[:], in_=cst[:, 3, :]), reads=[b_cst], writes=[bp])
        P.op("dve", lambda e: e.tensor_copy(out=self.neg_col[:], in_=cst[:, 4, :]), reads=[b_cst], writes=[bp])
        P.op("dve", lambda e: e.memset(self.ones_f[:], 1.0), writes=[bp])
        P.op("dve", lambda e: e.memset(self.ones_b[:], 1.0), writes=[bp])
        P.op("dve", lambda e: e.memset(self.zeros_b[:], 0.0), writes=[bp])

        lg = P.sbuf("lg", [128, NH_H, DEPTH], F32)
        ex = P.sbuf("ex", [128, NH_H, DEPTH], F32)
        sm = P.sbuf("sm", [128, NH_H], F32)
        lbt = P.sbuf("lbt", [128, NH_H, DEPTH], F32)
        b_lg, b_ex, b_sm, b_lb = Buf(), Buf(), Buf(), Buf()
        P.dma("sp", lg[:], self.lbT, b_lg, writes=[b_lg])
        P.op("act", lambda e: e.activation(out=ex[:], in_=lg[:], func=AF.Exp), reads=[b_lg], writes=[b_ex])
        P.op("dve", lambda e: e.tensor_reduce(out=sm[:], in_=ex[:], axis=AX.X, op=ALU.add), reads=[b_ex], writes=[b_sm])
        P.op("dve", lambda e: e.reciprocal(out=sm[:], in_=sm[:]), reads=[b_sm], writes=[b_sm])
        for h in range(NH_H):
            P.op("dve", lambda e, h=h: e.tensor_scalar(out=ex[:, h, :], in0=ex[:, h, :], scalar1=sm[:, h:h + 1],
                                                        scalar2=None, op0=ALU.mult),
                 reads=[b_sm, b_ex], writes=[b_ex])
        P.op("dve", lambda e: e.memset(lbt[:, :, 0:1], 0.0), writes=[b_lb])
        for l in range(1, DEPTH):
            P.op("dve", lambda e, l=l: e.tensor_tensor(out=lbt[:, :, l:l + 1], in0=lbt[:, :, l - 1:l],
                                                        in1=ex[:, :, l:l + 1], op=ALU.add),
                 reads=[b_ex, b_lb], writes=[b_lb])
        for l in range(DEPTH):
            P.op("dve", lambda e, l=l: e.tensor_scalar(out=self.lbA[:, l * 4:(l + 1) * 4], in0=lbt[:, :, l],
                                                        scalar1=-0.5, scalar2=0.5, op0=ALU.mult, op1=ALU.add),
                 reads=[b_lb], writes=[bp])
            P.op("dve", lambda e, l=l: e.tensor_scalar(out=self.lbB[:, l * 4:(l + 1) * 4], in0=lbt[:, :, l],
                                                        scalar1=0.5, scalar2=0.5, op0=ALU.mult, op1=ALU.add),
                 reads=[b_lb], writes=[bp])

        ct = P.sbuf("ct", [128, 8, ns], F32)
        sct = P.sbuf("sct", [128, 8, ns], F32)
        b_ct, b_sct = Buf(), Buf()
        P.dma("sp", ct[:], self.cT, b_ct, writes=[b_ct])
        P.op("act", lambda e: e.activation(out=sct[:], in_=ct[:], func=AF.Exp, scale=-1.0), reads=[b_ct], writes=[b_sct])
        P.op("dve", lambda e: e.tensor_scalar(out=sct[:], in0=sct[:], scalar1=1.0, scalar2=None, op0=ALU.add),
             reads=[b_sct], writes=[b_sct])
        P.op("dve", lambda e: e.reciprocal(out=sct[:], in_=sct[:]), reads=[b_sct], writes=[b_sct])
        P.op("dve", lambda e: e.tensor_tensor(out=sct[:], in0=sct[:], in1=ct[:], op=ALU.mult),
             reads=[b_sct, b_ct], writes=[b_sct])
        wbuf = [P.sbuf(f"wa{i}", [128, 8, 512], F32) for i in range(2)]
        b_w = [Buf(), Buf()]
        bias = [P.sbuf(f"bias{i}", [ns, 3 * D], F32) for i in range(2)]
        b_bias = [Buf(), Buf()]
        mt = [P.sbuf(f"mt{i}", [ns, 3 * D], F32) for i in range(2)]
        b_mt = [Buf(), Buf()]
        ps = [P.psum(f"ps{i}", [128, 512]) for i in range(2)]
        b_ps = [Buf(psum=True), Buf(psum=True)]
        k = 0
        for ls in range(8):
            l, s = divmod(ls, 2)
            if l not in cfg.layers:
                continue
            bi = ls % 2
            P.dma("sp", bias[bi][:], bcast_rows(self.b_ada[l, s, :], ns), b_bias[bi], writes=[b_bias[bi]])
            for n in range(6):
                wi = k % 2
                k += 1
                src = self.w_ada[l, s, :, n * 512:(n + 1) * 512].rearrange("(c p) f -> p c f", p=128)
                P.dma("sp", wbuf[wi][:], src, b_w[wi], writes=[b_w[wi]])
                for dc in range(8):
                    P.op("pe", lambda e, wi=wi, dc=dc: e.matmul(ps[wi][0:ns, :], sct[:, dc, :], wbuf[wi][:, dc, :],
                                                                  start=(dc == 0), stop=(dc == 7)),
                         reads=[b_sct, b_w[wi]], writes=[b_ps[wi]], signal=(dc == 7))
                P.op("dve", lambda e, wi=wi, bi=bi, n=n: e.tensor_tensor(
                    out=mt[bi][:, n * 512:(n + 1) * 512], in0=ps[wi][0:ns, :], in1=bias[bi][:, n * 512:(n + 1) * 512],
                    op=ALU.add), reads=[b_ps[wi], b_bias[bi]], writes=[b_mt[bi]])
            P.op("dve", lambda e, bi=bi: e.tensor_scalar(out=mt[bi][:, D:2 * D], in0=mt[bi][:, D:2 * D], scalar1=1.0,
                                                          scalar2=None, op0=ALU.add), reads=[b_mt[bi]], writes=[b_mt[bi]])
            P.dma("sp", self.mod[:, ls, :], mt[bi][:], b_mt[bi], reads=[b_mt[bi]])
        P.run()

    def load_bcast(self, P, tile, buf, src1d):
        P.dma("sp", tile[:], bcast_rows(src1d, 128), buf, writes=[buf])

    def phase_ffn(self, l, x_src, x_dst):
        nc, cfg = self.nc, self.cfg
        moe = (l % 2 == 1)
        j = l // 2
        P = Prog(nc, f"f{l}")
        TB = 1024
        NTI = TB // 128
        nblk = cfg.ntok // TB
        blk_per_seq = cfg.S // TB
        FF = FF_MOE if moe else FF_DENSE
        nfc = FF // 128
        groups = [(g0, min(4, nfc - g0)) for g0 in range(0, nfc, 4)]
        nexp = NE if moe else 1
        nexp = DBG.get('nexp', nexp)

        sc1 = P.sbuf("sc1", [128, D], F32); sh = P.sbuf("sh", [128, D], F32); gt = P.sbuf("gt", [128, D], F32)
        lng = P.sbuf("lng", [128, D], F32); lnb = P.sbuf("lnb", [128, D], F32)
        b_mod, b_ln = Buf("mod"), Buf("ln")
        xt = [P.sbuf(f"xt{i}", [128, D], F32) for i in range(2)]
        b_xt = [Buf(), Buf()]
        hf = [P.sbuf(f"hf{i}", [128, D], F32) for i in range(2)]
        b_hf = [Buf(), Buf()]
        hT = P.sbuf("hT", [128, 8, TB], BF16)
        b_hT = [Buf() for _ in range(NTI)]
        yacc = P.sbuf("yacc", [128, NTI, D], F32)
        b_y = [[Buf(), Buf()] for _ in range(NTI)]
        wg = [P.sbuf(f"wg{i}", [128, 8, 512], BF16) for i in range(2)]
        wu = [P.sbuf(f"wu{i}", [128, 8, 512], BF16) for i in range(2)]
        wd = [P.sbuf(f"wd{i}", [128, 4, D], BF16) for i in range(2)]
        b_wgu = [Buf(), Buf()]
        b_wd = [Buf(), Buf()]
        aT = [P.sbuf(f"aT{i}", [128, 4, TB], BF16) for i in range(2)]
        b_aT = [[Buf(), Buf()] for _ in range(2)]
        sg = [P.sbuf(f"sg{i}", [128, 512], F32) for i in range(2)]
        b_sg = [Buf(), Buf()]
        ot = [P.sbuf(f"ot{i}", [128, D], F32) for i in range(2)]
        b_ot = [Buf(), Buf()]
        stats = P.sbuf("stats", [128, NTI, 2, 6], F32)
        mv = P.sbuf("mv", [128, NTI, 2], F32)
        rstd = P.sbuf("rstd", [128, NTI], F32)
        b_stats = [Buf() for _ in range(NTI)]
        b_mv = Buf(); b_rstd = Buf()
        if moe:
            hTf = P.sbuf("hTf", [128, 8, 128], F32); b_hTf = Buf()
            wr = P.sbuf("wr", [128, 8, NE], F32); b_wr = Buf()
            comb = P.sbuf("comb", [128, NTI, NE], F32); b_comb = [Buf() for _ in range(NTI)]
            rt = P.sbuf("rt", [128, 8, NE], F32); b_rt = Buf()
            if DBG.get("router", 1):
                P.dma("sp", wr[:], self.w_router[j], b_wr, writes=[b_wr])
        psT = [P.psum(f"psT{i}", [128, 512]) for i in range(2)]; b_psT = [Buf(psum=True), Buf(psum=True)]
        gps = [P.psum(f"gps{i}", [128, 512]) for i in range(2)]; b_gps = [Buf(psum=True), Buf(psum=True)]
        ups = [P.psum(f"ups{i}", [128, 512]) for i in range(2)]; b_ups = [Buf(psum=True), Buf(psum=True)]
        yps = [P.psum(f"yps{i}", [128, 512]) for i in range(2)]; b_yps = [Buf(psum=True), Buf(psum=True)]

        P.dma("sp", lng[:], bcast_rows(self.ln_gain[l, 1, :], 128), b_ln, writes=[b_ln])
        P.dma("sp", lnb[:], bcast_rows(self.ln_bias[l, 1, :], 128), b_ln, writes=[b_ln])

        if moe:
            WG, WU, WD = self.wmg[j], self.wmu[j], self.wmd[j]
        else:
            WG, WU, WD = self.wdg[j:j + 1], self.wdu[j:j + 1], self.wdd[j:j + 1]

        gcount = 0
        xk = 0
        for blk in range(nblk):
            t0 = blk * TB
            bseq = blk // blk_per_seq
            if blk % blk_per_seq == 0:
                ls = l * 2 + 1
                P.dma("sp", sh[:], bcast_rows(self.mod[bseq, ls, 0:D], 128), b_mod, writes=[b_mod])
                P.dma("sp", sc1[:], bcast_rows(self.mod[bseq, ls, D:2 * D], 128), b_mod, writes=[b_mod])
                P.dma("sp", gt[:], bcast_rows(self.mod[bseq, ls, 2 * D:3 * D], 128), b_mod, writes=[b_mod])
            for i in range(NTI):
                xi = xk % 2
                xk += 1
                r0 = t0 + i * 128
                P.dma("sp", xt[xi][:], x_src[r0:r0 + 128, :], b_xt[xi], writes=[b_xt[xi]])
                P.op("dve", lambda e, xi=xi: e.tensor_tensor(out=hf[xi][:], in0=xt[xi][:], in1=sc1[:], op=ALU.mult),
                     reads=[b_xt[xi], b_mod], writes=[b_hf[xi]])
                P.op("pool", lambda e, xi=xi: e.tensor_tensor(out=hf[xi][:], in0=hf[xi][:], in1=sh[:], op=ALU.add),
                     reads=[b_hf[xi], b_mod], writes=[b_hf[xi]])
                for hb in range(2):
                    for q in range(4):
                        dc = hb * 4 + q
                        P.op("pe", lambda e, xi=xi, hb=hb, q=q, dc=dc: e.transpose(
                            psT[hb][:, q * 128:(q + 1) * 128], hf[xi][:, dc * 128:(dc + 1) * 128], self.identf[:]),
                            reads=[b_hf[xi]], writes=[b_psT[hb]], signal=(q == 3))
                    if moe:
                        P.op("act", lambda e, hb=hb: e.activation(
                            out=hTf[:, hb * 4:(hb + 1) * 4, :], in_=psT[hb][:].rearrange("p (c t) -> p c t", t=128),
                            func=AF.Copy), reads=[b_psT[hb]], writes=[b_hTf])
                    P.op("act" if not moe else "dve", lambda e, hb=hb, i=i: e.tensor_copy(
                        out=hT[:, hb * 4:(hb + 1) * 4, i * 128:(i + 1) * 128],
                        in_=psT[hb][:].rearrange("p (c t) -> p c t", t=128)) if moe else e.activation(
                        out=hT[:, hb * 4:(hb + 1) * 4, i * 128:(i + 1) * 128],
                        in_=psT[hb][:].rearrange("p (c t) -> p c t", t=128), func=AF.Copy),
                        reads=[b_psT[hb]], writes=[b_hT[i]])
                if moe and DBG.get("router", 1) == 0:
                    P.op("dve", lambda e, i=i: e.memset(comb[:, i, :], 0.125), writes=[b_comb[i]])
                if moe and DBG.get("router", 1):
                    for dc in range(8):
                        P.op("pe", lambda e, dc=dc: e.matmul(yps[1][:, 0:NE], hTf[:, dc, :], wr[:, dc, :],
                                                              start=(dc == 0), stop=(dc == 7)),
                             reads=[b_hTf, b_wr], writes=[b_yps[1]], signal=(dc == 7))
                    self.router_top2(P, yps[1], b_yps[1], rt, b_rt, comb, b_comb[i], i)
            for ex in range(nexp):
                for (g0, ng) in groups:
                    slot = gcount % 2
                    gcount += 1
                    f0 = g0 * 128
                    fw = ng * 128
                    P.dma("pool", wg[slot][:, :, 0:fw], WG[ex, :, f0:f0 + fw].rearrange("(c p) f -> p c f", p=128),
                          b_wgu[slot], writes=[b_wgu[slot]])
                    P.dma("pool", wu[slot][:, :, 0:fw], WU[ex, :, f0:f0 + fw].rearrange("(c p) f -> p c f", p=128),
                          b_wgu[slot], writes=[b_wgu[slot]])
                    P.dma("pool", wd[slot][:, 0:ng, :], WD[ex, f0:f0 + fw, :].rearrange("(c p) d -> p c d", p=128),
                          b_wd[slot], writes=[b_wd[slot]])
                    for fc in range(ng):
                        for half in range(2):
                            pb = (fc * 2 + half) % 2
                            for dc in range(8):
                                P.op("pe", lambda e, slot=slot, fc=fc, half=half, dc=dc, pb=pb: e.matmul(
                                    gps[pb][:], wg[slot][:, dc, fc * 128:(fc + 1) * 128],
                                    hT[:, dc, half * 512:(half + 1) * 512], start=(dc == 0), stop=(dc == 7)),
                                    reads=[b_wgu[slot]] + b_hT[half * 4:(half + 1) * 4], writes=[b_gps[pb]],
                                    signal=(dc == 7))
                            for dc in range(8):
                                P.op("pe", lambda e, slot=slot, fc=fc, half=half, dc=dc, pb=pb: e.matmul(
                                    ups[pb][:], wu[slot][:, dc, fc * 128:(fc + 1) * 128],
                                    hT[:, dc, half * 512:(half + 1) * 512], start=(dc == 0), stop=(dc == 7)),
                                    reads=[b_wgu[slot]] + b_hT[half * 4:(half + 1) * 4], writes=[b_ups[pb]],
                                    signal=(dc == 7))
                            P.op("act", lambda e, pb=pb: e.activation(out=sg[pb][:], in_=gps[pb][:], func=AF.Silu),
                                 reads=[b_gps[pb]], writes=[b_sg[pb]])
                            P.op("dve", lambda e, pb=pb, slot=slot, fc=fc, half=half: e.tensor_tensor(
                                out=aT[slot][:, fc, half * 512:(half + 1) * 512], in0=ups[pb][:], in1=sg[pb][:],
                                op=ALU.mult), reads=[b_ups[pb], b_sg[pb]], writes=[b_aT[slot][half]])
                    first = (ex == 0 and g0 == 0)
                    for i in range(NTI):
                        for h2 in range(2):
                            pb = (i * 2 + h2) % 2
                            for fc in range(ng):
                                P.op("pe", lambda e, slot=slot, fc=fc, i=i, h2=h2, pb=pb, ng=ng: e.matmul(
                                    yps[pb][:], aT[slot][:, fc, i * 128:(i + 1) * 128],
                                    wd[slot][:, fc, h2 * 512:(h2 + 1) * 512], start=(fc == 0), stop=(fc == ng - 1)),
                                    reads=[b_aT[slot][i // 4], b_wd[slot]], writes=[b_yps[pb]], signal=(fc == ng - 1))
                            ysl = yacc[:, i, h2 * 512:(h2 + 1) * 512]
                            if moe:
                                csc = comb[:, i, ex:ex + 1]
                                if first:
                                    P.op("dve", lambda e, pb=pb, ysl=ysl, csc=csc: e.tensor_scalar(
                                        out=ysl, in0=yps[pb][:], scalar1=csc, scalar2=None, op0=ALU.mult),
                                        reads=[b_yps[pb], b_comb[i]], writes=[b_y[i][h2]])
                                else:
                                    P.op("dve", lambda e, pb=pb, ysl=ysl, csc=csc: e.scalar_tensor_tensor(
                                        out=ysl, in0=yps[pb][:], scalar=csc, in1=ysl, op0=ALU.mult, op1=ALU.add),
                                        reads=[b_yps[pb], b_comb[i]], writes=[b_y[i][h2]])
                            else:
                                if first:
                                    P.op("act", lambda e, pb=pb, ysl=ysl: e.activation(out=ysl, in_=yps[pb][:], func=AF.Copy),
                                         reads=[b_yps[pb]], writes=[b_y[i][h2]])
                                else:
                                    P.op("dve", lambda e, pb=pb, ysl=ysl: e.tensor_tensor(
                                        out=ysl, in0=yps[pb][:], in1=ysl, op=ALU.add),
                                        reads=[b_yps[pb]], writes=[b_y[i][h2]])
            for i in range(NTI):
                xi = xk % 2
                xk += 1
                r0 = t0 + i * 128
                P.dma("sp", xt[xi][:], x_src[r0:r0 + 128, :], b_xt[xi], writes=[b_xt[xi]])
                P.op("pool", lambda e, i=i: e.tensor_tensor(out=yacc[:, i, :], in0=yacc[:, i, :], in1=gt[:], op=ALU.mult),
                     reads=[b_mod], writes=b_y[i])
                P.op("dve", lambda e, i=i, xi=xi: e.scalar_tensor_tensor(
                    out=yacc[:, i, :], in0=xt[xi][:], scalar=ALPHA, in1=yacc[:, i, :], op0=ALU.mult, op1=ALU.add),
                    reads=[b_xt[xi]], writes=b_y[i])
                for h2 in range(2):
                    P.op("dve", lambda e, i=i, h2=h2: e.bn_stats(out=stats[:, i, h2, :], in_=yacc[:, i, h2 * 512:(h2 + 1) * 512]),
                         reads=b_y[i], writes=[b_stats[i]])
                P.op("dve", lambda e, i=i: e.bn_aggr(out=mv[:, i, :], in_=stats[:, i, :, :].rearrange("p a b -> p (a b)")),
                     reads=[b_stats[i]], writes=[b_mv])
            self.rstd_from_var(P, mv, b_mv, rstd, b_rstd, NTI)
            for i in range(NTI):
                oi = i % 2
                r0 = t0 + i * 128
                P.op("dve", lambda e, i=i, oi=oi: e.tensor_scalar(
                    out=ot[oi][:], in0=yacc[:, i, :], scalar1=mv[:, i, 0:1], scalar2=rstd[:, i:i + 1],
                    op0=ALU.subtract, op1=ALU.mult), reads=b_y[i] + [b_mv, b_rstd], writes=[b_ot[oi]])
                P.op("pool", lambda e, oi=oi: e.tensor_tensor(out=ot[oi][:], in0=ot[oi][:], in1=lng[:], op=ALU.mult),
                     reads=[b_ln], writes=[b_ot[oi]])
                P.op("pool", lambda e, oi=oi: e.tensor_tensor(out=ot[oi][:], in0=ot[oi][:], in1=lnb[:], op=ALU.add),
                     reads=[b_ln], writes=[b_ot[oi]])
                P.dma("sp", x_dst[r0:r0 + 128, :], ot[oi][:], b_ot[oi], reads=[b_ot[oi]])
        P.run()

    def phase_inproj(self, l, x_src):
        nc, cfg = self.nc, self.cfg
        P = Prog(nc, f"i{l}")
        TB = 512
        nblk = cfg.ntok // TB
        blk_per_seq = cfg.S // TB
        w = P.sbuf("w", [128, 8, DIN], BF16); b_w = Buf()
        for n in range(7):
            P.dma("pool", w[:, :, n * 512:(n + 1) * 512],
                  self.w_in[l, :, n * 512:(n + 1) * 512].rearrange("(c p) f -> p c f", p=128), b_w, writes=[b_w])
        sc1 = P.sbuf("sc1", [128, D], F32); sh = P.sbuf("sh", [128, D], F32); b_mod = Buf()
        xt = [P.sbuf(f"xt{i}", [128, D], F32) for i in range(2)]; b_xt = [Buf(), Buf()]
        hf = [P.sbuf(f"hf{i}", [128, D], F32) for i in range(2)]; b_hf = [Buf(), Buf()]
        hT = [P.sbuf(f"hT{i}", [128, 8, TB], BF16) for i in range(2)]
        b_hT = [[Buf() for _ in range(4)] for _ in range(2)]
        NST = 4
        stf = [P.sbuf(f"stf{i}", [128, 512], F32) for i in range(NST)]; b_stf = [Buf() for _ in range(NST)]
        stb = [P.sbuf(f"stb{i}", [128, 512], BF16) for i in range(NST)]; b_stb = [Buf() for _ in range(NST)]
        psT = [P.psum(f"psT{i}", [128, 512]) for i in range(2)]; b_psT = [Buf(psum=True), Buf(psum=True)]
        NPS = 6
        mm = [P.psum(f"mm{i}", [128, 512]) for i in range(NPS)]; b_mm = [Buf(psum=True) for _ in range(NPS)]
        xk = 0; kf = 0; kb_ = 0; km = 0
        for blk in range(nblk):
            t0 = blk * TB
            bseq = blk // blk_per_seq
            hs = blk % 2
            if blk % blk_per_seq == 0:
                ls = l * 2
                P.dma("sp", sh[:], bcast_rows(self.mod[bseq, ls, 0:D], 128), b_mod, writes=[b_mod])
                P.dma("sp", sc1[:], bcast_rows(self.mod[bseq, ls, D:2 * D], 128), b_mod, writes=[b_mod])
            for i in range(4):
                xi = xk % 2; xk += 1
                r0 = t0 + i * 128
                P.dma("sp", xt[xi][:], x_src[r0:r0 + 128, :], b_xt[xi], writes=[b_xt[xi]])
                P.op("dve", lambda e, xi=xi: e.tensor_tensor(out=hf[xi][:], in0=xt[xi][:], in1=sc1[:], op=ALU.mult),
                     reads=[b_xt[xi], b_mod], writes=[b_hf[xi]])
                P.op("pool", lambda e, xi=xi: e.tensor_tensor(out=hf[xi][:], in0=hf[xi][:], in1=sh[:], op=ALU.add),
                     reads=[b_hf[xi], b_mod], writes=[b_hf[xi]])
                for hb in range(2):
                    for q in range(4):
                        dc = hb * 4 + q
                        P.op("pe", lambda e, xi=xi, hb=hb, q=q, dc=dc: e.transpose(
                            psT[hb][:, q * 128:(q + 1) * 128], hf[xi][:, dc * 128:(dc + 1) * 128], self.identf[:]),
                            reads=[b_hf[xi]], writes=[b_psT[hb]], signal=(q == 3))
                    P.op("act", lambda e, hb=hb, i=i, hs=hs: e.activation(
                        out=hT[hs][:, hb * 4:(hb + 1) * 4, i * 128:(i + 1) * 128],
                        in_=psT[hb][:].rearrange("p (c t) -> p c t", t=128), func=AF.Copy),
                        reads=[b_psT[hb]], writes=[b_hT[hs][i]])
            for cc in list(range(0, 8)) + list(range(16, 24)):
                pb = km % NPS; km += 1
                for dc in range(8):
                    P.op("pe", lambda e, cc=cc, dc=dc, pb=pb, hs=hs: e.matmul(
                        mm[pb][:], w[:, dc, cc * 128:(cc + 1) * 128], hT[hs][:, dc, :], start=(dc == 0), stop=(dc == 7)),
                        reads=[b_w] + b_hT[hs], writes=[b_mm[pb]], signal=(dc == 7))
                if cc < 8:
                    si = kf % NST; kf += 1
                    if cc < 4:
                        P.op("act", lambda e, pb=pb, si=si: e.activation(out=stf[si][:], in_=mm[pb][:], func=AF.Silu),
                             reads=[b_mm[pb]], writes=[b_stf[si]])
                        dst = self.hqT[cc * 128:(cc + 1) * 128, t0:t0 + TB]
                    else:
                        P.op("act", lambda e, pb=pb, si=si: e.activation(out=stf[si][:], in_=mm[pb][:], func=AF.Tanh, scale=0.5),
                             reads=[b_mm[pb]], writes=[b_stf[si]])
                        dst = self.hfT[(cc - 4) * 128:(cc - 3) * 128, t0:t0 + TB]
                    P.dma("sp", dst, stf[si][:], b_stf[si], reads=[b_stf[si]])
                else:
                    si = kb_ % NST; kb_ += 1
                    sc = 0.125 if cc < 20 else 1.0
                    P.op("dve", lambda e, pb=pb, si=si, sc=sc: e.tensor_scalar(
                        out=stb[si][:], in0=mm[pb][:], scalar1=sc, scalar2=None, op0=ALU.mult),
                        reads=[b_mm[pb]], writes=[b_stb[si]])
                    if cc < 20:
                        dst = self.sqT[(cc - 16) * 128:(cc - 15) * 128, t0:t0 + TB]
                    else:
                        dst = self.skT[(cc - 20) * 128:(cc - 19) * 128, t0:t0 + TB]
                    P.dma("sp", dst, stb[si][:], b_stb[si], reads=[b_stb[si]])
            for i in range(4):
                r0 = t0 + i * 128
                for seg, c0 in (("hi", 1024), ("hg", 1536), ("sv", 3072)):
                    pb = km % NPS; km += 1
                    for dc in range(8):
                        P.op("pe", lambda e, dc=dc, pb=pb, hs=hs, i=i, c0=c0: e.matmul(
                            mm[pb][:], hT[hs][:, dc, i * 128:(i + 1) * 128], w[:, dc, c0:c0 + 512],
                            start=(dc == 0), stop=(dc == 7)),
                            reads=[b_w, b_hT[hs][i]], writes=[b_mm[pb]], signal=(dc == 7))
                    if seg == "hg":
                        si = kf % NST; kf += 1
                        P.op("act", lambda e, pb=pb, si=si: e.activation(out=stf[si][:], in_=mm[pb][:], func=AF.Silu),
                             reads=[b_mm[pb]], writes=[b_stf[si]])
                        P.dma("sp", self.hg_tm[r0:r0 + 128, :], stf[si][:], b_stf[si], reads=[b_stf[si]])
                    else:
                        si = kb_ % NST; kb_ += 1
                        P.op("dve", lambda e, pb=pb, si=si: e.tensor_copy(out=stb[si][:], in_=mm[pb][:]),
                             reads=[b_mm[pb]], writes=[b_stb[si]])
                        dst = self.hi_tm if seg == "hi" else self.sv_tm
                        P.dma("sp", dst[r0:r0 + 128, :], stb[si][:], b_stb[si], reads=[b_stb[si]])
        P.run()

    def phase_outproj(self, l, x_src, x_dst):
        nc, cfg = self.nc, self.cfg
        P = Prog(nc, f"o{l}")
        TB = 512
        nblk = cfg.ntok // TB
        blk_per_seq = cfg.S // TB
        w = P.sbuf("w", [128, 8, D], BF16); b_w = Buf()
        P.dma("pool", w[:], self.w_out[l].rearrange("(c p) f -> p c f", p=128), b_w, writes=[b_w])
        gt = P.sbuf("gt", [128, D], F32); b_mod = Buf()
        lng = P.sbuf("lng", [128, D], F32); lnb = P.sbuf("lnb", [128, D], F32); b_ln = Buf()
        P.dma("sp", lng[:], bcast_rows(self.ln_gain[l, 0, :], 128), b_ln, writes=[b_ln])
        P.dma("sp", lnb[:], bcast_rows(self.ln_bias[l, 0, :], 128), b_ln, writes=[b_ln])
        oTs = [P.sbuf(f"oTs{i}", [128, 8, TB], BF16) for i in range(2)]; b_oTs = [Buf(), Buf()]
        xt = [P.sbuf(f"xt{i}", [128, D], F32) for i in range(2)]; b_xt = [Buf(), Buf()]
        zt = P.sbuf("zt", [128, 4, D], F32); b_z = [Buf() for _ in range(4)]
        ot = [P.sbuf(f"ot{i}", [128, D], F32) for i in range(2)]; b_ot = [Buf(), Buf()]
        stats = P.sbuf("stats", [128, 4, 2, 6], F32); b_stats = [Buf() for _ in range(4)]
        mv = P.sbuf("mv", [128, 4, 2], F32); b_mv = Buf()
        rstd = P.sbuf("rstd", [128, 4], F32); b_rstd = Buf()
        yps = [P.psum(f"yps{i}", [128, 512]) for i in range(4)]; b_yps = [Buf(psum=True) for _ in range(4)]
        xk = 0; kp = 0
        for blk in range(nblk):
            t0 = blk * TB
            bseq = blk // blk_per_seq
            os_ = blk % 2
            if blk % blk_per_seq == 0:
                P.dma("sp", gt[:], bcast_rows(self.mod[bseq, l * 2, 2 * D:3 * D], 128), b_mod, writes=[b_mod])
            P.dma("sp", oTs[os_][:], self.oT[:, t0:t0 + TB].rearrange("(c p) t -> p c t", p=128), b_oTs[os_],
                  writes=[b_oTs[os_]])
            for i in range(4):
                xi = xk % 2; xk += 1
                r0 = t0 + i * 128
                P.dma("sp", xt[xi][:], x_src[r0:r0 + 128, :], b_xt[xi], writes=[b_xt[xi]])
                for h2 in range(2):
                    pb = kp % 4; kp += 1
                    for cc in range(8):
                        P.op("pe", lambda e, cc=cc, pb=pb, os_=os_, i=i, h2=h2: e.matmul(
                            yps[pb][:], oTs[os_][:, cc, i * 128:(i + 1) * 128], w[:, cc, h2 * 512:(h2 + 1) * 512],
                            start=(cc == 0), stop=(cc == 7)), reads=[b_w, b_oTs[os_]], writes=[b_yps[pb]], signal=(cc == 7))
                    P.op("dve", lambda e, pb=pb, i=i, h2=h2: e.tensor_tensor(
                        out=zt[:, i, h2 * 512:(h2 + 1) * 512], in0=yps[pb][:], in1=gt[:, h2 * 512:(h2 + 1) * 512], op=ALU.mult),
                        reads=[b_yps[pb], b_mod], writes=[b_z[i]])
                P.op("dve", lambda e, i=i, xi=xi: e.scalar_tensor_tensor(
                    out=zt[:, i, :], in0=xt[xi][:], scalar=ALPHA, in1=zt[:, i, :], op0=ALU.mult, op1=ALU.add),
                    reads=[b_xt[xi]], writes=[b_z[i]])
                for h2 in range(2):
                    P.op("dve", lambda e, i=i, h2=h2: e.bn_stats(out=stats[:, i, h2, :], in_=zt[:, i, h2 * 512:(h2 + 1) * 512]),
                         reads=[b_z[i]], writes=[b_stats[i]])
                P.op("dve", lambda e, i=i: e.bn_aggr(out=mv[:, i, :], in_=stats[:, i, :, :].rearrange("p a b -> p (a b)")),
                     reads=[b_stats[i]], writes=[b_mv])
            self.rstd_from_var(P, mv, b_mv, rstd, b_rstd, 4)
            for i in range(4):
                oi = i % 2
                r0 = t0 + i * 128
                P.op("dve", lambda e, i=i, oi=oi: e.tensor_scalar(
                    out=ot[oi][:], in0=zt[:, i, :], scalar1=mv[:, i, 0:1], scalar2=rstd[:, i:i + 1],
                    op0=ALU.subtract, op1=ALU.mult), reads=[b_z[i], b_mv, b_rstd], writes=[b_ot[oi]])
                P.op("pool", lambda e, oi=oi: e.tensor_tensor(out=ot[oi][:], in0=ot[oi][:], in1=lng[:], op=ALU.mult),
                     reads=[b_ln], writes=[b_ot[oi]])
                P.op("pool", lambda e, oi=oi: e.tensor_tensor(out=ot[oi][:], in0=ot[oi][:], in1=lnb[:], op=ALU.add),
                     reads=[b_ln], writes=[b_ot[oi]])
                P.dma("sp", x_dst[r0:r0 + 128, :], ot[oi][:], b_ot[oi], reads=[b_ot[oi]])
        P.run()

    def phase_sb(self, l):
        nc, cfg = self.nc, self.cfg
        P = Prog(nc, f"s{l}")
        S = cfg.S
        NB = S // 128
        NG = S // 512
        ka = [P.sbuf(f"ka{i}", [65, S], BF16) for i in range(2)]; b_ka = [Buf(), Buf()]
        qz = [P.sbuf(f"qz{i}", [64, S], BF16) for i in range(2)]; b_qz = [Buf(), Buf()]
        qa = [[P.sbuf(f"qa{i}{j}", [65, S], BF16) for j in range(2)] for i in range(2)]
        b_qa = [[Buf(), Buf()] for _ in range(2)]
        vall = P.sbuf("vall", [128, NB, DSB], BF16); b_v = Buf()
        e1 = [P.sbuf(f"e1{i}", [128, 512], F32) for i in range(2)]; b_e1 = [Buf(), Buf()]
        NSP = 3
        sp = [P.sbuf(f"sp{i}", [128, 512], BF16) for i in range(NSP)]; b_sp = [Buf() for _ in range(NSP)]
        At = [P.sbuf(f"A{i}", [128, 512], BF16) for i in range(NSP)]; b_A = [Buf() for _ in range(NSP)]
        ost = [P.sbuf(f"ost{i}", [64, 512], BF16) for i in range(2)]; b_ost = [Buf(), Buf()]
        zps = [P.psum(f"z{i}", [128, 512]) for i in range(2)]; b_z = [Buf(psum=True), Buf(psum=True)]
        gps = [P.psum(f"g{i}", [128, 512]) for i in range(2)]; b_g = [Buf(psum=True), Buf(psum=True)]
        cps = [P.psum(f"c{i}", [128, 512]) for i in range(2)]; b_c = [Buf(psum=True), Buf(psum=True)]
        ops_ = [P.psum(f"o{i}", [128, 512]) for i in range(2)]; b_o = [Buf(psum=True), Buf(psum=True)]
        for i in range(2):
            P.op("pool", lambda e, i=i: e.memset(ka[i][64:65, :], 1.0), writes=[b_ka[i]])

        steps = []
        hk = 0
        for b in range(cfg.nseq):
            for h in range(NH_S):
                hp = hk % 2
                for g in range(NG):
                    kmax = 4 * g + 3
                    for kb in range(kmax, -1, -1):
                        i = kb - 4 * g
                        n0 = max(i, 0) * 128
                        steps.append(dict(b=b, h=h, hp=hp, g=g, kb=kb, n0=n0, diag=(i >= 0),
                                          first=(kb == kmax), last=(kb == 0), newhead=(g == 0 and kb == kmax),
                                          gk=None))
                hk += 1
        gk = -1
        for st in steps:
            if st["first"]:
                gk += 1
            st["gp"] = gk % 2
        ost_k = [0]

        def load_head(st):
            b, h, hp = st["b"], st["h"], st["hp"]
            c0 = b * S
            if h == 0:
                for q0 in range(0, NB, 8):
                    P.dma("sp", vall[:, q0:q0 + 8, :], self.sv_tm[c0 + q0 * 128:c0 + (q0 + 8) * 128, :].rearrange("(n p) d -> p n d", p=128),
                          b_v, writes=[b_v])
            P.dma("sp", ka[hp][0:64, :], self.skT[h * 64:(h + 1) * 64, c0:c0 + S], b_ka[hp], writes=[b_ka[hp]])
            P.dma("sp", qz[hp][:], self.sqT[h * 64:(h + 1) * 64, c0:c0 + S], b_qz[hp], writes=[b_qz[hp]])
            for j in range(2):
                P.dma("sp", qa[hp][j][0:64, :], self.sqT[h * 64:(h + 1) * 64, c0:c0 + S], b_qa[hp][j], writes=[b_qa[hp][j]])

        def emit_p1(j):
            st = steps[j]
            if st["newhead"]:
                load_head(st)
            hp, kb, n0 = st["hp"], st["kb"], st["n0"]
            c0 = st["g"] * 512
            zb = j % 2
            P.op("pe", lambda e: e.matmul(zps[zb][:, n0:512], ka[hp][0:64, kb * 128:(kb + 1) * 128],
                                           qz[hp][0:64, c0 + n0:c0 + 512], start=True, stop=True),
                 reads=[b_ka[hp], b_qz[hp]], writes=[b_z[zb]])

        def emit_a12(j):
            st = steps[j]
            n0 = st["n0"]
            zb = j % 2; eb = j % 2; sb = j % NSP
            P.op("act", lambda e: e.activation(out=e1[eb][:, n0:512], in_=zps[zb][:, n0:512], func=AF.Exp),
                 reads=[b_z[zb]], writes=[b_e1[eb]])
            P.op("act", lambda e: e.activation(out=sp[sb][:, n0:512], in_=e1[eb][:, n0:512], func=AF.Ln, bias=1.0, scale=1.0),
                 reads=[b_e1[eb]], writes=[b_sp[sb]])
            if st["diag"]:
                P.op("dve", lambda e: e.tensor_tensor(out=sp[sb][:, n0:n0 + 128], in0=sp[sb][:, n0:n0 + 128],
                                                       in1=self.mask_st[:], op=ALU.mult), writes=[b_sp[sb]])

        def emit_main(j):
            st = steps[j]
            hp, kb, n0, gp, h = st["hp"], st["kb"], st["n0"], st["gp"], st["h"]
            c0 = st["g"] * 512
            par = j % 2; sb = j % NSP; gb = j % 2; ab = j % NSP
            if st["first"]:
                P.op("pe", lambda e: e.matmul(cps[gp][0:65, :], self.zeros_b[:, 0:65], self.zeros_b[:, 0:512], start=True, stop=True),
                     writes=[b_c[gp]])
                P.op("pe", lambda e: e.matmul(ops_[gp][0:64, :], self.zeros_b[:, 0:64], self.zeros_b[:, 0:512], start=True, stop=True),
                     writes=[b_o[gp]])
            P.op("dve", lambda e: e.tensor_copy(out=qa[hp][par][64:65, c0 + n0:c0 + 512], in_=cps[gp][64:65, n0:512]),
                 reads=[b_c[gp]], writes=[b_qa[hp][par]])
            P.op("pe", lambda e: e.matmul(gps[gb][:, n0:512], ka[hp][0:65, kb * 128:(kb + 1) * 128],
                                           qa[hp][par][0:65, c0 + n0:c0 + 512], start=True, stop=False),
                 reads=[b_ka[hp], b_qa[hp][par]], writes=[b_g[gb]], signal=False)
            P.op("pe", lambda e: e.matmul(gps[gb][:, n0:512], self.tri_neg[:], sp[sb][:, n0:512], start=False, stop=True),
                 reads=[b_sp[sb]], writes=[b_g[gb]])
            P.op("pe", lambda e: e.matmul(cps[gp][0:65, n0:512], self.neg_col[:, 0:65], sp[sb][:, n0:512], start=False, stop=True,
                                           skip_group_check=True),
                 reads=[b_sp[sb]], writes=[b_c[gp]])
            P.op("act", lambda e: e.activation(out=At[ab][:, n0:512], in_=gps[gb][:, n0:512], func=AF.Exp),
                 reads=[b_g[gb]], writes=[b_A[ab]])
            if st["diag"]:
                P.op("dve", lambda e: e.tensor_tensor(out=At[ab][:, n0:n0 + 128], in0=At[ab][:, n0:n0 + 128],
                                                       in1=self.mask_st[:], op=ALU.mult), writes=[b_A[ab]])

        def emit_p4(j):
            st = steps[j]
            kb, n0, gp, h, b = st["kb"], st["n0"], st["gp"], st["h"], st["b"]
            ab = j % NSP
            P.op("pe", lambda e: e.matmul(ops_[gp][0:64, n0:512], vall[:, kb, h * 64:(h + 1) * 64], At[ab][:, n0:512],
                                           start=False, stop=True, skip_group_check=True),
                 reads=[b_v, b_A[ab]], writes=[b_o[gp]])
            if st["last"]:
                k = ost_k[0] % 2; ost_k[0] += 1
                c0 = b * S + st["g"] * 512
                P.op("dve", lambda e: e.tensor_copy(out=ost[k][:], in_=ops_[gp][0:64, :]), reads=[b_o[gp]], writes=[b_ost[k]])
                P.dma("sp", self.oT[DH + h * 64:DH + (h + 1) * 64, c0:c0 + 512], ost[k][:], b_ost[k], reads=[b_ost[k]])

        n = len(steps)
        LOOK = 2
        for j in range(min(LOOK, n)):
            emit_p1(j)
            emit_a12(j)
        for j in range(n):
            if j + LOOK < n:
                emit_p1(j + LOOK)
            emit_main(j)
            if j >= 1:
                emit_p4(j - 1)
            if j + LOOK < n:
                emit_a12(j + LOOK)
        emit_p4(n - 1)
        P.run()

    def phase_hgrn(self, l):
        nc, cfg = self.nc, self.cfg
        P = Prog(nc, f"h{l}")
        S = cfg.S
        ntile = S // 128
        R = 31
        gbc = P.sbuf("gbc", [128, DH], F32); b_gbc = Buf()
        P.dma("sp", gbc[:], bcast_rows(self.hgain[l, :], 128), b_gbc, writes=[b_gbc])
        tf = [P.sbuf(f"tf{i}", [128, 4, 128], F32) for i in range(2)]; b_tf = [Buf(), Buf()]
        qs = [P.sbuf(f"qs{i}", [128, 4, 128], F32) for i in range(2)]; b_qs = [Buf(), Buf()]
        vt = [P.sbuf(f"vt{i}", [128, DH], BF16) for i in range(2)]; b_vt = [Buf(), Buf()]
        gg = [P.sbuf(f"gg{i}", [128, DH], F32) for i in range(2)]; b_gg = [Buf(), Buf()]
        def two(name, shape, dt):
            return [P.sbuf(f"{name}{i}", shape, dt) for i in range(2)], [Buf(), Buf()]
        ff, b_ff = two("ff", [128, 128], F32)
        lf, b_lf = two("lf", [128, 128], F32)
        kk, b_kk = two("kk", [128, 128], F32)
        bb, b_bb = two("bb", [128, 128], F32)
        eq, b_eq = two("eq", [128, 128], F32)
        ek, b_ek = two("ek", [128, 128], F32)
        sc_, b_sc = two("sc", [128, 8], F32)
        qzp, b_qzp = two("qzp", [128, 384], BF16)
        kt, b_kt = two("kt", [128, 128], BF16)
        ktok, b_ktok = two("ktok", [128, 128], BF16)
        pT, b_pT = two("pT", [128, 128], BF16)
        tmp, b_tmp = two("tmp", [128, 128], F32)
        sq_, b_sq = two("sqj", [128, 128], F32)
        ones = self.ones_f
        St = [P.sbuf(f"St{i}", [128, 128], F32) for i in range(4)]; b_St = [Buf() for _ in range(4)]
        Sb = [[P.sbuf(f"Sb{i}{j}", [128, 128], BF16) for j in range(2)] for i in range(4)]
        b_Sb = [[Buf(), Buf()] for _ in range(4)]
        ss = [P.sbuf(f"ss{i}", [128, 4], F32) for i in range(2)]; b_ss = [Buf(), Buf()]
        on = [P.sbuf(f"on{i}", [128, DH], F32) for i in range(2)]; b_on = [Buf(), Buf()]
        onb = [P.sbuf(f"onb{i}", [128, DH], BF16) for i in range(2)]; b_onb = [Buf(), Buf()]
        oTst = [P.sbuf(f"oTst{i}", [128, 4, 128], BF16) for i in range(2)]; b_oTst = [Buf(), Buf()]
        _pkt = P.psum("pkt", [128, 1024], BF16); bk = Buf(psum=True)
        p_kt = [_pkt[:, 0:128], _pkt[:, 0:128]]; b_pkt = [bk, bk]
        _psc = [P.psum(f"psc{i}", [128, 512]) for i in range(2)]; b_psc = [Buf(psum=True), Buf(psum=True)]
        p_sc = [t[:, 0:128] for t in _psc]
        _po = [P.psum(f"po{i}", [128, 512]) for i in range(2)]; b_po = [Buf(psum=True), Buf(psum=True)]
        p_o = [t[:, 0:128] for t in _po]
        _pkv = [P.psum(f"pkv{i}", [128, 512]) for i in range(2)]; b_pkv = [Buf(psum=True), Buf(psum=True)]
        p_kv = [t[:, 0:128] for t in _pkv]
        _pot = P.psum("pot", [128, 1024], BF16)
        p_ot = [_pot[:, 0:512].rearrange("p (h t) -> p h t", t=128)]; b_pot = [Buf(psum=True)]
        for i in range(2):
            P.op("pool", lambda e, i=i: e.memset(qzp[i][:], 0.0), writes=[b_qzp[i]])
        hk = 0
        for b in range(cfg.nseq):
            for h in range(4):
                P.op("pool", lambda e, h=h: e.memset(St[h][:], 0.0), writes=[b_St[h]])
            for ti in range(ntile):
                tk = (b * ntile + ti) % 2
                r0 = b * S + ti * 128
                P.dma("sp", tf[tk][:], self.hfT[:, r0:r0 + 128].rearrange("(h p) t -> p h t", p=128), b_tf[tk], writes=[b_tf[tk]])
                P.dma("sp", qs[tk][:], self.hqT[:, r0:r0 + 128].rearrange("(h p) t -> p h t", p=128), b_qs[tk], writes=[b_qs[tk]])
                P.dma("sp", vt[tk][:], self.hi_tm[r0:r0 + 128, :], b_vt[tk], writes=[b_vt[tk]])
                P.dma("sp", gg[tk][:], self.hg_tm[r0:r0 + 128, :], b_gg[tk], writes=[b_gg[tk]])
                P.op("pool", lambda e, tk=tk: e.tensor_tensor(out=gg[tk][:], in0=gg[tk][:], in1=gbc[:], op=ALU.mult),
                     reads=[b_gbc], writes=[b_gg[tk]])
                for h in range(4):
                    hp = hk % 2; hk += 1
                    A_ = self.lbA[:, l * 4 + h:l * 4 + h + 1]
                    B_ = self.lbB[:, l * 4 + h:l * 4 + h + 1]
                    P.op("dve", lambda e, hp=hp, tk=tk, h=h, A_=A_, B_=B_: e.tensor_scalar(
                        out=ff[hp][:], in0=tf[tk][:, h, :], scalar1=A_, scalar2=B_, op0=ALU.mult, op1=ALU.add),
                        reads=[b_tf[tk]], writes=[b_ff[hp]])
                    P.op("act", lambda e, hp=hp: e.activation(out=lf[hp][:], in_=ff[hp][:], func=AF.Ln),
                         reads=[b_ff[hp]], writes=[b_lf[hp]])
                    P.op("pool", lambda e, hp=hp: e.tensor_scalar(out=kk[hp][:], in0=ff[hp][:], scalar1=-1.0, scalar2=1.0,
                                                                   op0=ALU.mult, op1=ALU.add),
                         reads=[b_ff[hp]], writes=[b_kk[hp]])
                    for c in range(2):
                        P.op("dve", lambda e, hp=hp, c=c: e.tensor_tensor_scan(
                            out=bb[hp][:, c * 64:(c + 1) * 64], data0=ones[:, 0:64], data1=lf[hp][:, c * 64:(c + 1) * 64],
                            initial=0.0, op0=ALU.mult, op1=ALU.add), reads=[b_lf[hp]], writes=[b_bb[hp]])
                    for c in range(2):
                        P.op("dve", lambda e, hp=hp, c=c: e.tensor_scalar(
                            out=sc_[hp][:, 3 * c:3 * c + 1], in0=bb[hp][:, c * 64 + R:c * 64 + R + 1], scalar1=-1.0, scalar2=None,
                            op0=ALU.mult), reads=[b_bb[hp]], writes=[b_sc[hp]])
                    for c in range(2):
                        br = bb[hp][:, c * 64 + R:c * 64 + R + 1]
                        nbr = sc_[hp][:, 3 * c:3 * c + 1]
                        P.op("act", lambda e, hp=hp, c=c, nbr=nbr: e.activation(
                            out=eq[hp][:, c * 64:(c + 1) * 64], in_=bb[hp][:, c * 64:(c + 1) * 64], func=AF.Exp, bias=nbr, scale=1.0),
                            reads=[b_bb[hp], b_sc[hp]], writes=[b_eq[hp]])
                        P.op("act", lambda e, hp=hp, c=c, br=br: e.activation(
                            out=ek[hp][:, c * 64:(c + 1) * 64], in_=bb[hp][:, c * 64:(c + 1) * 64], func=AF.Exp, bias=br, scale=-1.0),
                            reads=[b_bb[hp]], writes=[b_ek[hp]])
                        P.op("act", lambda e, hp=hp, c=c, br=br: e.activation(
                            out=sc_[hp][:, 3 * c + 1:3 * c + 2], in_=br, func=AF.Exp), reads=[b_bb[hp]], writes=[b_sc[hp]])
                        P.op("act", lambda e, hp=hp, c=c: e.activation(
                            out=sc_[hp][:, 3 * c + 2:3 * c + 3], in_=bb[hp][:, c * 64 + 63:c * 64 + 64], func=AF.Exp),
                            reads=[b_bb[hp]], writes=[b_sc[hp]])
                    P.op("dve", lambda e, hp=hp, tk=tk, h=h: e.tensor_tensor(
                        out=qzp[hp][:].rearrange("p (c x) -> p c x", x=192)[:, :, 0:64],
                        in0=qs[tk][:, h, :].rearrange("p (c x) -> p c x", x=64),
                        in1=eq[hp][:].rearrange("p (c x) -> p c x", x=64), op=ALU.mult),
                        reads=[b_qs[tk], b_eq[hp]], writes=[b_qzp[hp]])
                    P.op("pool", lambda e, hp=hp: e.tensor_tensor(out=kt[hp][:], in0=kk[hp][:], in1=ek[hp][:], op=ALU.mult),
                         reads=[b_kk[hp], b_ek[hp]], writes=[b_kt[hp]])
                    P.op("pe", lambda e, hp=hp: e.transpose(p_kt[hp], kt[hp][:], self.identb[:]),
                         reads=[b_kt[hp]], writes=[b_pkt[hp]])
                    P.op("act", lambda e, hp=hp: e.activation(out=ktok[hp][:], in_=p_kt[hp], func=AF.Copy),
                         reads=[b_pkt[hp]], writes=[b_ktok[hp]])
                    P.op("pe", lambda e, hp=hp: e.matmul(
                        p_sc[hp].rearrange("p (c x) -> p c x", x=64), kt[hp][:],
                        qzp[hp][:].rearrange("p (c x) -> p c x", x=192)[:, :, 0:64], start=True, stop=True),
                        reads=[b_kt[hp], b_qzp[hp]], writes=[b_psc[hp]])
                    P.op("dve", lambda e, hp=hp: e.tensor_scalar(out=sq_[hp][:], in0=p_sc[hp], scalar1=1e30, scalar2=-1e30,
                                                                  op0=ALU.min, op1=ALU.max),
                         reads=[b_psc[hp]], writes=[b_sq[hp]])
                    P.op("dve", lambda e, hp=hp: e.tensor_tensor(out=pT[hp][:], in0=sq_[hp][:], in1=self.mask_bd[:], op=ALU.mult),
                         reads=[b_sq[hp]], writes=[b_pT[hp]])
                    P.op("pe", lambda e, hp=hp, tk=tk, h=h: e.matmul(p_o[hp], pT[hp][:], vt[tk][:, h * 128:(h + 1) * 128],
                                                                      start=True, stop=False),
                         reads=[b_pT[hp], b_vt[tk]], writes=[b_po[hp]], signal=False)
                    for c in range(2):
                        P.op("dve", lambda e, hp=hp, h=h, c=c: e.tensor_scalar(
                            out=Sb[h][c][:], in0=St[h][:], scalar1=sc_[hp][:, 3 * c + 1:3 * c + 2], scalar2=None, op0=ALU.mult),
                            reads=[b_St[h], b_sc[hp]], writes=[b_Sb[h][c]])
                        P.op("pe", lambda e, hp=hp, h=h, c=c: e.matmul(
                            p_o[hp], qzp[hp][:, c * 128:(c + 1) * 128], Sb[h][c][:], start=False, stop=(c == 1),
                            skip_group_check=True),
                            reads=[b_qzp[hp], b_Sb[h][c]], writes=[b_po[hp]], signal=(c == 1))
                        P.op("pe", lambda e, hp=hp, tk=tk, h=h, c=c: e.matmul(
                            p_kv[c], ktok[hp][c * 64:(c + 1) * 64, :], vt[tk][c * 64:(c + 1) * 64, h * 128:(h + 1) * 128],
                            start=True, stop=True), reads=[b_ktok[hp], b_vt[tk]], writes=[b_pkv[c]])
                        P.op("dve", lambda e, hp=hp, c=c: e.tensor_scalar(
                            out=tmp[c][:], in0=p_kv[c], scalar1=eq[hp][:, c * 64 + 63:c * 64 + 64], scalar2=None, op0=ALU.mult),
                            reads=[b_pkv[c], b_eq[hp]], writes=[b_tmp[c]])
                        P.op("dve", lambda e, hp=hp, h=h, c=c: e.scalar_tensor_tensor(
                            out=St[h][:], in0=St[h][:], scalar=sc_[hp][:, 3 * c + 2:3 * c + 3], in1=tmp[c][:],
                            op0=ALU.mult, op1=ALU.add), reads=[b_tmp[c], b_sc[hp]], writes=[b_St[h]])
                    if ti == DBG.get("dti", 0) and h == DBG.get("dh", 0) and b == 0:
                        self.dump(P, "d_ff", ff[hp][:], b_ff[hp], [128, 128])
                        self.dump(P, "d_lf", lf[hp][:], b_lf[hp], [128, 128])
                        self.dump(P, "d_bb", bb[hp][:], b_bb[hp], [128, 128])
                        self.dump(P, "d_eq", eq[hp][:], b_eq[hp], [128, 128])
                        self.dump(P, "d_ek", ek[hp][:], b_ek[hp], [128, 128])
                        self.dump(P, "d_sc", sc_[hp][:], b_sc[hp], [128, 8])
                        self.dump(P, "d_qzp", qzp[hp][:], b_qzp[hp], [128, 384], BF16)
                        self.dump(P, "d_kt", kt[hp][:], b_kt[hp], [128, 128], BF16)
                        self.dump(P, "d_ktok", ktok[hp][:], b_ktok[hp], [128, 128], BF16)
                        self.dump(P, "d_pT", pT[hp][:], b_pT[hp], [128, 128], BF16)
                        self.dump(P, "d_St", St[h][:], b_St[h], [128, 128])
                    P.op("act", lambda e, hp=hp, tk=tk, h=h: e.activation(
                        out=sq_[hp][:], in_=p_o[hp], func=AF.Square, accum_out=ss[tk][:, h:h + 1]),
                        reads=[b_po[hp]], writes=[b_sq[hp], b_ss[tk]])
                    P.op("dve", lambda e, hp=hp, tk=tk, h=h: e.tensor_tensor(
                        out=on[tk][:, h * 128:(h + 1) * 128], in0=p_o[hp], in1=gg[tk][:, h * 128:(h + 1) * 128], op=ALU.mult),
                        reads=[b_po[hp], b_gg[tk]], writes=[b_on[tk]])
                P.op("dve", lambda e, tk=tk: e.tensor_scalar(out=ss[tk][:], in0=ss[tk][:], scalar1=1.0 / 128.0, scalar2=RMS_EPS,
                                                              op0=ALU.mult, op1=ALU.add), reads=[b_ss[tk]], writes=[b_ss[tk]])
                P.op("act", lambda e, tk=tk: e.activation(out=ss[tk][:], in_=ss[tk][:], func=AF.Ln), reads=[b_ss[tk]], writes=[b_ss[tk]])
                P.op("act", lambda e, tk=tk: e.activation(out=ss[tk][:], in_=ss[tk][:], func=AF.Exp, scale=-0.5),
                     reads=[b_ss[tk]], writes=[b_ss[tk]])
                for h in range(4):
                    P.op("pool", lambda e, tk=tk, h=h: e.tensor_scalar(
                        out=onb[tk][:, h * 128:(h + 1) * 128], in0=on[tk][:, h * 128:(h + 1) * 128], scalar1=ss[tk][:, h:h + 1],
                        scalar2=None, op0=ALU.mult), reads=[b_ss[tk], b_on[tk]], writes=[b_onb[tk]])
                if ti == DBG.get("dti", 0) and b == 0:
                    self.dump(P, "d_on", on[tk][:], b_on[tk], [128, DH])
                    self.dump(P, "d_ss", ss[tk][:], b_ss[tk], [128, 4])
                    self.dump(P, "d_gg", gg[tk][:], b_gg[tk], [128, DH])
                for h in range(4):
                    P.op("pe", lambda e, tk=tk, h=h: e.transpose(p_ot[0][:, h, :], onb[tk][:, h * 128:(h + 1) * 128], self.identb[:]),
                         reads=[b_onb[tk]], writes=[b_pot[0]], signal=(h == 3))
                P.op("act", lambda e, tk=tk: e.activation(out=oTst[tk][:], in_=p_ot[0], func=AF.Copy),
                     reads=[b_pot[0]], writes=[b_oTst[tk]])
                P.dma("sp", self.oT[0:DH, r0:r0 + 128].rearrange("(h p) t -> p h t", p=128), oTst[tk][:], b_oTst[tk],
                      reads=[b_oTst[tk]])
        P.run()

    def rstd_from_var(self, P, mv, b_mv, rstd, b_rstd, n):
        P.op("dve", lambda e: e.tensor_scalar(out=rstd[:, 0:n], in0=mv[:, 0:n, 1], scalar1=LN_EPS, scalar2=None,
                                               op0=ALU.add), reads=[b_mv], writes=[b_rstd])
        P.op("act", lambda e: e.activation(out=rstd[:, 0:n], in_=rstd[:, 0:n], func=AF.Ln), reads=[b_rstd], writes=[b_rstd])
        P.op("act", lambda e: e.activation(out=rstd[:, 0:n], in_=rstd[:, 0:n], func=AF.Exp, scale=-0.5),
             reads=[b_rstd], writes=[b_rstd])

    def router_top2(self, P, lps, b_lps, rt, b_rt, comb, b_c, i):
        L, m1, k1, L2, m2, k2, dd, p1 = (rt[:, q, :] for q in range(8))
        P.op("dve", lambda e: e.tensor_copy(out=L, in_=lps[:, 0:NE]), reads=[b_lps], writes=[b_rt])
        P.op("dve", lambda e: e.tensor_reduce(out=m1[:, 0:1], in_=L, axis=AX.X, op=ALU.max), reads=[b_rt], writes=[b_rt])
        P.op("dve", lambda e: e.tensor_scalar(out=k1, in0=L, scalar1=m1[:, 0:1], scalar2=None, op0=ALU.is_equal),
             reads=[b_rt], writes=[b_rt])
        P.op("dve", lambda e: e.scalar_tensor_tensor(out=L2, in0=k1, scalar=-1e30, in1=L, op0=ALU.mult, op1=ALU.add),
             reads=[b_rt], writes=[b_rt])
        P.op("dve", lambda e: e.tensor_reduce(out=m2[:, 0:1], in_=L2, axis=AX.X, op=ALU.max), reads=[b_rt], writes=[b_rt])
        P.op("dve", lambda e: e.tensor_scalar(out=k2, in0=L2, scalar1=m2[:, 0:1], scalar2=None, op0=ALU.is_equal),
             reads=[b_rt], writes=[b_rt])
        P.op("dve", lambda e: e.tensor_tensor(out=dd[:, 0:1], in0=m2[:, 0:1], in1=m1[:, 0:1], op=ALU.subtract),
             reads=[b_rt], writes=[b_rt])
        P.op("act", lambda e: e.activation(out=dd[:, 1:2], in_=dd[:, 0:1], func=AF.Exp), reads=[b_rt], writes=[b_rt])
        P.op("dve", lambda e: e.tensor_scalar(out=dd[:, 2:3], in0=dd[:, 1:2], scalar1=1.0, scalar2=None, op0=ALU.add),
             reads=[b_rt], writes=[b_rt])
        P.op("dve", lambda e: e.reciprocal(out=p1[:, 0:1], in_=dd[:, 2:3]), reads=[b_rt], writes=[b_rt])
        P.op("dve", lambda e: e.tensor_scalar(out=p1[:, 1:2], in0=p1[:, 0:1], scalar1=-1.0, scalar2=1.0,
                                               op0=ALU.mult, op1=ALU.add), reads=[b_rt], writes=[b_rt])
        P.op("dve", lambda e: e.tensor_scalar(out=comb[:, i, :], in0=k1, scalar1=p1[:, 0:1], scalar2=None, op0=ALU.mult),
             reads=[b_rt], writes=[b_c])
        P.op("dve", lambda e: e.scalar_tensor_tensor(out=comb[:, i, :], in0=k2, scalar=p1[:, 1:2], in1=comb[:, i, :],
                                                      op0=ALU.mult, op1=ALU.add), reads=[b_rt], writes=[b_c])


def const_tables():
    s = np.arange(128)[:, None]
    t = np.arange(128)[None, :]
    c = np.zeros((128, 5, 128), np.float32)
    c[:, 0, :] = (s == t)
    c[:, 1, :] = ((s // CHUNK) == (t // CHUNK)) & (s <= t)
    c[:, 2, :] = (s < t)
    c[:, 3, :] = -1.0 * (s >= t)
    c[:, 4, :] = -1.0 * (t == 64)
    return c


def core_inputs(inp, b0, nseq, S):
    f = lambda a: np.ascontiguousarray(np.asarray(a, dtype=np.float32))
    x = f(inp["x"])[b0:b0 + nseq, :S].reshape(nseq * S, D)
    c = f(inp["c"])[b0:b0 + nseq]
    cT = np.ascontiguousarray(c.T.reshape(8, 128, nseq).transpose(1, 0, 2))
    lb = f(inp["hgrn_lb_logits"])
    lbT = np.ascontiguousarray(lb.T.reshape(NH_H, 128, DEPTH).transpose(1, 0, 2))
    m = {
        "x": np.ascontiguousarray(x), "cT": cT, "lbT": lbT, "consts": const_tables(),
        "w_ada": f(inp["w_ada"]), "b_ada": f(inp["b_ada"]), "w_in": f(inp["w_in"]), "w_out": f(inp["w_out"]),
        "hgain": f(inp["hgrn_norm_gain"]),
        "w_dense_gate": f(inp["w_dense_gate"]), "w_dense_up": f(inp["w_dense_up"]),
        "w_dense_down": f(inp["w_dense_down"]),
        "w_router": np.ascontiguousarray(f(inp["w_router"]).reshape(2, 8, 128, NE).transpose(0, 2, 1, 3)),
        "w_moe_gate": f(inp["w_moe_gate"]), "w_moe_up": f(inp["w_moe_up"]), "w_moe_down": f(inp["w_moe_down"]),
        "ln_gain": f(inp["ln_gain"]), "ln_bias": f(inp["ln_bias"]),
    }
    return m


def kernel(**inputs):
    cfg = Cfg()
    nc = Builder(cfg).build()
    in_maps = [core_inputs(inputs, 2 * c, 2, 4096) for c in range(NCORES)]
    res = run_bass_kernel_spmd(nc, in_maps, core_ids=list(range(NCORES)))
    outs = [np.asarray(r["out"], dtype=np.float32).reshape(2, 4096, D) for r in res.results]
    return np.concatenate(outs, axis=0)
```

```python
import contextlib
import numpy as np
import ml_dtypes
import concourse.bass as bass
import concourse.mybir as mybir
from concourse.bass_utils import run_bass_kernel_spmd

F32 = mybir.dt.float32
BF16 = mybir.dt.bfloat16
AF = mybir.ActivationFunctionType
ALU = mybir.AluOpType
AX = mybir.AxisListType

D = 1024
DEPTH = 4
DH = 512
NH_H = 4
DSB = 512
NH_S = 8
DIN = 3584
FF_DENSE = 2816
FF_MOE = 3584
NE = 8
ALPHA = float((2 * DEPTH) ** 0.25)
LN_EPS = 1e-5
RMS_EPS = 1e-6
CHUNK = 64
NCORES = 8
DBG = {}


class Buf:
    __slots__ = ("name", "w", "r", "dsem", "dcnt", "psum")

    def __init__(self, name="", psum=False):
        self.name = name
        self.psum = psum
        self.w = None
        self.r = []
        self.dsem = None
        self.dcnt = 0


class _Eng:
    def __init__(self, name, sem):
        self.name = name
        self.sem = sem
        self.count = 0
        self.ops = []
        self.waited = {}


class Prog:
    ENG = ("pe", "act", "dve", "pool", "sp")

    def __init__(self, nc, name):
        self.nc = nc
        self.name = name
        self.stack = contextlib.ExitStack()
        self.eng = {}
        self.sems = []
        for e in self.ENG:
            sem = nc.alloc_semaphore(name=f"{name}_{e}")
            self.sems.append(sem)
            self.eng[e] = _Eng(e, sem)
        self.dma_toks = []
        self.nsem = 5

    def sbuf(self, name, shape, dt):
        return self.stack.enter_context(self.nc.sbuf_tensor(f"{self.name}_{name}", list(shape), dt))

    def psum(self, name, shape, dt=F32):
        return self.stack.enter_context(self.nc.psum_tensor(f"{self.name}_{name}", list(shape), dt))

    def _wait(self, e, tok):
        if tok is None:
            return
        sem, val, owner = tok
        if owner == "pe" and e.name == "pe":
            return
        key = id(sem)
        if e.waited.get(key, 0) >= val:
            return
        e.waited[key] = val
        e.ops.append(lambda eng, s=sem, v=val: eng.wait_ge(s, v))

    def _deps(self, e, reads, writes, extra):
        for b in reads:
            self._wait(e, b.w)
            if b.psum:
                for t in b.r:
                    if t[2] != e.name:
                        self._wait(e, t)
        for b in writes:
            self._wait(e, b.w)
            for t in b.r:
                self._wait(e, t)
        for t in extra:
            self._wait(e, t)

    def op(self, engine, fn, reads=(), writes=(), extra=(), signal=True):
        e = self.eng[engine]
        self._deps(e, reads, writes, extra)
        if signal:
            e.count += 1
            tok = (e.sem, e.count, engine)
            e.ops.append(lambda eng, f=fn, s=e.sem: f(eng).then_inc(s, 1))
        else:
            tok = (e.sem, e.count + 1, engine)
            e.ops.append(lambda eng, f=fn: f(eng))
        for b in writes:
            b.w = tok
            b.r = []
        for b in reads:
            b.r.append(tok)
        return tok

    def dma(self, queue, out, in_, owner, reads=(), writes=(), extra=(), **kw):
        e = self.eng[queue]
        self._deps(e, reads, writes, extra)
        if owner.dsem is None:
            owner.dsem = self.nc.alloc_semaphore(name=f"{self.name}_d{self.nsem}")
            self.sems.append(owner.dsem)
            self.nsem += 1
            owner.dcnt = 0
        owner.dcnt += 16
        tok = (owner.dsem, owner.dcnt, "dma")
        e.ops.append(lambda eng, o=out, i=in_, s=owner.dsem, k=kw: eng.dma_start(out=o, in_=i, **k).then_inc(s, 16))
        for b in writes:
            b.w = tok
            b.r = []
        for b in reads:
            b.r.append(tok)
        self.dma_toks.append(tok)
        return tok

    def run(self):
        nc = self.nc
        sp = self.eng["sp"]
        for t in self.dma_toks:
            self._wait(sp, t)
        with nc.Block() as block:
            @block.tensor
            def _(t):
                for f in self.eng["pe"].ops:
                    f(t)

            @block.scalar
            def _(a):
                for f in self.eng["act"].ops:
                    f(a)

            @block.vector
            def _(v):
                for f in self.eng["dve"].ops:
                    f(v)

            @block.gpsimd
            def _(g):
                for f in self.eng["pool"].ops:
                    f(g)

            @block.sync
            def _(s):
                for f in self.eng["sp"].ops:
                    f(s)
        nc.clear_and_free_semaphores(self.sems)
        nc.all_engine_barrier()
        self.stack.close()


class Cfg:
    def __init__(self, nseq=2, S=4096, layers=(0, 1, 2, 3), debug=()):
        self.nseq = nseq
        self.S = S
        self.ntok = nseq * S
        self.layers = tuple(layers)
        self.debug = tuple(debug)


def bcast_rows(ap1d, nparts):
    return ap1d.partition_broadcast(nparts)


class Builder:
    def __init__(self, cfg):
        self.cfg = cfg
        nc = self.nc = bass.Bass("TRN2", target_bir_lowering=False)
        NT = cfg.ntok
        ns = cfg.nseq

        def din(name, shape, dt=F32):
            return nc.dram_tensor(name, list(shape), dt, kind="ExternalInput").ap()

        def scratch(name, shape, dt=F32):
            kind = "ExternalOutput" if name in cfg.debug else "Internal"
            return nc.dram_tensor(name, list(shape), dt, kind=kind).ap()

        self.x_in = din("x", [NT, D])
        self.cT = din("cT", [128, 8, ns])
        self.w_ada = din("w_ada", [DEPTH, 2, D, 3 * D])
        self.b_ada = din("b_ada", [DEPTH, 2, 3 * D])
        self.w_in = din("w_in", [DEPTH, D, DIN])
        self.w_out = din("w_out", [DEPTH, D, D])
        self.lbT = din("lbT", [128, NH_H, DEPTH])
        self.hgain = din("hgain", [DEPTH, DH])
        self.wdg = din("w_dense_gate", [2, D, FF_DENSE])
        self.wdu = din("w_dense_up", [2, D, FF_DENSE])
        self.wdd = din("w_dense_down", [2, FF_DENSE, D])
        self.w_router = din("w_router", [2, 128, 8, NE])
        self.wmg = din("w_moe_gate", [2, NE, D, FF_MOE])
        self.wmu = din("w_moe_up", [2, NE, D, FF_MOE])
        self.wmd = din("w_moe_down", [2, NE, FF_MOE, D])
        self.ln_gain = din("ln_gain", [DEPTH, 2, D])
        self.ln_bias = din("ln_bias", [DEPTH, 2, D])
        self.consts = din("consts", [128, 5, 128])

        self.out = nc.dram_tensor("out", [NT, D], F32, kind="ExternalOutput").ap()
        self.xa = scratch("xa", [NT, D])
        self.xb = scratch("xb", [NT, D])
        self.mod = scratch("mod", [ns, 8, 3 * D])
        self.hqT = scratch("hqT", [DH, NT])
        self.hfT = scratch("hfT", [DH, NT])
        self.hi_tm = scratch("hi_tm", [NT, DH], BF16)
        self.hg_tm = scratch("hg_tm", [NT, DH])
        self.sqT = scratch("sqT", [DSB, NT], BF16)
        self.skT = scratch("skT", [DSB, NT], BF16)
        self.sv_tm = scratch("sv_tm", [NT, DSB], BF16)
        self.oT = scratch("oT", [D, NT], BF16)

    def dump(self, P, name, ap, buf, shape, dt=F32):
        if name not in self.cfg.debug:
            return
        t = self.nc.dram_tensor(name, list(shape), dt, kind="ExternalOutput").ap()
        P.dma("sp", t, ap, buf, reads=[buf])

    def build(self):
        nc = self.nc
        cfg = self.cfg
        with contextlib.ExitStack() as top:
            def pt(name, shape, dt):
                return top.enter_context(nc.sbuf_tensor(name, list(shape), dt))
            self.identf = pt("identf", [128, 128], F32)
            self.identb = pt("identb", [128, 128], BF16)
            self.mask_bd = pt("mask_bd", [128, 128], F32)
            self.mask_st = pt("mask_st", [128, 128], F32)
            self.tri_neg = pt("tri_neg", [128, 128], BF16)
            self.neg_col = pt("neg_col", [128, 128], BF16)
            self.lbA = pt("lbA", [128, NH_H * DEPTH], F32)
            self.lbB = pt("lbB", [128, NH_H * DEPTH], F32)
            self.ones_f = pt("ones_f", [128, 128], F32)
            self.zeros_b = pt("zeros_b", [128, 512], BF16)
            self.ones_b = pt("ones_b", [128, 128], BF16)

            self.phase_setup()
            xcur = self.x_in
            for li, l in enumerate(cfg.layers):
                last = li == len(cfg.layers) - 1
                self.phase_inproj(l, xcur)
                self.phase_hgrn(l)
                self.phase_sb(l)
                self.phase_outproj(l, xcur, self.xa)
                self.phase_ffn(l, self.xa, self.out if last else self.xb)
                xcur = self.xb
        return nc

    def phase_setup(self):
        nc, cfg = self.nc, self.cfg
        P = Prog(nc, "p0")
        ns = cfg.nseq
        cst = P.sbuf("cst", [128, 5, 128], F32)
        b_cst = Buf("cst")
        P.dma("sp", cst[:], self.consts, b_cst, writes=[b_cst])
        bp = Buf("persist")
        P.op("dve", lambda e: e.tensor_copy(out=self.identf[:], in_=cst[:, 0, :]), reads=[b_cst], writes=[bp])
        P.op("dve", lambda e: e.tensor_copy(out=self.identb[:], in_=cst[:, 0, :]), reads=[b_cst], writes=[bp])
        P.op("dve", lambda e: e.tensor_copy(out=self.mask_bd[:], in_=cst[:, 1, :]), reads=[b_cst], writes=[bp])
        P.op("dve", lambda e: e.tensor_copy(out=self.mask_st[:], in_=cst[:, 2, :]), reads=[b_cst], writes=[bp])
        P.op("dve", lambda e: e.tensor_copy(out=self.tri_neg[:], in_=cst[:, 3, :]), reads=[b_cst], writes=[bp])
        P.op("dve", lambda e: e.tensor_copy(out=self.neg_col[:], in_=cst[:, 4, :]), reads=[b_cst], writes=[bp])
        P.op("dve", lambda e: e.memset(self.ones_f[:], 1.0), writes=[bp])
        P.op("dve", lambda e: e.memset(self.ones_b[:], 1.0), writes=[bp])
        P.op("dve", lambda e: e.memset(self.zeros_b[:], 0.0), writes=[bp])

        lg = P.sbuf("lg", [128, NH_H, DEPTH], F32)
        ex = P.sbuf("ex", [128, NH_H, DEPTH], F32)
        sm = P.sbuf("sm", [128, NH_H], F32)
        lbt = P.sbuf("lbt", [128, NH_H, DEPTH], F32)
        b_lg, b_ex, b_sm, b_lb = Buf(), Buf(), Buf(), Buf()
        P.dma("sp", lg[:], self.lbT, b_lg, writes=[b_lg])
        P.op("act", lambda e: e.activation(out=ex[:], in_=lg[:], func=AF.Exp), reads=[b_lg], writes=[b_ex])
        P.op("dve", lambda e: e.tensor_reduce(out=sm[:], in_=ex[:], axis=AX.X, op=ALU.add), reads=[b_ex], writes=[b_sm])
        P.op("dve", lambda e: e.reciprocal(out=sm[:], in_=sm[:]), reads=[b_sm], writes=[b_sm])
        for h in range(NH_H):
            P.op("dve", lambda e, h=h: e.tensor_scalar(out=ex[:, h, :], in0=ex[:, h, :], scalar1=sm[:, h:h + 1],
                                                        scalar2=None, op0=ALU.mult),
                 reads=[b_sm, b_ex], writes=[b_ex])
        P.op("dve", lambda e: e.memset(lbt[:, :, 0:1], 0.0), writes=[b_lb])
        for l in range(1, DEPTH):
            P.op("dve", lambda e, l=l: e.tensor_tensor(out=lbt[:, :, l:l + 1], in0=lbt[:, :, l - 1:l],
                                                        in1=ex[:, :, l:l + 1], op=ALU.add),
                 reads=[b_ex, b_lb], writes=[b_lb])
        for l in range(DEPTH):
            P.op("dve", lambda e, l=l: e.tensor_scalar(out=self.lbA[:, l * 4:(l + 1) * 4], in0=lbt[:, :, l],
                                                        scalar1=-0.5, scalar2=0.5, op0=ALU.mult, op1=ALU.add),
                 reads=[b_lb], writes=[bp])
            P.op("dve", lambda e, l=l: e.tensor_scalar(out=self.lbB[:, l * 4:(l + 1) * 4], in0=lbt[:, :, l],
                                                        scalar1=0.5, scalar2=0.5, op0=ALU.mult, op1=ALU.add),
                 reads=[b_lb], writes=[bp])

        ct = P.sbuf("ct", [128, 8, ns], F32)
        sct = P.sbuf("sct", [128, 8, ns], F32)
        b_ct, b_sct = Buf(), Buf()
        P.dma("sp", ct[:], self.cT, b_ct, writes=[b_ct])
        P.op("act", lambda e: e.activation(out=sct[:], in_=ct[:], func=AF.Exp, scale=-1.0), reads=[b_ct], writes=[b_sct])
        P.op("dve", lambda e: e.tensor_scalar(out=sct[:], in0=sct[:], scalar1=1.0, scalar2=None, op0=ALU.add),
             reads=[b_sct], writes=[b_sct])
        P.op("dve", lambda e: e.reciprocal(out=sct[:], in_=sct[:]), reads=[b_sct], writes=[b_sct])
        P.op("dve", lambda e: e.tensor_tensor(out=sct[:], in0=sct[:], in1=ct[:], op=ALU.mult),
             reads=[b_sct, b_ct], writes=[b_sct])
        wbuf = [P.sbuf(f"wa{i}", [128, 8, 512], F32) for i in range(2)]
        b_w = [Buf(), Buf()]
        bias = [P.sbuf(f"bias{i}", [ns, 3 * D], F32) for i in range(2)]
        b_bias = [Buf(), Buf()]
        mt = [P.sbuf(f"mt{i}", [ns, 3 * D], F32) for i in range(2)]
        b_mt = [Buf(), Buf()]
        ps = [P.psum(f"ps{i}", [128, 512]) for i in range(2)]
        b_ps = [Buf(psum=True), Buf(psum=True)]
        k = 0
        for ls in range(8):
            l, s = divmod(ls, 2)
            if l not in cfg.layers:
                continue
            bi = ls % 2
            P.dma("sp", bias[bi][:], bcast_rows(self.b_ada[l, s, :], ns), b_bias[bi], writes=[b_bias[bi]])
            for n in range(6):
                wi = k % 2
                k += 1
                src = self.w_ada[l, s, :, n * 512:(n + 1) * 512].rearrange("(c p) f -> p c f", p=128)
                P.dma("sp", wbuf[wi][:], src, b_w[wi], writes=[b_w[wi]])
                for dc in range(8):
                    P.op("pe", lambda e, wi=wi, dc=dc: e.matmul(ps[wi][0:ns, :], sct[:, dc, :], wbuf[wi][:, dc, :],
                                                                  start=(dc == 0), stop=(dc == 7)),
                         reads=[b_sct, b_w[wi]], writes=[b_ps[wi]], signal=(dc == 7))
                P.op("dve", lambda e, wi=wi, bi=bi, n=n: e.tensor_tensor(
                    out=mt[bi][:, n * 512:(n + 1) * 512], in0=ps[wi][0:ns, :], in1=bias[bi][:, n * 512:(n + 1) * 512],
                    op=ALU.add), reads=[b_ps[wi], b_bias[bi]], writes=[b_mt[bi]])
            P.op("dve", lambda e, bi=bi: e.tensor_scalar(out=mt[bi][:, D:2 * D], in0=mt[bi][:, D:2 * D], scalar1=1.0,
                                                          scalar2=None, op0=ALU.add), reads=[b_mt[bi]], writes=[b_mt[bi]])
            P.dma("sp", self.mod[:, ls, :], mt[bi][:], b_mt[bi], reads=[b_mt[bi]])
        P.run()

    def load_bcast(self, P, tile, buf, src1d):
        P.dma("sp", tile[:], bcast_rows(src1d, 128), buf, writes=[buf])

    def phase_ffn(self, l, x_src, x_dst):
        nc, cfg = self.nc, self.cfg
        moe = (l % 2 == 1)
        j = l // 2
        P = Prog(nc, f"f{l}")
        TB = 1024
        NTI = TB // 128
        nblk = cfg.ntok // TB
        blk_per_seq = cfg.S // TB
        FF = FF_MOE if moe else FF_DENSE
        nfc = FF // 128
        groups = [(g0, min(4, nfc - g0)) for g0 in range(0, nfc, 4)]
        nexp = NE if moe else 1
        nexp = DBG.get('nexp', nexp)

        sc1 = P.sbuf("sc1", [128, D], F32); sh = P.sbuf("sh", [128, D], F32); gt = P.sbuf("gt", [128, D], F32)
        lng = P.sbuf("lng", [128, D], F32); lnb = P.sbuf("lnb", [128, D], F32)
        b_mod, b_ln = Buf("mod"), Buf("ln")
        xt = [P.sbuf(f"xt{i}", [128, D], F32) for i in range(2)]
        b_xt = [Buf(), Buf()]
        hf = [P.sbuf(f"hf{i}", [128, D], F32) for i in range(2)]
        b_hf = [Buf(), Buf()]
        hT = P.sbuf("hT", [128, 8, TB], BF16)
        b_hT = [Buf() for _ in range(NTI)]
        yacc = P.sbuf("yacc", [128, NTI, D], F32)
        b_y = [[Buf(), Buf()] for _ in range(NTI)]
        wg = [P.sbuf(f"wg{i}", [128, 8, 512], BF16) for i in range(2)]
        wu = [P.sbuf(f"wu{i}", [128, 8, 512], BF16) for i in range(2)]
        wd = [P.sbuf(f"wd{i}", [128, 4, D], BF16) for i in range(2)]
        b_wgu = [Buf(), Buf()]
        b_wd = [Buf(), Buf()]
        aT = [P.sbuf(f"aT{i}", [128, 4, TB], BF16) for i in range(2)]
        b_aT = [[Buf(), Buf()] for _ in range(2)]
        sg = [P.sbuf(f"sg{i}", [128, 512], F32) for i in range(2)]
        b_sg = [Buf(), Buf()]
        ot = [P.sbuf(f"ot{i}", [128, D], F32) for i in range(2)]
        b_ot = [Buf(), Buf()]
        stats = P.sbuf("stats", [128, NTI, 2, 6], F32)
        mv = P.sbuf("mv", [128, NTI, 2], F32)
        rstd = P.sbuf("rstd", [128, NTI], F32)
        b_stats = [Buf() for _ in range(NTI)]
        b_mv = Buf(); b_rstd = Buf()
        if moe:
            hTf = P.sbuf("hTf", [128, 8, 128], F32); b_hTf = Buf()
            wr = P.sbuf("wr", [128, 8, NE], F32); b_wr = Buf()
            comb = P.sbuf("comb", [128, NTI, NE], F32); b_comb = [Buf() for _ in range(NTI)]
            rt = P.sbuf("rt", [128, 8, NE], F32); b_rt = Buf()
            if DBG.get("router", 1):
                P.dma("sp", wr[:], self.w_router[j], b_wr, writes=[b_wr])
        psT = [P.psum(f"psT{i}", [128, 512]) for i in range(2)]; b_psT = [Buf(psum=True), Buf(psum=True)]
        gps = [P.psum(f"gps{i}", [128, 512]) for i in range(2)]; b_gps = [Buf(psum=True), Buf(psum=True)]
        ups = [P.psum(f"ups{i}", [128, 512]) for i in range(2)]; b_ups = [Buf(psum=True), Buf(psum=True)]
        yps = [P.psum(f"yps{i}", [128, 512]) for i in range(2)]; b_yps = [Buf(psum=True), Buf(psum=True)]

        P.dma("sp", lng[:], bcast_rows(self.ln_gain[l, 1, :], 128), b_ln, writes=[b_ln])
        P.dma("sp", lnb[:], bcast_rows(self.ln_bias[l, 1, :], 128), b_ln, writes=[b_ln])

        if moe:
            WG, WU, WD = self.wmg[j], self.wmu[j], self.wmd[j]
        else:
            WG, WU, WD = self.wdg[j:j + 1], self.wdu[j:j + 1], self.wdd[j:j + 1]

        gcount = 0
        xk = 0
        for blk in range(nblk):
            t0 = blk * TB
            bseq = blk // blk_per_seq
            if blk % blk_per_seq == 0:
                ls = l * 2 + 1
                P.dma("sp", sh[:], bcast_rows(self.mod[bseq, ls, 0:D], 128), b_mod, writes=[b_mod])
                P.dma("sp", sc1[:], bcast_rows(self.mod[bseq, ls, D:2 * D], 128), b_mod, writes=[b_mod])
                P.dma("sp", gt[:], bcast_rows(self.mod[bseq, ls, 2 * D:3 * D], 128), b_mod, writes=[b_mod])
            for i in range(NTI):
                xi = xk % 2
                xk += 1
                r0 = t0 + i * 128
                P.dma("sp", xt[xi][:], x_src[r0:r0 + 128, :], b_xt[xi], writes=[b_xt[xi]])
                P.op("dve", lambda e, xi=xi: e.tensor_tensor(out=hf[xi][:], in0=xt[xi][:], in1=sc1[:], op=ALU.mult),
                     reads=[b_xt[xi], b_mod], writes=[b_hf[xi]])
                P.op("pool", lambda e, xi=xi: e.tensor_tensor(out=hf[xi][:], in0=hf[xi][:], in1=sh[:], op=ALU.add),
                     reads=[b_hf[xi], b_mod], writes=[b_hf[xi]])
                for hb in range(2):
                    for q in range(4):
                        dc = hb * 4 + q
                        P.op("pe", lambda e, xi=xi, hb=hb, q=q, dc=dc: e.transpose(
                            psT[hb][:, q * 128:(q + 1) * 128], hf[xi][:, dc * 128:(dc + 1) * 128], self.identf[:]),
                            reads=[b_hf[xi]], writes=[b_psT[hb]], signal=(q == 3))
                    if moe:
                        P.op("act", lambda e, hb=hb: e.activation(
                            out=hTf[:, hb * 4:(hb + 1) * 4, :], in_=psT[hb][:].rearrange("p (c t) -> p c t", t=128),
                            func=AF.Copy), reads=[b_psT[hb]], writes=[b_hTf])
                    P.op("act" if not moe else "dve", lambda e, hb=hb, i=i: e.tensor_copy(
                        out=hT[:, hb * 4:(hb + 1) * 4, i * 128:(i + 1) * 128],
                        in_=psT[hb][:].rearrange("p (c t) -> p c t", t=128)) if moe else e.activation(
                        out=hT[:, hb * 4:(hb + 1) * 4, i * 128:(i + 1) * 128],
                        in_=psT[hb][:].rearrange("p (c t) -> p c t", t=128), func=AF.Copy),
                        reads=[b_psT[hb]], writes=[b_hT[i]])
                if moe and DBG.get("router", 1) == 0:
                    P.op("dve", lambda e, i=i: e.memset(comb[:, i, :], 0.125), writes=[b_comb[i]])
                if moe and DBG.get("router", 1):
                    for dc in range(8):
                        P.op("pe", lambda e, dc=dc: e.matmul(yps[1][:, 0:NE], hTf[:, dc, :], wr[:, dc, :],
                                                              start=(dc == 0), stop=(dc == 7)),
                             reads=[b_hTf, b_wr], writes=[b_yps[1]], signal=(dc == 7))
                    self.router_top2(P, yps[1], b_yps[1], rt, b_rt, comb, b_comb[i], i)
            for ex in range(nexp):
                for (g0, ng) in groups:
                    slot = gcount % 2
                    gcount += 1
                    f0 = g0 * 128
                    fw = ng * 128
                    P.dma("pool", wg[slot][:, :, 0:fw], WG[ex, :, f0:f0 + fw].rearrange("(c p) f -> p c f", p=128),
                          b_wgu[slot], writes=[b_wgu[slot]])
                    P.dma("pool", wu[slot][:, :, 0:fw], WU[ex, :, f0:f0 + fw].rearrange("(c p) f -> p c f", p=128),
                          b_wgu[slot], writes=[b_wgu[slot]])
                    P.dma("pool", wd[slot][:, 0:ng, :], WD[ex, f0:f0 + fw, :].rearrange("(c p) d -> p c d", p=128),
                          b_wd[slot], writes=[b_wd[slot]])
                    for fc in range(ng):
                        for half in range(2):
                            pb = (fc * 2 + half) % 2
                            for dc in range(8):
                                P.op("pe", lambda e, slot=slot, fc=fc, half=half, dc=dc, pb=pb: e.matmul(
                                    gps[pb][:], wg[slot][:, dc, fc * 128:(fc + 1) * 128],
                                    hT[:, dc, half * 512:(half + 1) * 512], start=(dc == 0), stop=(dc == 7)),
                                    reads=[b_wgu[slot]] + b_hT[half * 4:(half + 1) * 4], writes=[b_gps[pb]],
                                    signal=(dc == 7))
                            for dc in range(8):
                                P.op("pe", lambda e, slot=slot, fc=fc, half=half, dc=dc, pb=pb: e.matmul(
                                    ups[pb][:], wu[slot][:, dc, fc * 128:(fc + 1) * 128],
                                    hT[:, dc, half * 512:(half + 1) * 512], start=(dc == 0), stop=(dc == 7)),
                                    reads=[b_wgu[slot]] + b_hT[half * 4:(half + 1) * 4], writes=[b_ups[pb]],
                                    signal=(dc == 7))
                            P.op("act", lambda e, pb=pb: e.activation(out=sg[pb][:], in_=gps[pb][:], func=AF.Silu),
                                 reads=[b_gps[pb]], writes=[b_sg[pb]])
                            P.op("dve", lambda e, pb=pb, slot=slot, fc=fc, half=half: e.tensor_tensor(
                                out=aT[slot][:, fc, half * 512:(half + 1) * 512], in0=ups[pb][:], in1=sg[pb][:],
                                op=ALU.mult), reads=[b_ups[pb], b_sg[pb]], writes=[b_aT[slot][half]])
                    first = (ex == 0 and g0 == 0)
                    for i in range(NTI):
                        for h2 in range(2):
                            pb = (i * 2 + h2) % 2
                            for fc in range(ng):
                                P.op("pe", lambda e, slot=slot, fc=fc, i=i, h2=h2, pb=pb, ng=ng: e.matmul(
                                    yps[pb][:], aT[slot][:, fc, i * 128:(i + 1) * 128],
                                    wd[slot][:, fc, h2 * 512:(h2 + 1) * 512], start=(fc == 0), stop=(fc == ng - 1)),
                                    reads=[b_aT[slot][i // 4], b_wd[slot]], writes=[b_yps[pb]], signal=(fc == ng - 1))
                            ysl = yacc[:, i, h2 * 512:(h2 + 1) * 512]
                            if moe:
                                csc = comb[:, i, ex:ex + 1]
                                if first:
                                    P.op("dve", lambda e, pb=pb, ysl=ysl, csc=csc: e.tensor_scalar(
                                        out=ysl, in0=yps[pb][:], scalar1=csc, scalar2=None, op0=ALU.mult),
                                        reads=[b_yps[pb], b_comb[i]], writes=[b_y[i][h2]])
                                else:
                                    P.op("dve", lambda e, pb=pb, ysl=ysl, csc=csc: e.scalar_tensor_tensor(
                                        out=ysl, in0=yps[pb][:], scalar=csc, in1=ysl, op0=ALU.mult, op1=ALU.add),
                                        reads=[b_yps[pb], b_comb[i]], writes=[b_y[i][h2]])
                            else:
                                if first:
                                    P.op("act", lambda e, pb=pb, ysl=ysl: e.activation(out=ysl, in_=yps[pb][:], func=AF.Copy),
                                         reads=[b_yps[pb]], writes=[b_y[i][h2]])
                                else:
                                    P.op("dve", lambda e, pb=pb, ysl=ysl: e.tensor_tensor(
                                        out=ysl, in0=yps[pb][:], in1=ysl, op=ALU.add),
                                        reads=[b_yps[pb]], writes=[b_y[i][h2]])
            for i in range(NTI):
                xi = xk % 2
                xk += 1
                r0 = t0 + i * 128
                P.dma("sp", xt[xi][:], x_src[r0:r0 + 128, :], b_xt[xi], writes=[b_xt[xi]])
                P.op("pool", lambda e, i=i: e.tensor_tensor(out=yacc[:, i, :], in0=yacc[:, i, :], in1=gt[:], op=ALU.mult),
                     reads=[b_mod], writes=b_y[i])
                P.op("dve", lambda e, i=i, xi=xi: e.scalar_tensor_tensor(
                    out=yacc[:, i, :], in0=xt[xi][:], scalar=ALPHA, in1=yacc[:, i, :], op0=ALU.mult, op1=ALU.add),
                    reads=[b_xt[xi]], writes=b_y[i])
                for h2 in range(2):
                    P.op("dve", lambda e, i=i, h2=h2: e.bn_stats(out=stats[:, i, h2, :], in_=yacc[:, i, h2 * 512:(h2 + 1) * 512]),
                         reads=b_y[i], writes=[b_stats[i]])
                P.op("dve", lambda e, i=i: e.bn_aggr(out=mv[:, i, :], in_=stats[:, i, :, :].rearrange("p a b -> p (a b)")),
                     reads=[b_stats[i]], writes=[b_mv])
            self.rstd_from_var(P, mv, b_mv, rstd, b_rstd, NTI)
            for i in range(NTI):
                oi = i % 2
                r0 = t0 + i * 128
                P.op("dve", lambda e, i=i, oi=oi: e.tensor_scalar(
                    out=ot[oi][:], in0=yacc[:, i, :], scalar1=mv[:, i, 0:1], scalar2=rstd[:, i:i + 1],
                    op0=ALU.subtract, op1=ALU.mult), reads=b_y[i] + [b_mv, b_rstd], writes=[b_ot[oi]])
                P.op("pool", lambda e, oi=oi: e.tensor_tensor(out=ot[oi][:], in0=ot[oi][:], in1=lng[:], op=ALU.mult),
                     reads=[b_ln], writes=[b_ot[oi]])
                P.op("pool", lambda e, oi=oi: e.tensor_tensor(out=ot[oi][:], in0=ot[oi][:], in1=lnb[:], op=ALU.add),
                     reads=[b_ln], writes=[b_ot[oi]])
                P.dma("sp", x_dst[r0:r0 + 128, :], ot[oi][:], b_ot[oi], reads=[b_ot[oi]])
        P.run()

    def phase_inproj(self, l, x_src):
        nc, cfg = self.nc, self.cfg
        P = Prog(nc, f"i{l}")
        TB = 512
        nblk = cfg.ntok // TB
        blk_per_seq = cfg.S // TB
        w = P.sbuf("w", [128, 8, DIN], BF16); b_w = Buf()
        for n in range(7):
            P.dma("pool", w[:, :, n * 512:(n + 1) * 512],
                  self.w_in[l, :, n * 512:(n + 1) * 512].rearrange("(c p) f -> p c f", p=128), b_w, writes=[b_w])
        sc1 = P.sbuf("sc1", [128, D], F32); sh = P.sbuf("sh", [128, D], F32); b_mod = Buf()
        xt = [P.sbuf(f"xt{i}", [128, D], F32) for i in range(2)]; b_xt = [Buf(), Buf()]
        hf = [P.sbuf(f"hf{i}", [128, D], F32) for i in range(2)]; b_hf = [Buf(), Buf()]
        hT = [P.sbuf(f"hT{i}", [128, 8, TB], BF16) for i in range(2)]
        b_hT = [[Buf() for _ in range(4)] for _ in range(2)]
        NST = 4
        stf = [P.sbuf(f"stf{i}", [128, 512], F32) for i in range(NST)]; b_stf = [Buf() for _ in range(NST)]
        stb = [P.sbuf(f"stb{i}", [128, 512], BF16) for i in range(NST)]; b_stb = [Buf() for _ in range(NST)]
        psT = [P.psum(f"psT{i}", [128, 512]) for i in range(2)]; b_psT = [Buf(psum=True), Buf(psum=True)]
        NPS = 6
        mm = [P.psum(f"mm{i}", [128, 512]) for i in range(NPS)]; b_mm = [Buf(psum=True) for _ in range(NPS)]
        xk = 0; kf = 0; kb_ = 0; km = 0
        for blk in range(nblk):
            t0 = blk * TB
            bseq = blk // blk_per_seq
            hs = blk % 2
            if blk % blk_per_seq == 0:
                ls = l * 2
                P.dma("sp", sh[:], bcast_rows(self.mod[bseq, ls, 0:D], 128), b_mod, writes=[b_mod])
                P.dma("sp", sc1[:], bcast_rows(self.mod[bseq, ls, D:2 * D], 128), b_mod, writes=[b_mod])
            for i in range(4):
                xi = xk % 2; xk += 1
                r0 = t0 + i * 128
                P.dma("sp", xt[xi][:], x_src[r0:r0 + 128, :], b_xt[xi], writes=[b_xt[xi]])
                P.op("dve", lambda e, xi=xi: e.tensor_tensor(out=hf[xi][:], in0=xt[xi][:], in1=sc1[:], op=ALU.mult),
                     reads=[b_xt[xi], b_mod], writes=[b_hf[xi]])
                P.op("pool", lambda e, xi=xi: e.tensor_tensor(out=hf[xi][:], in0=hf[xi][:], in1=sh[:], op=ALU.add),
                     reads=[b_hf[xi], b_mod], writes=[b_hf[xi]])
                for hb in range(2):
                    for q in range(4):
                        dc = hb * 4 + q
                        P.op("pe", lambda e, xi=xi, hb=hb, q=q, dc=dc: e.transpose(
                            psT[hb][:, q * 128:(q + 1) * 128], hf[xi][:, dc * 128:(dc + 1) * 128], self.identf[:]),
                            reads=[b_hf[xi]], writes=[b_psT[hb]], signal=(q == 3))
                    P.op("act", lambda e, hb=hb, i=i, hs=hs: e.activation(
                        out=hT[hs][:, hb * 4:(hb + 1) * 4, i * 128:(i + 1) * 128],
                        in_=psT[hb][:].rearrange("p (c t) -> p c t", t=128), func=AF.Copy),
                        reads=[b_psT[hb]], writes=[b_hT[hs][i]])
            for cc in list(range(0, 8)) + list(range(16, 24)):
                pb = km % NPS; km += 1
                for dc in range(8):
                    P.op("pe", lambda e, cc=cc, dc=dc, pb=pb, hs=hs: e.matmul(
                        mm[pb][:], w[:, dc, cc * 128:(cc + 1) * 128], hT[hs][:, dc, :], start=(dc == 0), stop=(dc == 7)),
                        reads=[b_w] + b_hT[hs], writes=[b_mm[pb]], signal=(dc == 7))
                if cc < 8:
                    si = kf % NST; kf += 1
                    if cc < 4:
                        P.op("act", lambda e, pb=pb, si=si: e.activation(out=stf[si][:], in_=mm[pb][:], func=AF.Silu),
                             reads=[b_mm[pb]], writes=[b_stf[si]])
                        dst = self.hqT[cc * 128:(cc + 1) * 128, t0:t0 + TB]
                    else:
                        P.op("act", lambda e, pb=pb, si=si: e.activation(out=stf[si][:], in_=mm[pb][:], func=AF.Tanh, scale=0.5),
                             reads=[b_mm[pb]], writes=[b_stf[si]])
                        dst = self.hfT[(cc - 4) * 128:(cc - 3) * 128, t0:t0 + TB]
                    P.dma("sp", dst, stf[si][:], b_stf[si], reads=[b_stf[si]])
                else:
                    si = kb_ % NST; kb_ += 1
                    sc = 0.125 if cc < 20 else 1.0
                    P.op("dve", lambda e, pb=pb, si=si, sc=sc: e.tensor_scalar(
                        out=stb[si][:], in0=mm[pb][:], scalar1=sc, scalar2=None, op0=ALU.mult),
                        reads=[b_mm[pb]], writes=[b_stb[si]])
                    if cc < 20:
                        dst = self.sqT[(cc - 16) * 128:(cc - 15) * 128, t0:t0 + TB]
                    else:
                        dst = self.skT[(cc - 20) * 128:(cc - 19) * 128, t0:t0 + TB]
                    P.dma("sp", dst, stb[si][:], b_stb[si], reads=[b_stb[si]])
            for i in range(4):
                r0 = t0 + i * 128
                for seg, c0 in (("hi", 1024), ("hg", 1536), ("sv", 3072)):
                    pb = km % NPS; km += 1
                    for dc in range(8):
                        P.op("pe", lambda e, dc=dc, pb=pb, hs=hs, i=i, c0=c0: e.matmul(
                            mm[pb][:], hT[hs][:, dc, i * 128:(i + 1) * 128], w[:, dc, c0:c0 + 512],
                            start=(dc == 0), stop=(dc == 7)),
                            reads=[b_w, b_hT[hs][i]], writes=[b_mm[pb]], signal=(dc == 7))
                    if seg == "hg":
                        si = kf % NST; kf += 1
                        P.op("act", lambda e, pb=pb, si=si: e.activation(out=stf[si][:], in_=mm[pb][:], func=AF.Silu),
                             reads=[b_mm[pb]], writes=[b_stf[si]])
                        P.dma("sp", self.hg_tm[r0:r0 + 128, :], stf[si][:], b_stf[si], reads=[b_stf[si]])
                    else:
                        si = kb_ % NST; kb_ += 1
                        P.op("dve", lambda e, pb=pb, si=si: e.tensor_copy(out=stb[si][:], in_=mm[pb][:]),
                             reads=[b_mm[pb]], writes=[b_stb[si]])
                        dst = self.hi_tm if seg == "hi" else self.sv_tm
                        P.dma("sp", dst[r0:r0 + 128, :], stb[si][:], b_stb[si], reads=[b_stb[si]])
        P.run()

    def phase_outproj(self, l, x_src, x_dst):
        nc, cfg = self.nc, self.cfg
        P = Prog(nc, f"o{l}")
        TB = 512
        nblk = cfg.ntok // TB
        blk_per_seq = cfg.S // TB
        w = P.sbuf("w", [128, 8, D], BF16); b_w = Buf()
        P.dma("pool", w[:], self.w_out[l].rearrange("(c p) f -> p c f", p=128), b_w, writes=[b_w])
        gt = P.sbuf("gt", [128, D], F32); b_mod = Buf()
        lng = P.sbuf("lng", [128, D], F32); lnb = P.sbuf("lnb", [128, D], F32); b_ln = Buf()
        P.dma("sp", lng[:], bcast_rows(self.ln_gain[l, 0, :], 128), b_ln, writes=[b_ln])
        P.dma("sp", lnb[:], bcast_rows(self.ln_bias[l, 0, :], 128), b_ln, writes=[b_ln])
        oTs = [P.sbuf(f"oTs{i}", [128, 8, TB], BF16) for i in range(2)]; b_oTs = [Buf(), Buf()]
        xt = [P.sbuf(f"xt{i}", [128, D], F32) for i in range(2)]; b_xt = [Buf(), Buf()]
        zt = P.sbuf("zt", [128, 4, D], F32); b_z = [Buf() for _ in range(4)]
        ot = [P.sbuf(f"ot{i}", [128, D], F32) for i in range(2)]; b_ot = [Buf(), Buf()]
        stats = P.sbuf("stats", [128, 4, 2, 6], F32); b_stats = [Buf() for _ in range(4)]
        mv = P.sbuf("mv", [128, 4, 2], F32); b_mv = Buf()
        rstd = P.sbuf("rstd", [128, 4], F32); b_rstd = Buf()
        yps = [P.psum(f"yps{i}", [128, 512]) for i in range(4)]; b_yps = [Buf(psum=True) for _ in range(4)]
        xk = 0; kp = 0
        for blk in range(nblk):
            t0 = blk * TB
            bseq = blk // blk_per_seq
            os_ = blk % 2
            if blk % blk_per_seq == 0:
                P.dma("sp", gt[:], bcast_rows(self.mod[bseq, l * 2, 2 * D:3 * D], 128), b_mod, writes=[b_mod])
            P.dma("sp", oTs[os_][:], self.oT[:, t0:t0 + TB].rearrange("(c p) t -> p c t", p=128), b_oTs[os_],
                  writes=[b_oTs[os_]])
            for i in range(4):
                xi = xk % 2; xk += 1
                r0 = t0 + i * 128
                P.dma("sp", xt[xi][:], x_src[r0:r0 + 128, :], b_xt[xi], writes=[b_xt[xi]])
                for h2 in range(2):
                    pb = kp % 4; kp += 1
                    for cc in range(8):
                        P.op("pe", lambda e, cc=cc, pb=pb, os_=os_, i=i, h2=h2: e.matmul(
                            yps[pb][:], oTs[os_][:, cc, i * 128:(i + 1) * 128], w[:, cc, h2 * 512:(h2 + 1) * 512],
                            start=(cc == 0), stop=(cc == 7)), reads=[b_w, b_oTs[os_]], writes=[b_yps[pb]], signal=(cc == 7))
                    P.op("dve", lambda e, pb=pb, i=i, h2=h2: e.tensor_tensor(
                        out=zt[:, i, h2 * 512:(h2 + 1) * 512], in0=yps[pb][:], in1=gt[:, h2 * 512:(h2 + 1) * 512], op=ALU.mult),
                        reads=[b_yps[pb], b_mod], writes=[b_z[i]])
                P.op("dve", lambda e, i=i, xi=xi: e.scalar_tensor_tensor(
                    out=zt[:, i, :], in0=xt[xi][:], scalar=ALPHA, in1=zt[:, i, :], op0=ALU.mult, op1=ALU.add),
                    reads=[b_xt[xi]], writes=[b_z[i]])
                for h2 in range(2):
                    P.op("dve", lambda e, i=i, h2=h2: e.bn_stats(out=stats[:, i, h2, :], in_=zt[:, i, h2 * 512:(h2 + 1) * 512]),
                         reads=[b_z[i]], writes=[b_stats[i]])
                P.op("dve", lambda e, i=i: e.bn_aggr(out=mv[:, i, :], in_=stats[:, i, :, :].rearrange("p a b -> p (a b)")),
                     reads=[b_stats[i]], writes=[b_mv])
            self.rstd_from_var(P, mv, b_mv, rstd, b_rstd, 4)
            for i in range(4):
                oi = i % 2
                r0 = t0 + i * 128
                P.op("dve", lambda e, i=i, oi=oi: e.tensor_scalar(
                    out=ot[oi][:], in0=zt[:, i, :], scalar1=mv[:, i, 0:1], scalar2=rstd[:, i:i + 1],
                    op0=ALU.subtract, op1=ALU.mult), reads=[b_z[i], b_mv, b_rstd], writes=[b_ot[oi]])
                P.op("pool", lambda e, oi=oi: e.tensor_tensor(out=ot[oi][:], in0=ot[oi][:], in1=lng[:], op=ALU.mult),
                     reads=[b_ln], writes=[b_ot[oi]])
                P.op("pool", lambda e, oi=oi: e.tensor_tensor(out=ot[oi][:], in0=ot[oi][:], in1=lnb[:], op=ALU.add),
                     reads=[b_ln], writes=[b_ot[oi]])
                P.dma("sp", x_dst[r0:r0 + 128, :], ot[oi][:], b_ot[oi], reads=[b_ot[oi]])
        P.run()

    def phase_sb(self, l):
        nc, cfg = self.nc, self.cfg
        P = Prog(nc, f"s{l}")
        S = cfg.S
        NB = S // 128
        NG = S // 512
        ka = [P.sbuf(f"ka{i}", [65, S], BF16) for i in range(2)]; b_ka = [Buf(), Buf()]
        qz = [P.sbuf(f"qz{i}", [64, S], BF16) for i in range(2)]; b_qz = [Buf(), Buf()]
        NQA = 3
        qa = [[P.sbuf(f"qa{i}{j}", [65, S], BF16) for j in range(NQA)] for i in range(2)]
        b_qa = [[Buf() for _ in range(NQA)] for _ in range(2)]
        vall = P.sbuf("vall", [128, NB, DSB], BF16); b_v = Buf()
        e1 = [P.sbuf(f"e1{i}", [128, 512], F32) for i in range(2)]; b_e1 = [Buf(), Buf()]
        NSP = 4
        sp = [P.sbuf(f"sp{i}", [128, 512], BF16) for i in range(NSP)]; b_sp = [Buf() for _ in range(NSP)]
        At = [P.sbuf(f"A{i}", [128, 512], BF16) for i in range(NSP)]; b_A = [Buf() for _ in range(NSP)]
        ost = [P.sbuf(f"ost{i}", [64, 512], BF16) for i in range(2)]; b_ost = [Buf(), Buf()]
        zps = [P.psum(f"z{i}", [128, 512]) for i in range(2)]; b_z = [Buf(psum=True), Buf(psum=True)]
        gps = [P.psum(f"g{i}", [128, 512]) for i in range(2)]; b_g = [Buf(psum=True), Buf(psum=True)]
        cps = [P.psum(f"c{i}", [128, 512]) for i in range(2)]; b_c = [Buf(psum=True), Buf(psum=True)]
        ops_ = [P.psum(f"o{i}", [128, 512]) for i in range(2)]; b_o = [Buf(psum=True), Buf(psum=True)]
        for i in range(2):
            P.op("pool", lambda e, i=i: e.memset(ka[i][64:65, :], 1.0), writes=[b_ka[i]])

        steps = []
        hk = 0
        for b in range(cfg.nseq):
            for h in range(NH_S):
                hp = hk % 2
                for g in range(NG):
                    kmax = 4 * g + 3
                    for kb in range(kmax, -1, -1):
                        i = kb - 4 * g
                        n0 = max(i, 0) * 128
                        steps.append(dict(b=b, h=h, hp=hp, g=g, kb=kb, n0=n0, diag=(i >= 0),
                                          first=(kb == kmax), last=(kb == 0), newhead=(g == 0 and kb == kmax),
                                          gk=None))
                hk += 1
        gk = -1
        for st in steps:
            if st["first"]:
                gk += 1
            st["gp"] = gk % 2
        ost_k = [0]

        def load_head(st):
            b, h, hp = st["b"], st["h"], st["hp"]
            c0 = b * S
            if h == 0:
                for q0 in range(0, NB, 8):
                    P.dma("sp", vall[:, q0:q0 + 8, :], self.sv_tm[c0 + q0 * 128:c0 + (q0 + 8) * 128, :].rearrange("(n p) d -> p n d", p=128),
                          b_v, writes=[b_v])
            P.dma("sp", ka[hp][0:64, :], self.skT[h * 64:(h + 1) * 64, c0:c0 + S], b_ka[hp], writes=[b_ka[hp]])
            P.dma("sp", qz[hp][:], self.sqT[h * 64:(h + 1) * 64, c0:c0 + S], b_qz[hp], writes=[b_qz[hp]])
            for j in range(NQA):
                P.dma("sp", qa[hp][j][0:64, :], self.sqT[h * 64:(h + 1) * 64, c0:c0 + S], b_qa[hp][j], writes=[b_qa[hp][j]])

        def emit_p1(j):
            st = steps[j]
            if st["newhead"]:
                load_head(st)
            hp, kb, n0 = st["hp"], st["kb"], st["n0"]
            c0 = st["g"] * 512
            zb = j % 2
            P.op("pe", lambda e: e.matmul(zps[zb][:, n0:512], ka[hp][0:64, kb * 128:(kb + 1) * 128],
                                           qz[hp][0:64, c0 + n0:c0 + 512], start=True, stop=True),
                 reads=[b_ka[hp], b_qz[hp]], writes=[b_z[zb]])

        def emit_a12(j):
            st = steps[j]
            n0 = st["n0"]
            zb = j % 2; eb = j % 2; sb = j % NSP
            P.op("act", lambda e: e.activation(out=e1[eb][:, n0:512], in_=zps[zb][:, n0:512], func=AF.Exp),
                 reads=[b_z[zb]], writes=[b_e1[eb]])
            P.op("act", lambda e: e.activation(out=sp[sb][:, n0:512], in_=e1[eb][:, n0:512], func=AF.Ln, bias=1.0, scale=1.0),
                 reads=[b_e1[eb]], writes=[b_sp[sb]])
            if st["diag"]:
                P.op("dve", lambda e: e.tensor_tensor(out=sp[sb][:, n0:n0 + 128], in0=sp[sb][:, n0:n0 + 128],
                                                       in1=self.mask_st[:], op=ALU.mult), writes=[b_sp[sb]])

        def emit_carry(j):
            st = steps[j]
            n0 = st["n0"]
            sb = j % NSP
            hp, gp = st["hp"], st["gp"]
            c0 = st["g"] * 512
            par = j % NQA
            if st["first"]:
                P.op("pe", lambda e: e.matmul(cps[gp][0:65, :], self.zeros_b[:, 0:65], self.zeros_b[:, 0:512], start=True, stop=True),
                     writes=[b_c[gp]])
            P.op("dve", lambda e: e.tensor_copy(out=qa[hp][par][64:65, c0 + n0:c0 + 512], in_=cps[gp][64:65, n0:512]),
                 reads=[b_c[gp]], writes=[b_qa[hp][par]])
            P.op("pe", lambda e: e.matmul(cps[gp][0:65, n0:512], self.neg_col[:, 0:65], sp[sb][:, n0:512], start=False, stop=True,
                                           skip_group_check=True),
                 reads=[b_sp[sb]], writes=[b_c[gp]])

        def emit_main(j):
            st = steps[j]
            hp, kb, n0, gp, h = st["hp"], st["kb"], st["n0"], st["gp"], st["h"]
            c0 = st["g"] * 512
            par = j % NQA; sb = j % NSP; gb = j % 2; ab = j % NSP
            if st["first"]:
                P.op("pe", lambda e: e.matmul(ops_[gp][0:64, :], self.zeros_b[:, 0:64], self.zeros_b[:, 0:512], start=True, stop=True),
                     writes=[b_o[gp]])
            P.op("pe", lambda e: e.matmul(gps[gb][:, n0:512], ka[hp][0:65, kb * 128:(kb + 1) * 128],
                                           qa[hp][par][0:65, c0 + n0:c0 + 512], start=True, stop=False),
                 reads=[b_ka[hp], b_qa[hp][par]], writes=[b_g[gb]], signal=False)
            P.op("pe", lambda e: e.matmul(gps[gb][:, n0:512], self.tri_neg[:], sp[sb][:, n0:512], start=False, stop=True),
                 reads=[b_sp[sb]], writes=[b_g[gb]])

        def emit_a3(j):
            st = steps[j]
            n0 = st["n0"]
            gb = j % 2; ab = j % NSP
            P.op("act", lambda e: e.activation(out=At[ab][:, n0:512], in_=gps[gb][:, n0:512], func=AF.Exp),
                 reads=[b_g[gb]], writes=[b_A[ab]])
            if st["diag"]:
                P.op("dve", lambda e: e.tensor_tensor(out=At[ab][:, n0:n0 + 128], in0=At[ab][:, n0:n0 + 128],
                                                       in1=self.mask_st[:], op=ALU.mult), writes=[b_A[ab]])

        def emit_p4(j):
            st = steps[j]
            kb, n0, gp, h, b = st["kb"], st["n0"], st["gp"], st["h"], st["b"]
            ab = j % NSP
            P.op("pe", lambda e: e.matmul(ops_[gp][0:64, n0:512], vall[:, kb, h * 64:(h + 1) * 64], At[ab][:, n0:512],
                                           start=False, stop=True, skip_group_check=True),
                 reads=[b_v, b_A[ab]], writes=[b_o[gp]])
            if st["last"]:
                k = ost_k[0] % 2; ost_k[0] += 1
                c0 = b * S + st["g"] * 512
                P.op("dve", lambda e: e.tensor_copy(out=ost[k][:], in_=ops_[gp][0:64, :]), reads=[b_o[gp]], writes=[b_ost[k]])
                P.dma("sp", self.oT[DH + h * 64:DH + (h + 1) * 64, c0:c0 + 512], ost[k][:], b_ost[k], reads=[b_ost[k]])

        n = len(steps)
        ok = lambda m: 0 <= m < n
        for j in range(-3, n + 1):
            if ok(j + 3):
                emit_p1(j + 3)
            if ok(j + 1):
                emit_carry(j + 1)
            if ok(j):
                emit_main(j)
            if ok(j - 1):
                emit_p4(j - 1)
            if ok(j + 2):
                emit_a12(j + 2)
            if ok(j):
                emit_a3(j)
        P.run()

    def phase_hgrn(self, l):
        nc, cfg = self.nc, self.cfg
        P = Prog(nc, f"h{l}")
        S = cfg.S
        ns = cfg.nseq
        ntile = S // 128
        R = 31
        NS_ = 4 * ns
        gbc = P.sbuf("gbc", [128, DH], F32); b_gbc = Buf()
        P.dma("sp", gbc[:], bcast_rows(self.hgain[l, :], 128), b_gbc, writes=[b_gbc])
        NT_ = 2 * ns
        def many(name, n, shape, dt):
            return [P.sbuf(f"{name}{i}", shape, dt) for i in range(n)], [Buf() for _ in range(n)]
        tf, b_tf = many("tf", NT_, [128, 4, 128], F32)
        qs, b_qs = many("qs", NT_, [128, 4, 128], F32)
        vt, b_vt = many("vt", NT_, [128, DH], BF16)
        gg, b_gg = many("gg", NT_, [128, DH], F32)
        ss, b_ss = many("ss", NT_, [128, 4], F32)
        on, b_on = many("on", NT_, [128, DH], F32)
        onb, b_onb = many("onb", NT_, [128, DH], BF16)
        oTst, b_oTst = many("oTst", NT_, [128, 4, 128], BF16)
        ff, b_ff = many("ff", NS_, [128, 128], F32)
        lf, b_lf = many("lf", NS_, [128, 128], F32)
        kk, b_kk = many("kk", NS_, [128, 128], F32)
        bb, b_bb = many("bb", NS_, [128, 128], F32)
        eq, b_eq = many("eq", NS_, [128, 128], F32)
        ek, b_ek = many("ek", NS_, [128, 128], F32)
        sc_, b_sc = many("sc", NS_, [128, 8], F32)
        qzp, b_qzp = many("qzp", NS_, [128, 384], BF16)
        kt, b_kt = many("kt", NS_, [128, 128], BF16)
        ktok, b_ktok = many("ktok", NS_, [128, 128], BF16)
        pT, b_pT = many("pT", NS_, [128, 128], BF16)
        sq_, b_sq = many("sqj", NS_, [128, 128], F32)
        tmp, b_tmp = many("tmp", NS_, [128, 128], F32)
        St, b_St = many("St", NS_, [128, 128], F32)
        Sb, b_Sb = many("Sb", NS_, [128, 128], BF16)
        osb, b_osb = many("osb", NS_, [128, 128], F32)
        ones = self.ones_f
        _pkt = P.psum("pkt", [128, 1024], BF16); b_pkt = Buf(psum=True)
        p_kt = _pkt[:, 0:128]
        _psc = [P.psum(f"psc{i}", [128, 512]) for i in range(2)]; b_psc = [Buf(psum=True), Buf(psum=True)]
        p_sc = [t[:, 0:128] for t in _psc]
        _po = [P.psum(f"po{i}", [128, 512]) for i in range(2)]; b_po = [Buf(psum=True), Buf(psum=True)]
        p_o = [t[:, 0:128] for t in _po]
        _pkv = [P.psum(f"pkv{i}", [128, 512]) for i in range(2)]; b_pkv = [Buf(psum=True), Buf(psum=True)]
        p_kv = [t[:, 0:128] for t in _pkv]
        _pot = P.psum("pot", [128, 1024], BF16); b_pot = Buf(psum=True)
        p_ot = _pot[:, 0:512].rearrange("p (h t) -> p h t", t=128)
        for i in range(NS_):
            P.op("pool", lambda e, i=i: e.memset(qzp[i][:], 0.0), writes=[b_qzp[i]])
            P.op("pool", lambda e, i=i: e.memset(St[i][:], 0.0), writes=[b_St[i]])

        def head_stream(b, h, ti, tk):
            sk = b * 4 + h
            pp = sk % 2
            A_ = self.lbA[:, l * 4 + h:l * 4 + h + 1]
            B_ = self.lbB[:, l * 4 + h:l * 4 + h + 1]
            P.op("dve", lambda e: e.tensor_scalar(out=ff[sk][:], in0=tf[tk][:, h, :], scalar1=A_, scalar2=B_,
                                                   op0=ALU.mult, op1=ALU.add), reads=[b_tf[tk]], writes=[b_ff[sk]])
            yield
            P.op("act", lambda e: e.activation(out=lf[sk][:], in_=ff[sk][:], func=AF.Ln), reads=[b_ff[sk]], writes=[b_lf[sk]])
            P.op("pool", lambda e: e.tensor_scalar(out=kk[sk][:], in0=ff[sk][:], scalar1=-1.0, scalar2=1.0,
                                                    op0=ALU.mult, op1=ALU.add), reads=[b_ff[sk]], writes=[b_kk[sk]])
            yield
            for c in range(2):
                P.op("dve", lambda e, c=c: e.tensor_tensor_scan(
                    out=bb[sk][:, c * 64:(c + 1) * 64], data0=ones[:, 0:64], data1=lf[sk][:, c * 64:(c + 1) * 64],
                    initial=0.0, op0=ALU.mult, op1=ALU.add), reads=[b_lf[sk]], writes=[b_bb[sk]])
            yield
            for c in range(2):
                P.op("dve", lambda e, c=c: e.tensor_scalar(
                    out=sc_[sk][:, 3 * c:3 * c + 1], in0=bb[sk][:, c * 64 + R:c * 64 + R + 1], scalar1=-1.0, scalar2=None,
                    op0=ALU.mult), reads=[b_bb[sk]], writes=[b_sc[sk]])
            yield
            for c in range(2):
                br = bb[sk][:, c * 64 + R:c * 64 + R + 1]
                nbr = sc_[sk][:, 3 * c:3 * c + 1]
                P.op("act", lambda e, c=c, nbr=nbr: e.activation(
                    out=eq[sk][:, c * 64:(c + 1) * 64], in_=bb[sk][:, c * 64:(c + 1) * 64], func=AF.Exp, bias=nbr, scale=1.0),
                    reads=[b_bb[sk], b_sc[sk]], writes=[b_eq[sk]])
                P.op("act", lambda e, c=c, br=br: e.activation(
                    out=ek[sk][:, c * 64:(c + 1) * 64], in_=bb[sk][:, c * 64:(c + 1) * 64], func=AF.Exp, bias=br, scale=-1.0),
                    reads=[b_bb[sk]], writes=[b_ek[sk]])
                P.op("act", lambda e, c=c, br=br: e.activation(
                    out=sc_[sk][:, 3 * c + 1:3 * c + 2], in_=br, func=AF.Exp), reads=[b_bb[sk]], writes=[b_sc[sk]])
                P.op("act", lambda e, c=c: e.activation(
                    out=sc_[sk][:, 3 * c + 2:3 * c + 3], in_=bb[sk][:, c * 64 + 63:c * 64 + 64], func=AF.Exp),
                    reads=[b_bb[sk]], writes=[b_sc[sk]])
            yield
            P.op("dve", lambda e: e.tensor_tensor(
                out=qzp[sk][:].rearrange("p (c x) -> p c x", x=192)[:, :, 0:64],
                in0=qs[tk][:, h, :].rearrange("p (c x) -> p c x", x=64),
                in1=eq[sk][:].rearrange("p (c x) -> p c x", x=64), op=ALU.mult),
                reads=[b_qs[tk], b_eq[sk]], writes=[b_qzp[sk]])
            P.op("pool", lambda e: e.tensor_tensor(out=kt[sk][:], in0=kk[sk][:], in1=ek[sk][:], op=ALU.mult),
                 reads=[b_kk[sk], b_ek[sk]], writes=[b_kt[sk]])
            yield
            P.op("pe", lambda e: e.transpose(p_kt, kt[sk][:], self.identb[:]), reads=[b_kt[sk]], writes=[b_pkt])
            P.op("act", lambda e: e.activation(out=ktok[sk][:], in_=p_kt, func=AF.Copy), reads=[b_pkt], writes=[b_ktok[sk]])
            P.op("pe", lambda e: e.matmul(
                p_sc[pp].rearrange("p (c x) -> p c x", x=64), kt[sk][:],
                qzp[sk][:].rearrange("p (c x) -> p c x", x=192)[:, :, 0:64], start=True, stop=True),
                reads=[b_kt[sk], b_qzp[sk]], writes=[b_psc[pp]])
            P.op("dve", lambda e: e.tensor_scalar(out=sq_[sk][:], in0=p_sc[pp], scalar1=1e30, scalar2=-1e30,
                                                   op0=ALU.min, op1=ALU.max), reads=[b_psc[pp]], writes=[b_sq[sk]])
            yield
            P.op("dve", lambda e: e.tensor_tensor(out=pT[sk][:], in0=sq_[sk][:], in1=self.mask_bd[:], op=ALU.mult),
                 reads=[b_sq[sk]], writes=[b_pT[sk]])
            P.op("dve", lambda e: e.tensor_scalar(out=Sb[sk][:], in0=St[sk][:], scalar1=sc_[sk][:, 1:2], scalar2=None,
                                                   op0=ALU.mult), reads=[b_St[sk], b_sc[sk]], writes=[b_Sb[sk]])
            yield
            for c in range(2):
                if c == 0:
                    P.op("pe", lambda e: e.matmul(p_o[pp], pT[sk][:], vt[tk][:, h * 128:(h + 1) * 128], start=True, stop=False),
                         reads=[b_pT[sk], b_vt[tk]], writes=[b_po[pp]], signal=False)
                P.op("pe", lambda e, c=c: e.matmul(
                    p_o[pp], qzp[sk][:, c * 128:(c + 1) * 128], Sb[sk][:], start=(c == 1), stop=True),
                    reads=[b_qzp[sk], b_Sb[sk]], writes=[b_po[pp]])
                P.op("pe", lambda e, c=c: e.matmul(
                    p_kv[pp], ktok[sk][c * 64:(c + 1) * 64, :], vt[tk][c * 64:(c + 1) * 64, h * 128:(h + 1) * 128],
                    start=True, stop=True), reads=[b_ktok[sk], b_vt[tk]], writes=[b_pkv[pp]])
                if c == 0:
                    P.op("act", lambda e: e.activation(out=osb[sk][:], in_=p_o[pp], func=AF.Copy),
                         reads=[b_po[pp]], writes=[b_osb[sk]])
                else:
                    P.op("dve", lambda e: e.tensor_tensor(out=osb[sk][:], in0=p_o[pp], in1=osb[sk][:], op=ALU.add),
                         reads=[b_po[pp]], writes=[b_osb[sk]])
                P.op("dve", lambda e, c=c: e.tensor_scalar(
                    out=tmp[sk][:], in0=p_kv[pp], scalar1=eq[sk][:, c * 64 + 63:c * 64 + 64], scalar2=None, op0=ALU.mult),
                    reads=[b_pkv[pp], b_eq[sk]], writes=[b_tmp[sk]])
                P.op("dve", lambda e, c=c: e.scalar_tensor_tensor(
                    out=St[sk][:], in0=St[sk][:], scalar=sc_[sk][:, 3 * c + 2:3 * c + 3], in1=tmp[sk][:],
                    op0=ALU.mult, op1=ALU.add), reads=[b_tmp[sk], b_sc[sk]], writes=[b_St[sk]])
                if c == 0:
                    P.op("dve", lambda e: e.tensor_scalar(out=Sb[sk][:], in0=St[sk][:], scalar1=sc_[sk][:, 4:5], scalar2=None,
                                                           op0=ALU.mult), reads=[b_St[sk], b_sc[sk]], writes=[b_Sb[sk]])
                yield
            P.op("act", lambda e: e.activation(out=sq_[sk][:], in_=osb[sk][:], func=AF.Square, accum_out=ss[tk][:, h:h + 1]),
                 reads=[b_osb[sk]], writes=[b_sq[sk], b_ss[tk]])
            P.op("pool", lambda e: e.tensor_tensor(out=on[tk][:, h * 128:(h + 1) * 128], in0=osb[sk][:],
                                                    in1=gg[tk][:, h * 128:(h + 1) * 128], op=ALU.mult),
                 reads=[b_osb[sk], b_gg[tk]], writes=[b_on[tk]])
            yield

        def tile_tail(b, ti, tk):
            r0 = b * S + ti * 128
            P.op("dve", lambda e: e.tensor_scalar(out=ss[tk][:], in0=ss[tk][:], scalar1=1.0 / 128.0, scalar2=RMS_EPS,
                                                   op0=ALU.mult, op1=ALU.add), reads=[b_ss[tk]], writes=[b_ss[tk]])
            P.op("act", lambda e: e.activation(out=ss[tk][:], in_=ss[tk][:], func=AF.Ln), reads=[b_ss[tk]], writes=[b_ss[tk]])
            P.op("act", lambda e: e.activation(out=ss[tk][:], in_=ss[tk][:], func=AF.Exp, scale=-0.5),
                 reads=[b_ss[tk]], writes=[b_ss[tk]])
            for h in range(4):
                P.op("pool", lambda e, h=h: e.tensor_scalar(
                    out=onb[tk][:, h * 128:(h + 1) * 128], in0=on[tk][:, h * 128:(h + 1) * 128], scalar1=ss[tk][:, h:h + 1],
                    scalar2=None, op0=ALU.mult), reads=[b_ss[tk], b_on[tk]], writes=[b_onb[tk]])
            for h in range(4):
                P.op("pe", lambda e, h=h: e.transpose(p_ot[:, h, :], onb[tk][:, h * 128:(h + 1) * 128], self.identb[:]),
                     reads=[b_onb[tk]], writes=[b_pot], signal=(h == 3))
            P.op("act", lambda e: e.activation(out=oTst[tk][:], in_=p_ot, func=AF.Copy), reads=[b_pot], writes=[b_oTst[tk]])
            P.dma("sp", self.oT[0:DH, r0:r0 + 128].rearrange("(h p) t -> p h t", p=128), oTst[tk][:], b_oTst[tk],
                  reads=[b_oTst[tk]])

        def tile_loads(b, ti, tk):
            r0 = b * S + ti * 128
            P.dma("sp", tf[tk][:], self.hfT[:, r0:r0 + 128].rearrange("(h p) t -> p h t", p=128), b_tf[tk], writes=[b_tf[tk]])
            P.dma("sp", qs[tk][:], self.hqT[:, r0:r0 + 128].rearrange("(h p) t -> p h t", p=128), b_qs[tk], writes=[b_qs[tk]])
            P.dma("sp", vt[tk][:], self.hi_tm[r0:r0 + 128, :], b_vt[tk], writes=[b_vt[tk]])
            P.dma("sp", gg[tk][:], self.hg_tm[r0:r0 + 128, :], b_gg[tk], writes=[b_gg[tk]])
            P.op("pool", lambda e: e.tensor_tensor(out=gg[tk][:], in0=gg[tk][:], in1=gbc[:], op=ALU.mult),
                 reads=[b_gbc], writes=[b_gg[tk]])

        for b in range(ns):
            tile_loads(b, 0, b)
        for ti in range(ntile):
            tks = [(ti % 2) * ns + b for b in range(ns)]
            if ti + 1 < ntile:
                for b in range(ns):
                    tile_loads(b, ti + 1, ((ti + 1) % 2) * ns + b)
            gens = [head_stream(b, h, ti, tks[b]) for h in range(4) for b in range(ns)]
            alive = list(gens)
            while alive:
                nxt = []
                for g in alive:
                    try:
                        next(g)
                        nxt.append(g)
                    except StopIteration:
                        pass
                alive = nxt
            for b in range(ns):
                tile_tail(b, ti, tks[b])
        P.run()

    def rstd_from_var(self, P, mv, b_mv, rstd, b_rstd, n):
        P.op("dve", lambda e: e.tensor_scalar(out=rstd[:, 0:n], in0=mv[:, 0:n, 1], scalar1=LN_EPS, scalar2=None,
                                               op0=ALU.add), reads=[b_mv], writes=[b_rstd])
        P.op("act", lambda e: e.activation(out=rstd[:, 0:n], in_=rstd[:, 0:n], func=AF.Ln), reads=[b_rstd], writes=[b_rstd])
        P.op("act", lambda e: e.activation(out=rstd[:, 0:n], in_=rstd[:, 0:n], func=AF.Exp, scale=-0.5),
             reads=[b_rstd], writes=[b_rstd])

    def router_top2(self, P, lps, b_lps, rt, b_rt, comb, b_c, i):
        L, m1, k1, L2, m2, k2, dd, p1 = (rt[:, q, :] for q in range(8))
        P.op("dve", lambda e: e.tensor_copy(out=L, in_=lps[:, 0:NE]), reads=[b_lps], writes=[b_rt])
        P.op("dve", lambda e: e.tensor_reduce(out=m1[:, 0:1], in_=L, axis=AX.X, op=ALU.max), reads=[b_rt], writes=[b_rt])
        P.op("dve", lambda e: e.tensor_scalar(out=k1, in0=L, scalar1=m1[:, 0:1], scalar2=None, op0=ALU.is_equal),
             reads=[b_rt], writes=[b_rt])
        P.op("dve", lambda e: e.scalar_tensor_tensor(out=L2, in0=k1, scalar=-1e30, in1=L, op0=ALU.mult, op1=ALU.add),
             reads=[b_rt], writes=[b_rt])
        P.op("dve", lambda e: e.tensor_reduce(out=m2[:, 0:1], in_=L2, axis=AX.X, op=ALU.max), reads=[b_rt], writes=[b_rt])
        P.op("dve", lambda e: e.tensor_scalar(out=k2, in0=L2, scalar1=m2[:, 0:1], scalar2=None, op0=ALU.is_equal),
             reads=[b_rt], writes=[b_rt])
        P.op("dve", lambda e: e.tensor_tensor(out=dd[:, 0:1], in0=m2[:, 0:1], in1=m1[:, 0:1], op=ALU.subtract),
             reads=[b_rt], writes=[b_rt])
        P.op("act", lambda e: e.activation(out=dd[:, 1:2], in_=dd[:, 0:1], func=AF.Exp), reads=[b_rt], writes=[b_rt])
        P.op("dve", lambda e: e.tensor_scalar(out=dd[:, 2:3], in0=dd[:, 1:2], scalar1=1.0, scalar2=None, op0=ALU.add),
             reads=[b_rt], writes=[b_rt])
        P.op("dve", lambda e: e.reciprocal(out=p1[:, 0:1], in_=dd[:, 2:3]), reads=[b_rt], writes=[b_rt])
        P.op("dve", lambda e: e.tensor_scalar(out=p1[:, 1:2], in0=p1[:, 0:1], scalar1=-1.0, scalar2=1.0,
                                               op0=ALU.mult, op1=ALU.add), reads=[b_rt], writes=[b_rt])
        P.op("dve", lambda e: e.tensor_scalar(out=comb[:, i, :], in0=k1, scalar1=p1[:, 0:1], scalar2=None, op0=ALU.mult),
             reads=[b_rt], writes=[b_c])
        P.op("dve", lambda e: e.scalar_tensor_tensor(out=comb[:, i, :], in0=k2, scalar=p1[:, 1:2], in1=comb[:, i, :],
                                                      op0=ALU.mult, op1=ALU.add), reads=[b_rt], writes=[b_c])


def const_tables():
    s = np.arange(128)[:, None]
    t = np.arange(128)[None, :]
    c = np.zeros((128, 5, 128), np.float32)
    c[:, 0, :] = (s == t)
    c[:, 1, :] = ((s // CHUNK) == (t // CHUNK)) & (s <= t)
    c[:, 2, :] = (s < t)
    c[:, 3, :] = -1.0 * (s >= t)
    c[:, 4, :] = -1.0 * (t == 64)
    return c


def core_inputs(inp, b0, nseq, S):
    f = lambda a: np.ascontiguousarray(np.asarray(a, dtype=np.float32))
    x = f(inp["x"])[b0:b0 + nseq, :S].reshape(nseq * S, D)
    c = f(inp["c"])[b0:b0 + nseq]
    cT = np.ascontiguousarray(c.T.reshape(8, 128, nseq).transpose(1, 0, 2))
    lb = f(inp["hgrn_lb_logits"])
    lbT = np.ascontiguousarray(lb.T.reshape(NH_H, 128, DEPTH).transpose(1, 0, 2))
    m = {
        "x": np.ascontiguousarray(x), "cT": cT, "lbT": lbT, "consts": const_tables(),
        "w_ada": f(inp["w_ada"]), "b_ada": f(inp["b_ada"]), "w_in": f(inp["w_in"]), "w_out": f(inp["w_out"]),
        "hgain": f(inp["hgrn_norm_gain"]),
        "w_dense_gate": f(inp["w_dense_gate"]), "w_dense_up": f(inp["w_dense_up"]),
        "w_dense_down": f(inp["w_dense_down"]),
        "w_router": np.ascontiguousarray(f(inp["w_router"]).reshape(2, 8, 128, NE).transpose(0, 2, 1, 3)),
        "w_moe_gate": f(inp["w_moe_gate"]), "w_moe_up": f(inp["w_moe_up"]), "w_moe_down": f(inp["w_moe_down"]),
        "ln_gain": f(inp["ln_gain"]), "ln_bias": f(inp["ln_bias"]),
    }
    return m


def kernel(**inputs):
    cfg = Cfg()
    nc = Builder(cfg).build()
    in_maps = [core_inputs(inputs, 2 * c, 2, 4096) for c in range(NCORES)]
    res = run_bass_kernel_spmd(nc, in_maps, core_ids=list(range(NCORES)))
    outs = [np.asarray(r["out"], dtype=np.float32).reshape(2, 4096, D) for r in res.results]
    return np.concatenate(outs, axis=0)
```

```python
import contextlib
import numpy as np
import ml_dtypes
import concourse.bass as bass
import concourse.mybir as mybir
from concourse.bass_utils import run_bass_kernel_spmd

F32 = mybir.dt.float32
BF16 = mybir.dt.bfloat16
AF = mybir.ActivationFunctionType
ALU = mybir.AluOpType
AX = mybir.AxisListType

D = 1024
DEPTH = 4
DH = 512
NH_H = 4
DSB = 512
NH_S = 8
DIN = 3584
FF_DENSE = 2816
FF_MOE = 3584
NE = 8
ALPHA = float((2 * DEPTH) ** 0.25)
LN_EPS = 1e-5
RMS_EPS = 1e-6
CHUNK = 64
NCORES = 8
DBG = {}


class Buf:
    __slots__ = ("name", "w", "r", "dsem", "dcnt", "psum")

    def __init__(self, name="", psum=False):
        self.name = name
        self.psum = psum
        self.w = None
        self.r = []
        self.dsem = None
        self.dcnt = 0


class _Eng:
    def __init__(self, name, sem):
        self.name = name
        self.sem = sem
        self.count = 0
        self.ops = []
        self.waited = {}


class Prog:
    ENG = ("pe", "act", "dve", "pool", "sp")

    def __init__(self, nc, name):
        self.nc = nc
        self.name = name
        self.stack = contextlib.ExitStack()
        self.eng = {}
        self.sems = []
        for e in self.ENG:
            sem = nc.alloc_semaphore(name=f"{name}_{e}")
            self.sems.append(sem)
            self.eng[e] = _Eng(e, sem)
        self.dma_toks = []
        self.nsem = 5

    def sbuf(self, name, shape, dt):
        return self.stack.enter_context(self.nc.sbuf_tensor(f"{self.name}_{name}", list(shape), dt))

    def psum(self, name, shape, dt=F32):
        return self.stack.enter_context(self.nc.psum_tensor(f"{self.name}_{name}", list(shape), dt))

    def _wait(self, e, tok):
        if tok is None:
            return
        sem, val, owner = tok
        if owner == "pe" and e.name == "pe":
            return
        key = id(sem)
        if e.waited.get(key, 0) >= val:
            return
        e.waited[key] = val
        e.ops.append(lambda eng, s=sem, v=val: eng.wait_ge(s, v))

    def _deps(self, e, reads, writes, extra):
        for b in reads:
            self._wait(e, b.w)
            if b.psum:
                for t in b.r:
                    if t[2] != e.name:
                        self._wait(e, t)
        for b in writes:
            self._wait(e, b.w)
            for t in b.r:
                self._wait(e, t)
        for t in extra:
            self._wait(e, t)

    def op(self, engine, fn, reads=(), writes=(), extra=(), signal=True):
        e = self.eng[engine]
        self._deps(e, reads, writes, extra)
        if signal:
            e.count += 1
            tok = (e.sem, e.count, engine)
            e.ops.append(lambda eng, f=fn, s=e.sem: f(eng).then_inc(s, 1))
        else:
            tok = (e.sem, e.count + 1, engine)
            e.ops.append(lambda eng, f=fn: f(eng))
        for b in writes:
            b.w = tok
            b.r = []
        for b in reads:
            b.r.append(tok)
        return tok

    def dma(self, queue, out, in_, owner, reads=(), writes=(), extra=(), **kw):
        e = self.eng[queue]
        self._deps(e, reads, writes, extra)
        if owner.dsem is None:
            owner.dsem = self.nc.alloc_semaphore(name=f"{self.name}_d{self.nsem}")
            self.sems.append(owner.dsem)
            self.nsem += 1
            owner.dcnt = 0
        owner.dcnt += 16
        tok = (owner.dsem, owner.dcnt, "dma")
        e.ops.append(lambda eng, o=out, i=in_, s=owner.dsem, k=kw: eng.dma_start(out=o, in_=i, **k).then_inc(s, 16))
        for b in writes:
            b.w = tok
            b.r = []
        for b in reads:
            b.r.append(tok)
        self.dma_toks.append(tok)
        return tok

    def run(self):
        nc = self.nc
        sp = self.eng["sp"]
        for t in self.dma_toks:
            self._wait(sp, t)
        with nc.Block() as block:
            @block.tensor
            def _(t):
                for f in self.eng["pe"].ops:
                    f(t)

            @block.scalar
            def _(a):
                for f in self.eng["act"].ops:
                    f(a)

            @block.vector
            def _(v):
                for f in self.eng["dve"].ops:
                    f(v)

            @block.gpsimd
            def _(g):
                for f in self.eng["pool"].ops:
                    f(g)

            @block.sync
            def _(s):
                for f in self.eng["sp"].ops:
                    f(s)
        nc.clear_and_free_semaphores(self.sems)
        nc.all_engine_barrier()
        self.stack.close()


class Cfg:
    def __init__(self, nseq=2, S=4096, layers=(0, 1, 2, 3), debug=()):
        self.nseq = nseq
        self.S = S
        self.ntok = nseq * S
        self.layers = tuple(layers)
        self.debug = tuple(debug)


def bcast_rows(ap1d, nparts):
    return ap1d.partition_broadcast(nparts)


class Builder:
    def __init__(self, cfg):
        self.cfg = cfg
        nc = self.nc = bass.Bass("TRN2", target_bir_lowering=False)
        NT = cfg.ntok
        ns = cfg.nseq

        def din(name, shape, dt=F32):
            return nc.dram_tensor(name, list(shape), dt, kind="ExternalInput").ap()

        def scratch(name, shape, dt=F32):
            kind = "ExternalOutput" if name in cfg.debug else "Internal"
            return nc.dram_tensor(name, list(shape), dt, kind=kind).ap()

        self.x_in = din("x", [NT, D])
        self.cT = din("cT", [128, 8, ns])
        self.w_ada = din("w_ada", [DEPTH, 2, D, 3 * D])
        self.b_ada = din("b_ada", [DEPTH, 2, 3 * D])
        self.w_in = din("w_in", [DEPTH, D, DIN])
        self.w_out = din("w_out", [DEPTH, D, D])
        self.lbT = din("lbT", [128, NH_H, DEPTH])
        self.hgain = din("hgain", [DEPTH, DH])
        self.wdg = din("w_dense_gate", [2, D, FF_DENSE])
        self.wdu = din("w_dense_up", [2, D, FF_DENSE])
        self.wdd = din("w_dense_down", [2, FF_DENSE, D])
        self.w_router = din("w_router", [2, 128, 8, NE])
        self.wmg = din("w_moe_gate", [2, NE, D, FF_MOE])
        self.wmu = din("w_moe_up", [2, NE, D, FF_MOE])
        self.wmd = din("w_moe_down", [2, NE, FF_MOE, D])
        self.ln_gain = din("ln_gain", [DEPTH, 2, D])
        self.ln_bias = din("ln_bias", [DEPTH, 2, D])
        self.consts = din("consts", [128, 5, 128])

        self.out = nc.dram_tensor("out", [NT, D], F32, kind="ExternalOutput").ap()
        self.xa = scratch("xa", [NT, D])
        self.xb = scratch("xb", [NT, D])
        self.mod = scratch("mod", [ns, 8, 3 * D])
        self.hqT = scratch("hqT", [DH, NT])
        self.hfT = scratch("hfT", [DH, NT])
        self.hi_tm = scratch("hi_tm", [NT, DH], BF16)
        self.hg_tm = scratch("hg_tm", [NT, DH])
        self.sqT = scratch("sqT", [DSB, NT], BF16)
        self.skT = scratch("skT", [DSB, NT], BF16)
        self.sv_tm = scratch("sv_tm", [NT, DSB], BF16)
        self.oT = scratch("oT", [D, NT], BF16)

    def dump(self, P, name, ap, buf, shape, dt=F32):
        if name not in self.cfg.debug:
            return
        t = self.nc.dram_tensor(name, list(shape), dt, kind="ExternalOutput").ap()
        P.dma("sp", t, ap, buf, reads=[buf])

    def build(self):
        nc = self.nc
        cfg = self.cfg
        with contextlib.ExitStack() as top:
            def pt(name, shape, dt):
                return top.enter_context(nc.sbuf_tensor(name, list(shape), dt))
            self.identf = pt("identf", [128, 128], F32)
            self.identb = pt("identb", [128, 128], BF16)
            self.mask_bd = pt("mask_bd", [128, 128], F32)
            self.mask_st = pt("mask_st", [128, 128], F32)
            self.tri_neg = pt("tri_neg", [128, 128], BF16)
            self.neg_col = pt("neg_col", [128, 128], BF16)
            self.lbA = pt("lbA", [128, NH_H * DEPTH], F32)
            self.lbB = pt("lbB", [128, NH_H * DEPTH], F32)
            self.ones_f = pt("ones_f", [128, 128], F32)
            self.zeros_b = pt("zeros_b", [128, 512], BF16)
            self.ones_b = pt("ones_b", [128, 128], BF16)

            self.phase_setup()
            xcur = self.x_in
            for li, l in enumerate(cfg.layers):
                last = li == len(cfg.layers) - 1
                self.phase_inproj(l, xcur)
                self.phase_hgrn(l)
                self.phase_sb(l)
                self.phase_outproj(l, xcur, self.xa)
                self.phase_ffn(l, self.xa, self.out if last else self.xb)
                xcur = self.xb
        return nc

    def phase_setup(self):
        nc, cfg = self.nc, self.cfg
        P = Prog(nc, "p0")
        ns = cfg.nseq
        cst = P.sbuf("cst", [128, 5, 128], F32)
        b_cst = Buf("cst")
        P.dma("sp", cst[:], self.consts, b_cst, writes=[b_cst])
        bp = Buf("persist")
        P.op("dve", lambda e: e.tensor_copy(out=self.identf[:], in_=cst[:, 0, :]), reads=[b_cst], writes=[bp])
        P.op("dve", lambda e: e.tensor_copy(out=self.identb[:], in_=cst[:, 0, :]), reads=[b_cst], writes=[bp])
        P.op("dve", lambda e: e.tensor_copy(out=self.mask_bd[:], in_=cst[:, 1, :]), reads=[b_cst], writes=[bp])
        P.op("dve", lambda e: e.tensor_copy(out=self.mask_st[:], in_=cst[:, 2, :]), reads=[b_cst], writes=[bp])
        P.op("dve", lambda e: e.tensor_copy(out=self.tri_neg[:], in_=cst[:, 3, :]), reads=[b_cst], writes=[bp])
        P.op("dve", lambda e: e.tensor_copy(out=self.neg_col[:], in_=cst[:, 4, :]), reads=[b_cst], writes=[bp])
        P.op("dve", lambda e: e.memset(self.ones_f[:], 1.0), writes=[bp])
        P.op("dve", lambda e: e.memset(self.ones_b[:], 1.0), writes=[bp])
        P.op("dve", lambda e: e.memset(self.zeros_b[:], 0.0), writes=[bp])

        lg = P.sbuf("lg", [128, NH_H, DEPTH], F32)
        ex = P.sbuf("ex", [128, NH_H, DEPTH], F32)
        sm = P.sbuf("sm", [128, NH_H], F32)
        lbt = P.sbuf("lbt", [128, NH_H, DEPTH], F32)
        b_lg, b_ex, b_sm, b_lb = Buf(), Buf(), Buf(), Buf()
        P.dma("sp", lg[:], self.lbT, b_lg, writes=[b_lg])
        P.op("act", lambda e: e.activation(out=ex[:], in_=lg[:], func=AF.Exp), reads=[b_lg], writes=[b_ex])
        P.op("dve", lambda e: e.tensor_reduce(out=sm[:], in_=ex[:], axis=AX.X, op=ALU.add), reads=[b_ex], writes=[b_sm])
        P.op("dve", lambda e: e.reciprocal(out=sm[:], in_=sm[:]), reads=[b_sm], writes=[b_sm])
        for h in range(NH_H):
            P.op("dve", lambda e, h=h: e.tensor_scalar(out=ex[:, h, :], in0=ex[:, h, :], scalar1=sm[:, h:h + 1],
                                                        scalar2=None, op0=ALU.mult),
                 reads=[b_sm, b_ex], writes=[b_ex])
        P.op("dve", lambda e: e.memset(lbt[:, :, 0:1], 0.0), writes=[b_lb])
        for l in range(1, DEPTH):
            P.op("dve", lambda e, l=l: e.tensor_tensor(out=lbt[:, :, l:l + 1], in0=lbt[:, :, l - 1:l],
                                                        in1=ex[:, :, l:l + 1], op=ALU.add),
                 reads=[b_ex, b_lb], writes=[b_lb])
        for l in range(DEPTH):
            P.op("dve", lambda e, l=l: e.tensor_scalar(out=self.lbA[:, l * 4:(l + 1) * 4], in0=lbt[:, :, l],
                                                        scalar1=-0.5, scalar2=0.5, op0=ALU.mult, op1=ALU.add),
                 reads=[b_lb], writes=[bp])
            P.op("dve", lambda e, l=l: e.tensor_scalar(out=self.lbB[:, l * 4:(l + 1) * 4], in0=lbt[:, :, l],
                                                        scalar1=0.5, scalar2=0.5, op0=ALU.mult, op1=ALU.add),
                 reads=[b_lb], writes=[bp])

        ct = P.sbuf("ct", [128, 8, ns], F32)
        sct = P.sbuf("sct", [128, 8, ns], F32)
        b_ct, b_sct = Buf(), Buf()
        P.dma("sp", ct[:], self.cT, b_ct, writes=[b_ct])
        P.op("act", lambda e: e.activation(out=sct[:], in_=ct[:], func=AF.Exp, scale=-1.0), reads=[b_ct], writes=[b_sct])
        P.op("dve", lambda e: e.tensor_scalar(out=sct[:], in0=sct[:], scalar1=1.0, scalar2=None, op0=ALU.add),
             reads=[b_sct], writes=[b_sct])
        P.op("dve", lambda e: e.reciprocal(out=sct[:], in_=sct[:]), reads=[b_sct], writes=[b_sct])
        P.op("dve", lambda e: e.tensor_tensor(out=sct[:], in0=sct[:], in1=ct[:], op=ALU.mult),
             reads=[b_sct, b_ct], writes=[b_sct])
        wbuf = [P.sbuf(f"wa{i}", [128, 8, 512], F32) for i in range(2)]
        b_w = [Buf(), Buf()]
        bias = [P.sbuf(f"bias{i}", [ns, 3 * D], F32) for i in range(2)]
        b_bias = [Buf(), Buf()]
        mt = [P.sbuf(f"mt{i}", [ns, 3 * D], F32) for i in range(2)]
        b_mt = [Buf(), Buf()]
        ps = [P.psum(f"ps{i}", [128, 512]) for i in range(2)]
        b_ps = [Buf(psum=True), Buf(psum=True)]
        k = 0
        for ls in range(8):
            l, s = divmod(ls, 2)
            if l not in cfg.layers:
                continue
            bi = ls % 2
            P.dma("sp", bias[bi][:], bcast_rows(self.b_ada[l, s, :], ns), b_bias[bi], writes=[b_bias[bi]])
            for n in range(6):
                wi = k % 2
                k += 1
                src = self.w_ada[l, s, :, n * 512:(n + 1) * 512].rearrange("(c p) f -> p c f", p=128)
                P.dma("sp", wbuf[wi][:], src, b_w[wi], writes=[b_w[wi]])
                for dc in range(8):
                    P.op("pe", lambda e, wi=wi, dc=dc: e.matmul(ps[wi][0:ns, :], sct[:, dc, :], wbuf[wi][:, dc, :],
                                                                  start=(dc == 0), stop=(dc == 7)),
                         reads=[b_sct, b_w[wi]], writes=[b_ps[wi]], signal=(dc == 7))
                P.op("dve", lambda e, wi=wi, bi=bi, n=n: e.tensor_tensor(
                    out=mt[bi][:, n * 512:(n + 1) * 512], in0=ps[wi][0:ns, :], in1=bias[bi][:, n * 512:(n + 1) * 512],
                    op=ALU.add), reads=[b_ps[wi], b_bias[bi]], writes=[b_mt[bi]])
            P.op("dve", lambda e, bi=bi: e.tensor_scalar(out=mt[bi][:, D:2 * D], in0=mt[bi][:, D:2 * D], scalar1=1.0,
                                                          scalar2=None, op0=ALU.add), reads=[b_mt[bi]], writes=[b_mt[bi]])
            P.dma("sp", self.mod[:, ls, :], mt[bi][:], b_mt[bi], reads=[b_mt[bi]])
        P.run()

    def load_bcast(self, P, tile, buf, src1d):
        P.dma("sp", tile[:], bcast_rows(src1d, 128), buf, writes=[buf])

    def phase_ffn(self, l, x_src, x_dst):
        nc, cfg = self.nc, self.cfg
        moe = (l % 2 == 1)
        j = l // 2
        P = Prog(nc, f"f{l}")
        TB = 1024
        NTI = TB // 128
        nblk = cfg.ntok // TB
        blk_per_seq = cfg.S // TB
        FF = FF_MOE if moe else FF_DENSE
        nfc = FF // 128
        groups = [(g0, min(4, nfc - g0)) for g0 in range(0, nfc, 4)]
        nexp = NE if moe else 1
        nexp = DBG.get('nexp', nexp)

        sc1 = P.sbuf("sc1", [128, D], F32); sh = P.sbuf("sh", [128, D], F32); gt = P.sbuf("gt", [128, D], F32)
        lng = P.sbuf("lng", [128, D], F32); lnb = P.sbuf("lnb", [128, D], F32)
        b_mod, b_ln = Buf("mod"), Buf("ln")
        xt = [P.sbuf(f"xt{i}", [128, D], F32) for i in range(2)]
        b_xt = [Buf(), Buf()]
        hf = [P.sbuf(f"hf{i}", [128, D], F32) for i in range(2)]
        b_hf = [Buf(), Buf()]
        hT = P.sbuf("hT", [128, 8, TB], BF16)
        b_hT = [Buf() for _ in range(NTI)]
        yacc = P.sbuf("yacc", [128, NTI, D], F32)
        b_y = [[Buf(), Buf()] for _ in range(NTI)]
        wg = [P.sbuf(f"wg{i}", [128, 8, 512], BF16) for i in range(2)]
        wu = [P.sbuf(f"wu{i}", [128, 8, 512], BF16) for i in range(2)]
        wd = [P.sbuf(f"wd{i}", [128, 4, D], BF16) for i in range(2)]
        b_wgu = [Buf(), Buf()]
        b_wd = [Buf(), Buf()]
        aT = [P.sbuf(f"aT{i}", [128, 4, TB], BF16) for i in range(2)]
        b_aT = [[Buf(), Buf()] for _ in range(2)]
        sg = [P.sbuf(f"sg{i}", [128, 512], F32) for i in range(2)]
        b_sg = [Buf(), Buf()]
        ot = [P.sbuf(f"ot{i}", [128, D], F32) for i in range(2)]
        b_ot = [Buf(), Buf()]
        stats = P.sbuf("stats", [128, NTI, 2, 6], F32)
        mv = P.sbuf("mv", [128, NTI, 2], F32)
        rstd = P.sbuf("rstd", [128, NTI], F32)
        b_stats = [Buf() for _ in range(NTI)]
        b_mv = Buf(); b_rstd = Buf()
        if moe:
            hTf = P.sbuf("hTf", [128, 8, 128], F32); b_hTf = Buf()
            wr = P.sbuf("wr", [128, 8, NE], F32); b_wr = Buf()
            comb = P.sbuf("comb", [128, NTI, NE], F32); b_comb = [Buf() for _ in range(NTI)]
            rt = P.sbuf("rt", [128, 8, NE], F32); b_rt = Buf()
            if DBG.get("router", 1):
                P.dma("sp", wr[:], self.w_router[j], b_wr, writes=[b_wr])
        psT = [P.psum(f"psT{i}", [128, 512]) for i in range(2)]; b_psT = [Buf(psum=True), Buf(psum=True)]
        gps = [P.psum(f"gps{i}", [128, 512]) for i in range(2)]; b_gps = [Buf(psum=True), Buf(psum=True)]
        ups = [P.psum(f"ups{i}", [128, 512]) for i in range(2)]; b_ups = [Buf(psum=True), Buf(psum=True)]
        yps = [P.psum(f"yps{i}", [128, 512]) for i in range(2)]; b_yps = [Buf(psum=True), Buf(psum=True)]

        P.dma("sp", lng[:], bcast_rows(self.ln_gain[l, 1, :], 128), b_ln, writes=[b_ln])
        P.dma("sp", lnb[:], bcast_rows(self.ln_bias[l, 1, :], 128), b_ln, writes=[b_ln])

        if moe:
            WG, WU, WD = self.wmg[j], self.wmu[j], self.wmd[j]
        else:
            WG, WU, WD = self.wdg[j:j + 1], self.wdu[j:j + 1], self.wdd[j:j + 1]

        gcount = 0
        xk = 0
        for blk in range(nblk):
            t0 = blk * TB
            bseq = blk // blk_per_seq
            if blk % blk_per_seq == 0:
                ls = l * 2 + 1
                P.dma("sp", sh[:], bcast_rows(self.mod[bseq, ls, 0:D], 128), b_mod, writes=[b_mod])
                P.dma("sp", sc1[:], bcast_rows(self.mod[bseq, ls, D:2 * D], 128), b_mod, writes=[b_mod])
                P.dma("sp", gt[:], bcast_rows(self.mod[bseq, ls, 2 * D:3 * D], 128), b_mod, writes=[b_mod])
            for i in range(NTI):
                xi = xk % 2
                xk += 1
                r0 = t0 + i * 128
                P.dma("sp", xt[xi][:], x_src[r0:r0 + 128, :], b_xt[xi], writes=[b_xt[xi]])
                P.op("dve", lambda e, xi=xi: e.tensor_tensor(out=hf[xi][:], in0=xt[xi][:], in1=sc1[:], op=ALU.mult),
                     reads=[b_xt[xi], b_mod], writes=[b_hf[xi]])
                P.op("pool", lambda e, xi=xi: e.tensor_tensor(out=hf[xi][:], in0=hf[xi][:], in1=sh[:], op=ALU.add),
                     reads=[b_hf[xi], b_mod], writes=[b_hf[xi]])
                for hb in range(2):
                    for q in range(4):
                        dc = hb * 4 + q
                        P.op("pe", lambda e, xi=xi, hb=hb, q=q, dc=dc: e.transpose(
                            psT[hb][:, q * 128:(q + 1) * 128], hf[xi][:, dc * 128:(dc + 1) * 128], self.identf[:]),
                            reads=[b_hf[xi]], writes=[b_psT[hb]], signal=(q == 3))
                    if moe:
                        P.op("act", lambda e, hb=hb: e.activation(
                            out=hTf[:, hb * 4:(hb + 1) * 4, :], in_=psT[hb][:].rearrange("p (c t) -> p c t", t=128),
                            func=AF.Copy), reads=[b_psT[hb]], writes=[b_hTf])
                    P.op("act" if not moe else "dve", lambda e, hb=hb, i=i: e.tensor_copy(
                        out=hT[:, hb * 4:(hb + 1) * 4, i * 128:(i + 1) * 128],
                        in_=psT[hb][:].rearrange("p (c t) -> p c t", t=128)) if moe else e.activation(
                        out=hT[:, hb * 4:(hb + 1) * 4, i * 128:(i + 1) * 128],
                        in_=psT[hb][:].rearrange("p (c t) -> p c t", t=128), func=AF.Copy),
                        reads=[b_psT[hb]], writes=[b_hT[i]])
                if moe and DBG.get("router", 1) == 0:
                    P.op("dve", lambda e, i=i: e.memset(comb[:, i, :], 0.125), writes=[b_comb[i]])
                if moe and DBG.get("router", 1):
                    for dc in range(8):
                        P.op("pe", lambda e, dc=dc: e.matmul(yps[1][:, 0:NE], hTf[:, dc, :], wr[:, dc, :],
                                                              start=(dc == 0), stop=(dc == 7)),
                             reads=[b_hTf, b_wr], writes=[b_yps[1]], signal=(dc == 7))
                    self.router_top2(P, yps[1], b_yps[1], rt, b_rt, comb, b_comb[i], i)
            for ex in range(nexp):
                for (g0, ng) in groups:
                    slot = gcount % 2
                    gcount += 1
                    f0 = g0 * 128
                    fw = ng * 128
                    P.dma("pool", wg[slot][:, :, 0:fw], WG[ex, :, f0:f0 + fw].rearrange("(c p) f -> p c f", p=128),
                          b_wgu[slot], writes=[b_wgu[slot]])
                    P.dma("pool", wu[slot][:, :, 0:fw], WU[ex, :, f0:f0 + fw].rearrange("(c p) f -> p c f", p=128),
                          b_wgu[slot], writes=[b_wgu[slot]])
                    P.dma("pool", wd[slot][:, 0:ng, :], WD[ex, f0:f0 + fw, :].rearrange("(c p) d -> p c d", p=128),
                          b_wd[slot], writes=[b_wd[slot]])
                    for fc in range(ng):
                        for half in range(2):
                            pb = (fc * 2 + half) % 2
                            for dc in range(8):
                                P.op("pe", lambda e, slot=slot, fc=fc, half=half, dc=dc, pb=pb: e.matmul(
                                    gps[pb][:], wg[slot][:, dc, fc * 128:(fc + 1) * 128],
                                    hT[:, dc, half * 512:(half + 1) * 512], start=(dc == 0), stop=(dc == 7)),
                                    reads=[b_wgu[slot]] + b_hT[half * 4:(half + 1) * 4], writes=[b_gps[pb]],
                                    signal=(dc == 7))
                            for dc in range(8):
                                P.op("pe", lambda e, slot=slot, fc=fc, half=half, dc=dc, pb=pb: e.matmul(
                                    ups[pb][:], wu[slot][:, dc, fc * 128:(fc + 1) * 128],
                                    hT[:, dc, half * 512:(half + 1) * 512], start=(dc == 0), stop=(dc == 7)),
                                    reads=[b_wgu[slot]] + b_hT[half * 4:(half + 1) * 4], writes=[b_ups[pb]],
                                    signal=(dc == 7))
                            P.op("act", lambda e, pb=pb: e.activation(out=sg[pb][:], in_=gps[pb][:], func=AF.Silu),
                                 reads=[b_gps[pb]], writes=[b_sg[pb]])
                            P.op("dve", lambda e, pb=pb, slot=slot, fc=fc, half=half: e.tensor_tensor(
                                out=aT[slot][:, fc, half * 512:(half + 1) * 512], in0=ups[pb][:], in1=sg[pb][:],
                                op=ALU.mult), reads=[b_ups[pb], b_sg[pb]], writes=[b_aT[slot][half]])
                    first = (ex == 0 and g0 == 0)
                    for i in range(NTI):
                        for h2 in range(2):
                            pb = (i * 2 + h2) % 2
                            for fc in range(ng):
                                P.op("pe", lambda e, slot=slot, fc=fc, i=i, h2=h2, pb=pb, ng=ng: e.matmul(
                                    yps[pb][:], aT[slot][:, fc, i * 128:(i + 1) * 128],
                                    wd[slot][:, fc, h2 * 512:(h2 + 1) * 512], start=(fc == 0), stop=(fc == ng - 1)),
                                    reads=[b_aT[slot][i // 4], b_wd[slot]], writes=[b_yps[pb]], signal=(fc == ng - 1))
                            ysl = yacc[:, i, h2 * 512:(h2 + 1) * 512]
                            if moe:
                                csc = comb[:, i, ex:ex + 1]
                                if first:
                                    P.op("dve", lambda e, pb=pb, ysl=ysl, csc=csc: e.tensor_scalar(
                                        out=ysl, in0=yps[pb][:], scalar1=csc, scalar2=None, op0=ALU.mult),
                                        reads=[b_yps[pb], b_comb[i]], writes=[b_y[i][h2]])
                                else:
                                    P.op("dve", lambda e, pb=pb, ysl=ysl, csc=csc: e.scalar_tensor_tensor(
                                        out=ysl, in0=yps[pb][:], scalar=csc, in1=ysl, op0=ALU.mult, op1=ALU.add),
                                        reads=[b_yps[pb], b_comb[i]], writes=[b_y[i][h2]])
                            else:
                                if first:
                                    P.op("act", lambda e, pb=pb, ysl=ysl: e.activation(out=ysl, in_=yps[pb][:], func=AF.Copy),
                                         reads=[b_yps[pb]], writes=[b_y[i][h2]])
                                else:
                                    P.op("dve", lambda e, pb=pb, ysl=ysl: e.tensor_tensor(
                                        out=ysl, in0=yps[pb][:], in1=ysl, op=ALU.add),
                                        reads=[b_yps[pb]], writes=[b_y[i][h2]])
            for i in range(NTI):
                xi = xk % 2
                xk += 1
                r0 = t0 + i * 128
                P.dma("sp", xt[xi][:], x_src[r0:r0 + 128, :], b_xt[xi], writes=[b_xt[xi]])
                P.op("pool", lambda e, i=i: e.tensor_tensor(out=yacc[:, i, :], in0=yacc[:, i, :], in1=gt[:], op=ALU.mult),
                     reads=[b_mod], writes=b_y[i])
                P.op("dve", lambda e, i=i, xi=xi: e.scalar_tensor_tensor(
                    out=yacc[:, i, :], in0=xt[xi][:], scalar=ALPHA, in1=yacc[:, i, :], op0=ALU.mult, op1=ALU.add),
                    reads=[b_xt[xi]], writes=b_y[i])
                for h2 in range(2):
                    P.op("dve", lambda e, i=i, h2=h2: e.bn_stats(out=stats[:, i, h2, :], in_=yacc[:, i, h2 * 512:(h2 + 1) * 512]),
                         reads=b_y[i], writes=[b_stats[i]])
                P.op("dve", lambda e, i=i: e.bn_aggr(out=mv[:, i, :], in_=stats[:, i, :, :].rearrange("p a b -> p (a b)")),
                     reads=[b_stats[i]], writes=[b_mv])
            self.rstd_from_var(P, mv, b_mv, rstd, b_rstd, NTI)
            for i in range(NTI):
                oi = i % 2
                r0 = t0 + i * 128
                P.op("dve", lambda e, i=i, oi=oi: e.tensor_scalar(
                    out=ot[oi][:], in0=yacc[:, i, :], scalar1=mv[:, i, 0:1], scalar2=rstd[:, i:i + 1],
                    op0=ALU.subtract, op1=ALU.mult), reads=b_y[i] + [b_mv, b_rstd], writes=[b_ot[oi]])
                P.op("pool", lambda e, oi=oi: e.tensor_tensor(out=ot[oi][:], in0=ot[oi][:], in1=lng[:], op=ALU.mult),
                     reads=[b_ln], writes=[b_ot[oi]])
                P.op("pool", lambda e, oi=oi: e.tensor_tensor(out=ot[oi][:], in0=ot[oi][:], in1=lnb[:], op=ALU.add),
                     reads=[b_ln], writes=[b_ot[oi]])
                P.dma("sp", x_dst[r0:r0 + 128, :], ot[oi][:], b_ot[oi], reads=[b_ot[oi]])
        P.run()

    def phase_inproj(self, l, x_src):
        nc, cfg = self.nc, self.cfg
        P = Prog(nc, f"i{l}")
        TB = 512
        nblk = cfg.ntok // TB
        blk_per_seq = cfg.S // TB
        w = P.sbuf("w", [128, 8, DIN], BF16); b_w = Buf()
        for n in range(7):
            P.dma("pool", w[:, :, n * 512:(n + 1) * 512],
                  self.w_in[l, :, n * 512:(n + 1) * 512].rearrange("(c p) f -> p c f", p=128), b_w, writes=[b_w])
        sc1 = P.sbuf("sc1", [128, D], F32); sh = P.sbuf("sh", [128, D], F32); b_mod = Buf()
        xt = [P.sbuf(f"xt{i}", [128, D], F32) for i in range(2)]; b_xt = [Buf(), Buf()]
        hf = [P.sbuf(f"hf{i}", [128, D], F32) for i in range(2)]; b_hf = [Buf(), Buf()]
        hT = [P.sbuf(f"hT{i}", [128, 8, TB], BF16) for i in range(2)]
        b_hT = [[Buf() for _ in range(4)] for _ in range(2)]
        NST = 4
        stf = [P.sbuf(f"stf{i}", [128, 512], F32) for i in range(NST)]; b_stf = [Buf() for _ in range(NST)]
        stb = [P.sbuf(f"stb{i}", [128, 512], BF16) for i in range(NST)]; b_stb = [Buf() for _ in range(NST)]
        psT = [P.psum(f"psT{i}", [128, 512]) for i in range(2)]; b_psT = [Buf(psum=True), Buf(psum=True)]
        NPS = 6
        mm = [P.psum(f"mm{i}", [128, 512]) for i in range(NPS)]; b_mm = [Buf(psum=True) for _ in range(NPS)]
        xk = 0; kf = 0; kb_ = 0; km = 0
        for blk in range(nblk):
            t0 = blk * TB
            bseq = blk // blk_per_seq
            hs = blk % 2
            if blk % blk_per_seq == 0:
                ls = l * 2
                P.dma("sp", sh[:], bcast_rows(self.mod[bseq, ls, 0:D], 128), b_mod, writes=[b_mod])
                P.dma("sp", sc1[:], bcast_rows(self.mod[bseq, ls, D:2 * D], 128), b_mod, writes=[b_mod])
            for i in range(4):
                xi = xk % 2; xk += 1
                r0 = t0 + i * 128
                P.dma("sp", xt[xi][:], x_src[r0:r0 + 128, :], b_xt[xi], writes=[b_xt[xi]])
                P.op("dve", lambda e, xi=xi: e.tensor_tensor(out=hf[xi][:], in0=xt[xi][:], in1=sc1[:], op=ALU.mult),
                     reads=[b_xt[xi], b_mod], writes=[b_hf[xi]])
                P.op("pool", lambda e, xi=xi: e.tensor_tensor(out=hf[xi][:], in0=hf[xi][:], in1=sh[:], op=ALU.add),
                     reads=[b_hf[xi], b_mod], writes=[b_hf[xi]])
                for hb in range(2):
                    for q in range(4):
                        dc = hb * 4 + q
                        P.op("pe", lambda e, xi=xi, hb=hb, q=q, dc=dc: e.transpose(
                            psT[hb][:, q * 128:(q + 1) * 128], hf[xi][:, dc * 128:(dc + 1) * 128], self.identf[:]),
                            reads=[b_hf[xi]], writes=[b_psT[hb]], signal=(q == 3))
                    P.op("act", lambda e, hb=hb, i=i, hs=hs: e.activation(
                        out=hT[hs][:, hb * 4:(hb + 1) * 4, i * 128:(i + 1) * 128],
                        in_=psT[hb][:].rearrange("p (c t) -> p c t", t=128), func=AF.Copy),
                        reads=[b_psT[hb]], writes=[b_hT[hs][i]])
            for cc in list(range(0, 8)) + list(range(16, 24)):
                pb = km % NPS; km += 1
                for dc in range(8):
                    P.op("pe", lambda e, cc=cc, dc=dc, pb=pb, hs=hs: e.matmul(
                        mm[pb][:], w[:, dc, cc * 128:(cc + 1) * 128], hT[hs][:, dc, :], start=(dc == 0), stop=(dc == 7)),
                        reads=[b_w] + b_hT[hs], writes=[b_mm[pb]], signal=(dc == 7))
                if cc < 8:
                    si = kf % NST; kf += 1
                    if cc < 4:
                        P.op("act", lambda e, pb=pb, si=si: e.activation(out=stf[si][:], in_=mm[pb][:], func=AF.Silu),
                             reads=[b_mm[pb]], writes=[b_stf[si]])
                        dst = self.hqT[cc * 128:(cc + 1) * 128, t0:t0 + TB]
                    else:
                        P.op("act", lambda e, pb=pb, si=si: e.activation(out=stf[si][:], in_=mm[pb][:], func=AF.Tanh, scale=0.5),
                             reads=[b_mm[pb]], writes=[b_stf[si]])
                        dst = self.hfT[(cc - 4) * 128:(cc - 3) * 128, t0:t0 + TB]
                    P.dma("sp", dst, stf[si][:], b_stf[si], reads=[b_stf[si]])
                else:
                    si = kb_ % NST; kb_ += 1
                    sc = 0.125 if cc < 20 else 1.0
                    P.op("dve", lambda e, pb=pb, si=si, sc=sc: e.tensor_scalar(
                        out=stb[si][:], in0=mm[pb][:], scalar1=sc, scalar2=None, op0=ALU.mult),
                        reads=[b_mm[pb]], writes=[b_stb[si]])
                    if cc < 20:
                        dst = self.sqT[(cc - 16) * 128:(cc - 15) * 128, t0:t0 + TB]
                    else:
                        dst = self.skT[(cc - 20) * 128:(cc - 19) * 128, t0:t0 + TB]
                    P.dma("sp", dst, stb[si][:], b_stb[si], reads=[b_stb[si]])
            for i in range(4):
                r0 = t0 + i * 128
                for seg, c0 in (("hi", 1024), ("hg", 1536), ("sv", 3072)):
                    pb = km % NPS; km += 1
                    for dc in range(8):
                        P.op("pe", lambda e, dc=dc, pb=pb, hs=hs, i=i, c0=c0: e.matmul(
                            mm[pb][:], hT[hs][:, dc, i * 128:(i + 1) * 128], w[:, dc, c0:c0 + 512],
                            start=(dc == 0), stop=(dc == 7)),
                            reads=[b_w, b_hT[hs][i]], writes=[b_mm[pb]], signal=(dc == 7))
                    if seg == "hg":
                        si = kf % NST; kf += 1
                        P.op("act", lambda e, pb=pb, si=si: e.activation(out=stf[si][:], in_=mm[pb][:], func=AF.Silu),
                             reads=[b_mm[pb]], writes=[b_stf[si]])
                        P.dma("sp", self.hg_tm[r0:r0 + 128, :], stf[si][:], b_stf[si], reads=[b_stf[si]])
                    else:
                        si = kb_ % NST; kb_ += 1
                        P.op("dve", lambda e, pb=pb, si=si: e.tensor_copy(out=stb[si][:], in_=mm[pb][:]),
                             reads=[b_mm[pb]], writes=[b_stb[si]])
                        dst = self.hi_tm if seg == "hi" else self.sv_tm
                        P.dma("sp", dst[r0:r0 + 128, :], stb[si][:], b_stb[si], reads=[b_stb[si]])
        P.run()

    def phase_outproj(self, l, x_src, x_dst):
        nc, cfg = self.nc, self.cfg
        P = Prog(nc, f"o{l}")
        TB = 512
        nblk = cfg.ntok // TB
        blk_per_seq = cfg.S // TB
        w = P.sbuf("w", [128, 8, D], BF16); b_w = Buf()
        P.dma("pool", w[:], self.w_out[l].rearrange("(c p) f -> p c f", p=128), b_w, writes=[b_w])
        gt = P.sbuf("gt", [128, D], F32); b_mod = Buf()
        lng = P.sbuf("lng", [128, D], F32); lnb = P.sbuf("lnb", [128, D], F32); b_ln = Buf()
        P.dma("sp", lng[:], bcast_rows(self.ln_gain[l, 0, :], 128), b_ln, writes=[b_ln])
        P.dma("sp", lnb[:], bcast_rows(self.ln_bias[l, 0, :], 128), b_ln, writes=[b_ln])
        oTs = [P.sbuf(f"oTs{i}", [128, 8, TB], BF16) for i in range(2)]; b_oTs = [Buf(), Buf()]
        xt = [P.sbuf(f"xt{i}", [128, D], F32) for i in range(2)]; b_xt = [Buf(), Buf()]
        zt = P.sbuf("zt", [128, 4, D], F32); b_z = [Buf() for _ in range(4)]
        ot = [P.sbuf(f"ot{i}", [128, D], F32) for i in range(2)]; b_ot = [Buf(), Buf()]
        stats = P.sbuf("stats", [128, 4, 2, 6], F32); b_stats = [Buf() for _ in range(4)]
        mv = P.sbuf("mv", [128, 4, 2], F32); b_mv = Buf()
        rstd = P.sbuf("rstd", [128, 4], F32); b_rstd = Buf()
        yps = [P.psum(f"yps{i}", [128, 512]) for i in range(4)]; b_yps = [Buf(psum=True) for _ in range(4)]
        xk = 0; kp = 0
        for blk in range(nblk):
            t0 = blk * TB
            bseq = blk // blk_per_seq
            os_ = blk % 2
            if blk % blk_per_seq == 0:
                P.dma("sp", gt[:], bcast_rows(self.mod[bseq, l * 2, 2 * D:3 * D], 128), b_mod, writes=[b_mod])
            P.dma("sp", oTs[os_][:], self.oT[:, t0:t0 + TB].rearrange("(c p) t -> p c t", p=128), b_oTs[os_],
                  writes=[b_oTs[os_]])
            for i in range(4):
                xi = xk % 2; xk += 1
                r0 = t0 + i * 128
                P.dma("sp", xt[xi][:], x_src[r0:r0 + 128, :], b_xt[xi], writes=[b_xt[xi]])
                for h2 in range(2):
                    pb = kp % 4; kp += 1
                    for cc in range(8):
                        P.op("pe", lambda e, cc=cc, pb=pb, os_=os_, i=i, h2=h2: e.matmul(
                            yps[pb][:], oTs[os_][:, cc, i * 128:(i + 1) * 128], w[:, cc, h2 * 512:(h2 + 1) * 512],
                            start=(cc == 0), stop=(cc == 7)), reads=[b_w, b_oTs[os_]], writes=[b_yps[pb]], signal=(cc == 7))
                    P.op("dve", lambda e, pb=pb, i=i, h2=h2: e.tensor_tensor(
                        out=zt[:, i, h2 * 512:(h2 + 1) * 512], in0=yps[pb][:], in1=gt[:, h2 * 512:(h2 + 1) * 512], op=ALU.mult),
                        reads=[b_yps[pb], b_mod], writes=[b_z[i]])
                P.op("dve", lambda e, i=i, xi=xi: e.scalar_tensor_tensor(
                    out=zt[:, i, :], in0=xt[xi][:], scalar=ALPHA, in1=zt[:, i, :], op0=ALU.mult, op1=ALU.add),
                    reads=[b_xt[xi]], writes=[b_z[i]])
                for h2 in range(2):
                    P.op("dve", lambda e, i=i, h2=h2: e.bn_stats(out=stats[:, i, h2, :], in_=zt[:, i, h2 * 512:(h2 + 1) * 512]),
                         reads=[b_z[i]], writes=[b_stats[i]])
                P.op("dve", lambda e, i=i: e.bn_aggr(out=mv[:, i, :], in_=stats[:, i, :, :].rearrange("p a b -> p (a b)")),
                     reads=[b_stats[i]], writes=[b_mv])
            self.rstd_from_var(P, mv, b_mv, rstd, b_rstd, 4)
            for i in range(4):
                oi = i % 2
                r0 = t0 + i * 128
                P.op("dve", lambda e, i=i, oi=oi: e.tensor_scalar(
                    out=ot[oi][:], in0=zt[:, i, :], scalar1=mv[:, i, 0:1], scalar2=rstd[:, i:i + 1],
                    op0=ALU.subtract, op1=ALU.mult), reads=[b_z[i], b_mv, b_rstd], writes=[b_ot[oi]])
                P.op("pool", lambda e, oi=oi: e.tensor_tensor(out=ot[oi][:], in0=ot[oi][:], in1=lng[:], op=ALU.mult),
                     reads=[b_ln], writes=[b_ot[oi]])
                P.op("pool", lambda e, oi=oi: e.tensor_tensor(out=ot[oi][:], in0=ot[oi][:], in1=lnb[:], op=ALU.add),
                     reads=[b_ln], writes=[b_ot[oi]])
                P.dma("sp", x_dst[r0:r0 + 128, :], ot[oi][:], b_ot[oi], reads=[b_ot[oi]])
        P.run()

    def phase_sb(self, l):
        nc, cfg = self.nc, self.cfg
        P = Prog(nc, f"s{l}")
        S = cfg.S
        NB = S // 128
        NG = S // 512
        ka = [P.sbuf(f"ka{i}", [128, S], BF16) for i in range(2)]; b_ka = [Buf(), Buf()]
        qz = [P.sbuf(f"qz{i}", [128, S], BF16) for i in range(2)]; b_qz = [Buf(), Buf()]
        NQA = 3
        qa = [[P.sbuf(f"qa{i}{j}", [128, S], BF16) for j in range(NQA)] for i in range(2)]
        b_qa = [[Buf() for _ in range(NQA)] for _ in range(2)]
        vall = P.sbuf("vall", [128, NB, DSB], BF16); b_v = Buf()
        e1 = [P.sbuf(f"e1{i}", [128, 512], F32) for i in range(2)]; b_e1 = [Buf(), Buf()]
        NSP = 4
        sp = [P.sbuf(f"sp{i}", [128, 512], BF16) for i in range(NSP)]; b_sp = [Buf() for _ in range(NSP)]
        At = [P.sbuf(f"A{i}", [128, 512], BF16) for i in range(NSP)]; b_A = [Buf() for _ in range(NSP)]
        ost = [P.sbuf(f"ost{i}", [64, 512], BF16) for i in range(2)]; b_ost = [Buf(), Buf()]
        zps = [P.psum(f"z{i}", [128, 512]) for i in range(2)]; b_z = [Buf(psum=True), Buf(psum=True)]
        gps = [P.psum(f"g{i}", [128, 512]) for i in range(2)]; b_g = [Buf(psum=True), Buf(psum=True)]
        cps = [P.psum(f"c{i}", [128, 512]) for i in range(2)]; b_c = [Buf(psum=True), Buf(psum=True)]
        ops_ = [P.psum(f"o{i}", [128, 512]) for i in range(2)]; b_o = [Buf(psum=True), Buf(psum=True)]
        for i in range(2):
            P.op("pool", lambda e, i=i: e.memset(ka[i][64:128, :], 0.0), writes=[b_ka[i]])
            P.op("pool", lambda e, i=i: e.memset(ka[i][64:65, :], 1.0), writes=[b_ka[i]])
            P.op("pool", lambda e, i=i: e.memset(qz[i][64:128, :], 0.0), writes=[b_qz[i]])
            for j in range(NQA):
                P.op("pool", lambda e, i=i, j=j: e.memset(qa[i][j][64:128, :], 0.0), writes=[b_qa[i][j]])

        steps = []
        hk = 0
        for b in range(cfg.nseq):
            for h in range(NH_S):
                hp = hk % 2
                for g in range(NG):
                    kmax = 4 * g + 3
                    for kb in range(kmax, -1, -1):
                        i = kb - 4 * g
                        n0 = max(i, 0) * 128
                        steps.append(dict(b=b, h=h, hp=hp, g=g, kb=kb, n0=n0, diag=(i >= 0),
                                          first=(kb == kmax), last=(kb == 0), newhead=(g == 0 and kb == kmax),
                                          gk=None))
                hk += 1
        gk = -1
        for st in steps:
            if st["first"]:
                gk += 1
            st["gp"] = gk % 2
        ost_k = [0]

        def load_head(st):
            b, h, hp = st["b"], st["h"], st["hp"]
            c0 = b * S
            P.dma("pool", ka[hp][0:64, :], self.skT[h * 64:(h + 1) * 64, c0:c0 + S], b_ka[hp], writes=[b_ka[hp]])
            P.dma("pool", qz[hp][0:64, :], self.sqT[h * 64:(h + 1) * 64, c0:c0 + S], b_qz[hp], writes=[b_qz[hp]])
            for j in range(NQA):
                P.dma("pool", qa[hp][j][0:64, :], self.sqT[h * 64:(h + 1) * 64, c0:c0 + S], b_qa[hp][j], writes=[b_qa[hp][j]])

        def emit_p1(j):
            st = steps[j]
            hp, kb, n0 = st["hp"], st["kb"], st["n0"]
            c0 = st["g"] * 512
            zb = j % 2
            P.op("pe", lambda e: e.matmul(zps[zb][:, n0:512], ka[hp][:, kb * 128:(kb + 1) * 128],
                                           qz[hp][:, c0 + n0:c0 + 512], start=True, stop=True),
                 reads=[b_ka[hp], b_qz[hp]], writes=[b_z[zb]])

        def emit_a12(j):
            st = steps[j]
            n0 = st["n0"]
            zb = j % 2; eb = j % 2; sb = j % NSP
            P.op("act", lambda e: e.activation(out=e1[eb][:, n0:512], in_=zps[zb][:, n0:512], func=AF.Exp),
                 reads=[b_z[zb]], writes=[b_e1[eb]])
            P.op("act", lambda e: e.activation(out=sp[sb][:, n0:512], in_=e1[eb][:, n0:512], func=AF.Ln, bias=1.0, scale=1.0),
                 reads=[b_e1[eb]], writes=[b_sp[sb]])
            if st["diag"]:
                P.op("dve", lambda e: e.tensor_tensor(out=sp[sb][:, n0:n0 + 128], in0=sp[sb][:, n0:n0 + 128],
                                                       in1=self.mask_st[:], op=ALU.mult), writes=[b_sp[sb]])

        def emit_carry(j):
            st = steps[j]
            n0 = st["n0"]
            sb = j % NSP
            hp, gp = st["hp"], st["gp"]
            c0 = st["g"] * 512
            par = j % NQA
            if st["first"]:
                P.op("pe", lambda e: e.matmul(cps[gp][:, :], self.zeros_b[:, 0:128], self.zeros_b[:, 0:512], start=True, stop=True),
                     writes=[b_c[gp]])
            P.op("dve", lambda e: e.tensor_copy(out=qa[hp][par][64:65, c0 + n0:c0 + 512], in_=cps[gp][64:65, n0:512]),
                 reads=[b_c[gp]], writes=[b_qa[hp][par]])
            P.op("pe", lambda e: e.matmul(cps[gp][:, n0:512], self.neg_col[:, :], sp[sb][:, n0:512], start=False, stop=True,
                                           skip_group_check=True),
                 reads=[b_sp[sb]], writes=[b_c[gp]])

        def emit_main(j):
            st = steps[j]
            hp, kb, n0, gp, h = st["hp"], st["kb"], st["n0"], st["gp"], st["h"]
            c0 = st["g"] * 512
            par = j % NQA; sb = j % NSP; gb = j % 2; ab = j % NSP
            if st["first"]:
                P.op("pe", lambda e: e.matmul(ops_[gp][:, :], self.zeros_b[:, 0:128], self.zeros_b[:, 0:512], start=True, stop=True),
                     writes=[b_o[gp]])
            P.op("pe", lambda e: e.matmul(gps[gb][:, n0:512], ka[hp][:, kb * 128:(kb + 1) * 128],
                                           qa[hp][par][:, c0 + n0:c0 + 512], start=True, stop=False),
                 reads=[b_ka[hp], b_qa[hp][par]], writes=[b_g[gb]], signal=False)
            P.op("pe", lambda e: e.matmul(gps[gb][:, n0:512], self.tri_neg[:], sp[sb][:, n0:512], start=False, stop=True),
                 reads=[b_sp[sb]], writes=[b_g[gb]])

        def emit_a3(j):
            st = steps[j]
            n0 = st["n0"]
            gb = j % 2; ab = j % NSP
            P.op("act", lambda e: e.activation(out=At[ab][:, n0:512], in_=gps[gb][:, n0:512], func=AF.Exp),
                 reads=[b_g[gb]], writes=[b_A[ab]])
            if st["diag"]:
                P.op("dve", lambda e: e.tensor_tensor(out=At[ab][:, n0:n0 + 128], in0=At[ab][:, n0:n0 + 128],
                                                       in1=self.mask_st[:], op=ALU.mult), writes=[b_A[ab]])

        def emit_p4(j):
            st = steps[j]
            kb, n0, gp, h, b = st["kb"], st["n0"], st["gp"], st["h"], st["b"]
            ab = j % NSP
            if st["newhead"] and h == 0:
                for q0 in range(0, NB, 8):
                    P.dma("sp", vall[:, q0:q0 + 8, :],
                          self.sv_tm[b * S + q0 * 128:b * S + (q0 + 8) * 128, :].rearrange("(n p) d -> p n d", p=128),
                          b_v, writes=[b_v])
            P.op("pe", lambda e: e.matmul(ops_[gp][:, n0:512], vall[:, kb, (h // 2) * 128:(h // 2 + 1) * 128], At[ab][:, n0:512],
                                           start=False, stop=True, skip_group_check=True),
                 reads=[b_v, b_A[ab]], writes=[b_o[gp]])
            if st["last"]:
                k = ost_k[0] % 2; ost_k[0] += 1
                c0 = b * S + st["g"] * 512
                P.op("dve", lambda e: e.tensor_copy(out=ost[k][:], in_=ops_[gp][(h % 2) * 64:(h % 2) * 64 + 64, :]), reads=[b_o[gp]], writes=[b_ost[k]])
                P.dma("sp", self.oT[DH + h * 64:DH + (h + 1) * 64, c0:c0 + 512], ost[k][:], b_ost[k], reads=[b_ost[k]])

        n = len(steps)
        ok = lambda m: 0 <= m < n
        per_head = n // (cfg.nseq * NH_S)
        LOOKH = max(3, min(110, per_head - 8))
        for j in range(0, min(LOOKH, n)):
            if steps[j]["newhead"]:
                load_head(steps[j])
        for j in range(-3, n + 1):
            if ok(j + LOOKH) and steps[j + LOOKH]["newhead"] and j + LOOKH >= LOOKH:
                load_head(steps[j + LOOKH])
            if ok(j + 3):
                emit_p1(j + 3)
            if ok(j + 1):
                emit_carry(j + 1)
            if ok(j):
                emit_main(j)
            if ok(j - 1):
                emit_p4(j - 1)
            if ok(j + 2):
                emit_a12(j + 2)
            if ok(j):
                emit_a3(j)
        P.run()

    def phase_hgrn(self, l):
        nc, cfg = self.nc, self.cfg
        P = Prog(nc, f"h{l}")
        S = cfg.S
        ns = cfg.nseq
        ntile = S // 128
        R = 31
        NS_ = 4 * ns
        gbc = P.sbuf("gbc", [128, DH], F32); b_gbc = Buf()
        P.dma("sp", gbc[:], bcast_rows(self.hgain[l, :], 128), b_gbc, writes=[b_gbc])
        NT_ = 2 * ns
        def many(name, n, shape, dt):
            return [P.sbuf(f"{name}{i}", shape, dt) for i in range(n)], [Buf() for _ in range(n)]
        tf, b_tf = many("tf", NT_, [128, 4, 128], F32)
        qs, b_qs = many("qs", NT_, [128, 4, 128], F32)
        vt, b_vt = many("vt", NT_, [128, DH], BF16)
        gg, b_gg = many("gg", NT_, [128, DH], F32)
        ss, b_ss = many("ss", NT_, [128, 4], F32)
        on, b_on = many("on", NT_, [128, DH], F32)
        onb, b_onb = many("onb", NT_, [128, DH], BF16)
        oTst, b_oTst = many("oTst", NT_, [128, 4, 128], BF16)
        ff, b_ff = many("ff", NS_, [128, 128], F32)
        lf, b_lf = many("lf", NS_, [128, 128], F32)
        kk, b_kk = many("kk", NS_, [128, 128], F32)
        bb, b_bb = many("bb", NS_, [128, 128], F32)
        eq, b_eq = many("eq", NS_, [128, 128], F32)
        ek, b_ek = many("ek", NS_, [128, 128], F32)
        sc_, b_sc = many("sc", NS_, [128, 8], F32)
        qzp, b_qzp = many("qzp", NS_, [128, 384], BF16)
        kt, b_kt = many("kt", NS_, [128, 128], BF16)
        ktok, b_ktok = many("ktok", NS_, [128, 128], BF16)
        pT, b_pT = many("pT", NS_, [128, 128], BF16)
        sq_, b_sq = many("sqj", NS_, [128, 128], F32)
        tmp, b_tmp = many("tmp", NS_, [128, 128], F32)
        St, b_St = many("St", NS_, [128, 128], F32)
        Sb, b_Sb = many("Sb", NS_, [128, 128], BF16)
        osb, b_osb = many("osb", NS_, [128, 128], F32)
        ones = self.ones_f
        _pkt = P.psum("pkt", [128, 1024], BF16); b_pkt = Buf(psum=True)
        p_kt = _pkt[:, 0:128]
        _psc = [P.psum(f"psc{i}", [128, 512]) for i in range(2)]; b_psc = [Buf(psum=True), Buf(psum=True)]
        p_sc = [t[:, 0:128] for t in _psc]
        _po = [P.psum(f"po{i}", [128, 512]) for i in range(2)]; b_po = [Buf(psum=True), Buf(psum=True)]
        p_o = [t[:, 0:128] for t in _po]
        _pkv = [P.psum(f"pkv{i}", [128, 512]) for i in range(2)]; b_pkv = [Buf(psum=True), Buf(psum=True)]
        p_kv = [t[:, 0:128] for t in _pkv]
        _pot = P.psum("pot", [128, 1024], BF16); b_pot = Buf(psum=True)
        p_ot = _pot[:, 0:512].rearrange("p (h t) -> p h t", t=128)
        for i in range(NS_):
            P.op("pool", lambda e, i=i: e.memset(qzp[i][:], 0.0), writes=[b_qzp[i]])
            P.op("pool", lambda e, i=i: e.memset(St[i][:], 0.0), writes=[b_St[i]])

        def head_stream(b, h, ti, tk):
            sk = b * 4 + h
            pp = sk % 2
            A_ = self.lbA[:, l * 4 + h:l * 4 + h + 1]
            B_ = self.lbB[:, l * 4 + h:l * 4 + h + 1]
            P.op("dve", lambda e: e.tensor_scalar(out=ff[sk][:], in0=tf[tk][:, h, :], scalar1=A_, scalar2=B_,
                                                   op0=ALU.mult, op1=ALU.add), reads=[b_tf[tk]], writes=[b_ff[sk]])
            yield
            P.op("act", lambda e: e.activation(out=lf[sk][:], in_=ff[sk][:], func=AF.Ln), reads=[b_ff[sk]], writes=[b_lf[sk]])
            P.op("pool", lambda e: e.tensor_scalar(out=kk[sk][:], in0=ff[sk][:], scalar1=-1.0, scalar2=1.0,
                                                    op0=ALU.mult, op1=ALU.add), reads=[b_ff[sk]], writes=[b_kk[sk]])
            yield
            for c in range(2):
                P.op("dve", lambda e, c=c: e.tensor_tensor_scan(
                    out=bb[sk][:, c * 64:(c + 1) * 64], data0=ones[:, 0:64], data1=lf[sk][:, c * 64:(c + 1) * 64],
                    initial=0.0, op0=ALU.mult, op1=ALU.add), reads=[b_lf[sk]], writes=[b_bb[sk]])
            yield
            for c in range(2):
                P.op("dve", lambda e, c=c: e.tensor_scalar(
                    out=sc_[sk][:, 3 * c:3 * c + 1], in0=bb[sk][:, c * 64 + R:c * 64 + R + 1], scalar1=-1.0, scalar2=None,
                    op0=ALU.mult), reads=[b_bb[sk]], writes=[b_sc[sk]])
            yield
            for c in range(2):
                br = bb[sk][:, c * 64 + R:c * 64 + R + 1]
                nbr = sc_[sk][:, 3 * c:3 * c + 1]
                P.op("act", lambda e, c=c, nbr=nbr: e.activation(
                    out=eq[sk][:, c * 64:(c + 1) * 64], in_=bb[sk][:, c * 64:(c + 1) * 64], func=AF.Exp, bias=nbr, scale=1.0),
                    reads=[b_bb[sk], b_sc[sk]], writes=[b_eq[sk]])
                P.op("act", lambda e, c=c, br=br: e.activation(
                    out=ek[sk][:, c * 64:(c + 1) * 64], in_=bb[sk][:, c * 64:(c + 1) * 64], func=AF.Exp, bias=br, scale=-1.0),
                    reads=[b_bb[sk]], writes=[b_ek[sk]])
                P.op("act", lambda e, c=c, br=br: e.activation(
                    out=sc_[sk][:, 3 * c + 1:3 * c + 2], in_=br, func=AF.Exp), reads=[b_bb[sk]], writes=[b_sc[sk]])
                P.op("act", lambda e, c=c: e.activation(
                    out=sc_[sk][:, 3 * c + 2:3 * c + 3], in_=bb[sk][:, c * 64 + 63:c * 64 + 64], func=AF.Exp),
                    reads=[b_bb[sk]], writes=[b_sc[sk]])
            yield
            P.op("dve", lambda e: e.tensor_tensor(
                out=qzp[sk][:].rearrange("p (c x) -> p c x", x=192)[:, :, 0:64],
                in0=qs[tk][:, h, :].rearrange("p (c x) -> p c x", x=64),
                in1=eq[sk][:].rearrange("p (c x) -> p c x", x=64), op=ALU.mult),
                reads=[b_qs[tk], b_eq[sk]], writes=[b_qzp[sk]])
            P.op("pool", lambda e: e.tensor_tensor(out=kt[sk][:], in0=kk[sk][:], in1=ek[sk][:], op=ALU.mult),
                 reads=[b_kk[sk], b_ek[sk]], writes=[b_kt[sk]])
            yield
            P.op("pe", lambda e: e.transpose(p_kt, kt[sk][:], self.identb[:]), reads=[b_kt[sk]], writes=[b_pkt])
            P.op("act", lambda e: e.activation(out=ktok[sk][:], in_=p_kt, func=AF.Copy), reads=[b_pkt], writes=[b_ktok[sk]])
            P.op("pe", lambda e: e.matmul(
                p_sc[pp].rearrange("p (c x) -> p c x", x=64), kt[sk][:],
                qzp[sk][:].rearrange("p (c x) -> p c x", x=192)[:, :, 0:64], start=True, stop=True),
                reads=[b_kt[sk], b_qzp[sk]], writes=[b_psc[pp]])
            P.op("dve", lambda e: e.tensor_scalar(out=sq_[sk][:], in0=p_sc[pp], scalar1=1e30, scalar2=-1e30,
                                                   op0=ALU.min, op1=ALU.max), reads=[b_psc[pp]], writes=[b_sq[sk]])
            yield
            P.op("dve", lambda e: e.tensor_tensor(out=pT[sk][:], in0=sq_[sk][:], in1=self.mask_bd[:], op=ALU.mult),
                 reads=[b_sq[sk]], writes=[b_pT[sk]])
            P.op("dve", lambda e: e.tensor_scalar(out=Sb[sk][:], in0=St[sk][:], scalar1=sc_[sk][:, 1:2], scalar2=None,
                                                   op0=ALU.mult), reads=[b_St[sk], b_sc[sk]], writes=[b_Sb[sk]])
            yield
            for c in range(2):
                if c == 0:
                    P.op("pe", lambda e: e.matmul(p_o[pp], pT[sk][:], vt[tk][:, h * 128:(h + 1) * 128], start=True, stop=False),
                         reads=[b_pT[sk], b_vt[tk]], writes=[b_po[pp]], signal=False)
                P.op("pe", lambda e, c=c: e.matmul(
                    p_o[pp], qzp[sk][:, c * 128:(c + 1) * 128], Sb[sk][:], start=(c == 1), stop=True),
                    reads=[b_qzp[sk], b_Sb[sk]], writes=[b_po[pp]])
                P.op("pe", lambda e, c=c: e.matmul(
                    p_kv[pp], ktok[sk][c * 64:(c + 1) * 64, :], vt[tk][c * 64:(c + 1) * 64, h * 128:(h + 1) * 128],
                    start=True, stop=True), reads=[b_ktok[sk], b_vt[tk]], writes=[b_pkv[pp]])
                if c == 0:
                    P.op("act", lambda e: e.activation(out=osb[sk][:], in_=p_o[pp], func=AF.Copy),
                         reads=[b_po[pp]], writes=[b_osb[sk]])
                else:
                    P.op("dve", lambda e: e.tensor_tensor(out=osb[sk][:], in0=p_o[pp], in1=osb[sk][:], op=ALU.add),
                         reads=[b_po[pp]], writes=[b_osb[sk]])
                P.op("dve", lambda e, c=c: e.tensor_scalar(
                    out=tmp[sk][:], in0=p_kv[pp], scalar1=eq[sk][:, c * 64 + 63:c * 64 + 64], scalar2=None, op0=ALU.mult),
                    reads=[b_pkv[pp], b_eq[sk]], writes=[b_tmp[sk]])
                P.op("dve", lambda e, c=c: e.scalar_tensor_tensor(
                    out=St[sk][:], in0=St[sk][:], scalar=sc_[sk][:, 3 * c + 2:3 * c + 3], in1=tmp[sk][:],
                    op0=ALU.mult, op1=ALU.add), reads=[b_tmp[sk], b_sc[sk]], writes=[b_St[sk]])
                if c == 0:
                    P.op("dve", lambda e: e.tensor_scalar(out=Sb[sk][:], in0=St[sk][:], scalar1=sc_[sk][:, 4:5], scalar2=None,
                                                           op0=ALU.mult), reads=[b_St[sk], b_sc[sk]], writes=[b_Sb[sk]])
                yield
            P.op("act", lambda e: e.activation(out=sq_[sk][:], in_=osb[sk][:], func=AF.Square, accum_out=ss[tk][:, h:h + 1]),
                 reads=[b_osb[sk]], writes=[b_sq[sk], b_ss[tk]])
            P.op("pool", lambda e: e.tensor_tensor(out=on[tk][:, h * 128:(h + 1) * 128], in0=osb[sk][:],
                                                    in1=gg[tk][:, h * 128:(h + 1) * 128], op=ALU.mult),
                 reads=[b_osb[sk], b_gg[tk]], writes=[b_on[tk]])
            yield

        def tile_tail(b, ti, tk):
            r0 = b * S + ti * 128
            P.op("dve", lambda e: e.tensor_scalar(out=ss[tk][:], in0=ss[tk][:], scalar1=1.0 / 128.0, scalar2=RMS_EPS,
                                                   op0=ALU.mult, op1=ALU.add), reads=[b_ss[tk]], writes=[b_ss[tk]])
            P.op("act", lambda e: e.activation(out=ss[tk][:], in_=ss[tk][:], func=AF.Ln), reads=[b_ss[tk]], writes=[b_ss[tk]])
            P.op("act", lambda e: e.activation(out=ss[tk][:], in_=ss[tk][:], func=AF.Exp, scale=-0.5),
                 reads=[b_ss[tk]], writes=[b_ss[tk]])
            for h in range(4):
                P.op("pool", lambda e, h=h: e.tensor_scalar(
                    out=onb[tk][:, h * 128:(h + 1) * 128], in0=on[tk][:, h * 128:(h + 1) * 128], scalar1=ss[tk][:, h:h + 1],
                    scalar2=None, op0=ALU.mult), reads=[b_ss[tk], b_on[tk]], writes=[b_onb[tk]])
            for h in range(4):
                P.op("pe", lambda e, h=h: e.transpose(p_ot[:, h, :], onb[tk][:, h * 128:(h + 1) * 128], self.identb[:]),
                     reads=[b_onb[tk]], writes=[b_pot], signal=(h == 3))
            P.op("act", lambda e: e.activation(out=oTst[tk][:], in_=p_ot, func=AF.Copy), reads=[b_pot], writes=[b_oTst[tk]])
            P.dma("sp", self.oT[0:DH, r0:r0 + 128].rearrange("(h p) t -> p h t", p=128), oTst[tk][:], b_oTst[tk],
                  reads=[b_oTst[tk]])

        def tile_loads(b, ti, tk):
            r0 = b * S + ti * 128
            P.dma("sp", tf[tk][:], self.hfT[:, r0:r0 + 128].rearrange("(h p) t -> p h t", p=128), b_tf[tk], writes=[b_tf[tk]])
            P.dma("sp", qs[tk][:], self.hqT[:, r0:r0 + 128].rearrange("(h p) t -> p h t", p=128), b_qs[tk], writes=[b_qs[tk]])
            P.dma("sp", vt[tk][:], self.hi_tm[r0:r0 + 128, :], b_vt[tk], writes=[b_vt[tk]])
            P.dma("sp", gg[tk][:], self.hg_tm[r0:r0 + 128, :], b_gg[tk], writes=[b_gg[tk]])
            P.op("pool", lambda e: e.tensor_tensor(out=gg[tk][:], in0=gg[tk][:], in1=gbc[:], op=ALU.mult),
                 reads=[b_gbc], writes=[b_gg[tk]])

        for b in range(ns):
            tile_loads(b, 0, b)
        for ti in range(ntile):
            tks = [(ti % 2) * ns + b for b in range(ns)]
            if ti + 1 < ntile:
                for b in range(ns):
                    tile_loads(b, ti + 1, ((ti + 1) % 2) * ns + b)
            gens = [head_stream(b, h, ti, tks[b]) for h in range(4) for b in range(ns)]
            alive = list(gens)
            while alive:
                nxt = []
                for g in alive:
                    try:
                        next(g)
                        nxt.append(g)
                    except StopIteration:
                        pass
                alive = nxt
            for b in range(ns):
                tile_tail(b, ti, tks[b])
        P.run()

    def rstd_from_var(self, P, mv, b_mv, rstd, b_rstd, n):
        P.op("dve", lambda e: e.tensor_scalar(out=rstd[:, 0:n], in0=mv[:, 0:n, 1], scalar1=LN_EPS, scalar2=None,
                                               op0=ALU.add), reads=[b_mv], writes=[b_rstd])
        P.op("act", lambda e: e.activation(out=rstd[:, 0:n], in_=rstd[:, 0:n], func=AF.Ln), reads=[b_rstd], writes=[b_rstd])
        P.op("act", lambda e: e.activation(out=rstd[:, 0:n], in_=rstd[:, 0:n], func=AF.Exp, scale=-0.5),
             reads=[b_rstd], writes=[b_rstd])

    def router_top2(self, P, lps, b_lps, rt, b_rt, comb, b_c, i):
        L, m1, k1, L2, m2, k2, dd, p1 = (rt[:, q, :] for q in range(8))
        P.op("dve", lambda e: e.tensor_copy(out=L, in_=lps[:, 0:NE]), reads=[b_lps], writes=[b_rt])
        P.op("dve", lambda e: e.tensor_reduce(out=m1[:, 0:1], in_=L, axis=AX.X, op=ALU.max), reads=[b_rt], writes=[b_rt])
        P.op("dve", lambda e: e.tensor_scalar(out=k1, in0=L, scalar1=m1[:, 0:1], scalar2=None, op0=ALU.is_equal),
             reads=[b_rt], writes=[b_rt])
        P.op("dve", lambda e: e.scalar_tensor_tensor(out=L2, in0=k1, scalar=-1e30, in1=L, op0=ALU.mult, op1=ALU.add),
             reads=[b_rt], writes=[b_rt])
        P.op("dve", lambda e: e.tensor_reduce(out=m2[:, 0:1], in_=L2, axis=AX.X, op=ALU.max), reads=[b_rt], writes=[b_rt])
        P.op("dve", lambda e: e.tensor_scalar(out=k2, in0=L2, scalar1=m2[:, 0:1], scalar2=None, op0=ALU.is_equal),
             reads=[b_rt], writes=[b_rt])
        P.op("dve", lambda e: e.tensor_tensor(out=dd[:, 0:1], in0=m2[:, 0:1], in1=m1[:, 0:1], op=ALU.subtract),
             reads=[b_rt], writes=[b_rt])
        P.op("act", lambda e: e.activation(out=dd[:, 1:2], in_=dd[:, 0:1], func=AF.Exp), reads=[b_rt], writes=[b_rt])
        P.op("dve", lambda e: e.tensor_scalar(out=dd[:, 2:3], in0=dd[:, 1:2], scalar1=1.0, scalar2=None, op0=ALU.add),
             reads=[b_rt], writes=[b_rt])
        P.op("dve", lambda e: e.reciprocal(out=p1[:, 0:1], in_=dd[:, 2:3]), reads=[b_rt], writes=[b_rt])
        P.op("dve", lambda e: e.tensor_scalar(out=p1[:, 1:2], in0=p1[:, 0:1], scalar1=-1.0, scalar2=1.0,
                                               op0=ALU.mult, op1=ALU.add), reads=[b_rt], writes=[b_rt])
        P.op("dve", lambda e: e.tensor_scalar(out=comb[:, i, :], in0=k1, scalar1=p1[:, 0:1], scalar2=None, op0=ALU.mult),
             reads=[b_rt], writes=[b_c])
        P.op("dve", lambda e: e.scalar_tensor_tensor(out=comb[:, i, :], in0=k2, scalar=p1[:, 1:2], in1=comb[:, i, :],
                                                      op0=ALU.mult, op1=ALU.add), reads=[b_rt], writes=[b_c])


def const_tables():
    s = np.arange(128)[:, None]
    t = np.arange(128)[None, :]
    c = np.zeros((128, 5, 128), np.float32)
    c[:, 0, :] = (s == t)
    c[:, 1, :] = ((s // CHUNK) == (t // CHUNK)) & (s <= t)
    c[:, 2, :] = (s < t)
    c[:, 3, :] = -1.0 * (s >= t)
    c[:, 4, :] = -1.0 * (t == 64)
    return c


def core_inputs(inp, b0, nseq, S):
    f = lambda a: np.ascontiguousarray(np.asarray(a, dtype=np.float32))
    x = f(inp["x"])[b0:b0 + nseq, :S].reshape(nseq * S, D)
    c = f(inp["c"])[b0:b0 + nseq]
    cT = np.ascontiguousarray(c.T.reshape(8, 128, nseq).transpose(1, 0, 2))
    lb = f(inp["hgrn_lb_logits"])
    lbT = np.ascontiguousarray(lb.T.reshape(NH_H, 128, DEPTH).transpose(1, 0, 2))
    m = {
        "x": np.ascontiguousarray(x), "cT": cT, "lbT": lbT, "consts": const_tables(),
        "w_ada": f(inp["w_ada"]), "b_ada": f(inp["b_ada"]), "w_in": f(inp["w_in"]), "w_out": f(inp["w_out"]),
        "hgain": f(inp["hgrn_norm_gain"]),
        "w_dense_gate": f(inp["w_dense_gate"]), "w_dense_up": f(inp["w_dense_up"]),
        "w_dense_down": f(inp["w_dense_down"]),
        "w_router": np.ascontiguousarray(f(inp["w_router"]).reshape(2, 8, 128, NE).transpose(0, 2, 1, 3)),
        "w_moe_gate": f(inp["w_moe_gate"]), "w_moe_up": f(inp["w_moe_up"]), "w_moe_down": f(inp["w_moe_down"]),
        "ln_gain": f(inp["ln_gain"]), "ln_bias": f(inp["ln_bias"]),
    }
    return m


def kernel(**inputs):
    cfg = Cfg()
    nc = Builder(cfg).build()
    in_maps = [core_inputs(inputs, 2 * c, 2, 4096) for c in range(NCORES)]
    res = run_bass_kernel_spmd(nc, in_maps, core_ids=list(range(NCORES)))
    outs = [np.asarray(r["out"], dtype=np.float32).reshape(2, 4096, D) for r in res.results]
    return np.concatenate(outs, axis=0)
```

```python
import contextlib
import numpy as np
import ml_dtypes
import concourse.bass as bass
import concourse.mybir as mybir
from concourse.bass_utils import run_bass_kernel_spmd

F32 = mybir.dt.float32
BF16 = mybir.dt.bfloat16
AF = mybir.ActivationFunctionType
ALU = mybir.AluOpType
AX = mybir.AxisListType

D = 1024
DEPTH = 4
DH = 512
NH_H = 4
DSB = 512
NH_S = 8
DIN = 3584
FF_DENSE = 2816
FF_MOE = 3584
NE = 8
ALPHA = float((2 * DEPTH) ** 0.25)
LN_EPS = 1e-5
RMS_EPS = 1e-6
CHUNK = 64
NCORES = 8
DBG = {}


class Buf:
    __slots__ = ("name", "w", "r", "dsem", "dcnt", "psum")

    def __init__(self, name="", psum=False):
        self.name = name
        self.psum = psum
        self.w = None
        self.r = []
        self.dsem = None
        self.dcnt = 0


class _Eng:
    def __init__(self, name, sem):
        self.name = name
        self.sem = sem
        self.count = 0
        self.ops = []
        self.waited = {}


class Prog:
    ENG = ("pe", "act", "dve", "pool", "sp")

    def __init__(self, nc, name):
        self.nc = nc
        self.name = name
        self.stack = contextlib.ExitStack()
        self.eng = {}
        self.sems = []
        for e in self.ENG:
            sem = nc.alloc_semaphore(name=f"{name}_{e}")
            self.sems.append(sem)
            self.eng[e] = _Eng(e, sem)
        self.dma_toks = []
        self.nsem = 5

    def sbuf(self, name, shape, dt):
        return self.stack.enter_context(self.nc.sbuf_tensor(f"{self.name}_{name}", list(shape), dt))

    def psum(self, name, shape, dt=F32):
        return self.stack.enter_context(self.nc.psum_tensor(f"{self.name}_{name}", list(shape), dt))

    def _wait(self, e, tok):
        if tok is None:
            return
        sem, val, owner = tok
        if owner == "pe" and e.name == "pe":
            return
        key = id(sem)
        if e.waited.get(key, 0) >= val:
            return
        e.waited[key] = val
        e.ops.append(lambda eng, s=sem, v=val: eng.wait_ge(s, v))

    def _deps(self, e, reads, writes, extra):
        for b in reads:
            self._wait(e, b.w)
            if b.psum:
                for t in b.r:
                    if t[2] != e.name:
                        self._wait(e, t)
        for b in writes:
            self._wait(e, b.w)
            for t in b.r:
                self._wait(e, t)
        for t in extra:
            self._wait(e, t)

    def op(self, engine, fn, reads=(), writes=(), extra=(), signal=True):
        e = self.eng[engine]
        self._deps(e, reads, writes, extra)
        if signal:
            e.count += 1
            tok = (e.sem, e.count, engine)
            e.ops.append(lambda eng, f=fn, s=e.sem: f(eng).then_inc(s, 1))
        else:
            tok = (e.sem, e.count + 1, engine)
            e.ops.append(lambda eng, f=fn: f(eng))
        for b in writes:
            b.w = tok
            b.r = []
        for b in reads:
            b.r.append(tok)
        return tok

    def dma(self, queue, out, in_, owner, reads=(), writes=(), extra=(), **kw):
        e = self.eng[queue]
        self._deps(e, reads, writes, extra)
        if owner.dsem is None:
            owner.dsem = self.nc.alloc_semaphore(name=f"{self.name}_d{self.nsem}")
            self.sems.append(owner.dsem)
            self.nsem += 1
            owner.dcnt = 0
        owner.dcnt += 16
        tok = (owner.dsem, owner.dcnt, "dma")
        e.ops.append(lambda eng, o=out, i=in_, s=owner.dsem, k=kw: eng.dma_start(out=o, in_=i, **k).then_inc(s, 16))
        for b in writes:
            b.w = tok
            b.r = []
        for b in reads:
            b.r.append(tok)
        self.dma_toks.append(tok)
        return tok

    def run(self):
        nc = self.nc
        sp = self.eng["sp"]
        for t in self.dma_toks:
            self._wait(sp, t)
        with nc.Block() as block:
            @block.tensor
            def _(t):
                for f in self.eng["pe"].ops:
                    f(t)

            @block.scalar
            def _(a):
                for f in self.eng["act"].ops:
                    f(a)

            @block.vector
            def _(v):
                for f in self.eng["dve"].ops:
                    f(v)

            @block.gpsimd
            def _(g):
                for f in self.eng["pool"].ops:
                    f(g)

            @block.sync
            def _(s):
                for f in self.eng["sp"].ops:
                    f(s)
        nc.clear_and_free_semaphores(self.sems)
        nc.all_engine_barrier()
        self.stack.close()


class Cfg:
    def __init__(self, nseq=2, S=4096, layers=(0, 1, 2, 3), debug=()):
        self.nseq = nseq
        self.S = S
        self.ntok = nseq * S
        self.layers = tuple(layers)
        self.debug = tuple(debug)


def bcast_rows(ap1d, nparts):
    return ap1d.partition_broadcast(nparts)


class Builder:
    def __init__(self, cfg):
        self.cfg = cfg
        nc = self.nc = bass.Bass("TRN2", target_bir_lowering=False)
        NT = cfg.ntok
        ns = cfg.nseq

        def din(name, shape, dt=F32):
            return nc.dram_tensor(name, list(shape), dt, kind="ExternalInput").ap()

        def scratch(name, shape, dt=F32):
            kind = "ExternalOutput" if name in cfg.debug else "Internal"
            return nc.dram_tensor(name, list(shape), dt, kind=kind).ap()

        self.x_in = din("x", [NT, D])
        self.cT = din("cT", [128, 8, ns])
        self.w_ada = din("w_ada", [DEPTH, 2, D, 3 * D])
        self.b_ada = din("b_ada", [DEPTH, 2, 3 * D])
        self.w_in = din("w_in", [DEPTH, D, DIN])
        self.w_out = din("w_out", [DEPTH, D, D])
        self.lbT = din("lbT", [128, NH_H, DEPTH])
        self.hgain = din("hgain", [DEPTH, DH])
        self.wdg = din("w_dense_gate", [2, D, FF_DENSE])
        self.wdu = din("w_dense_up", [2, D, FF_DENSE])
        self.wdd = din("w_dense_down", [2, FF_DENSE, D])
        self.w_router = din("w_router", [2, 128, 8, NE])
        self.wmg = din("w_moe_gate", [2, NE, D, FF_MOE])
        self.wmu = din("w_moe_up", [2, NE, D, FF_MOE])
        self.wmd = din("w_moe_down", [2, NE, FF_MOE, D])
        self.ln_gain = din("ln_gain", [DEPTH, 2, D])
        self.ln_bias = din("ln_bias", [DEPTH, 2, D])
        self.consts = din("consts", [128, 5, 128])

        self.out = nc.dram_tensor("out", [NT, D], F32, kind="ExternalOutput").ap()
        self.xa = scratch("xa", [NT, D])
        self.xb = scratch("xb", [NT, D])
        self.mod = scratch("mod", [ns, 8, 3 * D])
        self.hqT = scratch("hqT", [DH, NT])
        self.hfT = scratch("hfT", [DH, NT])
        self.hi_tm = scratch("hi_tm", [NT, DH], BF16)
        self.hg_tm = scratch("hg_tm", [NT, DH])
        self.sqT = scratch("sqT", [DSB, NT], BF16)
        self.skT = scratch("skT", [DSB, NT], BF16)
        self.sv_tm = scratch("sv_tm", [NT, DSB], BF16)
        self.oT = scratch("oT", [D, NT], BF16)

    def dump(self, P, name, ap, buf, shape, dt=F32):
        if name not in self.cfg.debug:
            return
        t = self.nc.dram_tensor(name, list(shape), dt, kind="ExternalOutput").ap()
        P.dma("sp", t, ap, buf, reads=[buf])

    def build(self):
        nc = self.nc
        cfg = self.cfg
        with contextlib.ExitStack() as top:
            def pt(name, shape, dt):
                return top.enter_context(nc.sbuf_tensor(name, list(shape), dt))
            self.identf = pt("identf", [128, 128], F32)
            self.identb = pt("identb", [128, 128], BF16)
            self.mask_bd = pt("mask_bd", [128, 128], F32)
            self.mask_st = pt("mask_st", [128, 128], F32)
            self.tri_neg = pt("tri_neg", [128, 128], BF16)
            self.neg_col = pt("neg_col", [128, 128], BF16)
            self.lbA = pt("lbA", [128, NH_H * DEPTH], F32)
            self.lbB = pt("lbB", [128, NH_H * DEPTH], F32)
            self.ones_f = pt("ones_f", [128, 128], F32)
            self.zeros_b = pt("zeros_b", [128, 512], BF16)
            self.ones_b = pt("ones_b", [128, 128], BF16)

            self.phase_setup()
            xcur = self.x_in
            for li, l in enumerate(cfg.layers):
                last = li == len(cfg.layers) - 1
                self.phase_inproj(l, xcur)
                self.phase_hgrn(l)
                self.phase_sb(l)
                self.phase_outproj(l, xcur, self.xa)
                self.phase_ffn(l, self.xa, self.out if last else self.xb)
                xcur = self.xb
        return nc

    def phase_setup(self):
        nc, cfg = self.nc, self.cfg
        P = Prog(nc, "p0")
        ns = cfg.nseq
        cst = P.sbuf("cst", [128, 5, 128], F32)
        b_cst = Buf("cst")
        P.dma("sp", cst[:], self.consts, b_cst, writes=[b_cst])
        bp = Buf("persist")
        P.op("dve", lambda e: e.tensor_copy(out=self.identf[:], in_=cst[:, 0, :]), reads=[b_cst], writes=[bp])
        P.op("dve", lambda e: e.tensor_copy(out=self.identb[:], in_=cst[:, 0, :]), reads=[b_cst], writes=[bp])
        P.op("dve", lambda e: e.tensor_copy(out=self.mask_bd[:], in_=cst[:, 1, :]), reads=[b_cst], writes=[bp])
        P.op("dve", lambda e: e.tensor_copy(out=self.mask_st[:], in_=cst[:, 2, :]), reads=[b_cst], writes=[bp])
        P.op("dve", lambda e: e.tensor_copy(out=self.tri_neg[:], in_=cst[:, 3, :]), reads=[b_cst], writes=[bp])
        P.op("dve", lambda e: e.tensor_copy(out=self.neg_col[:], in_=cst[:, 4, :]), reads=[b_cst], writes=[bp])
        P.op("dve", lambda e: e.memset(self.ones_f[:], 1.0), writes=[bp])
        P.op("dve", lambda e: e.memset(self.ones_b[:], 1.0), writes=[bp])
        P.op("dve", lambda e: e.memset(self.zeros_b[:], 0.0), writes=[bp])

        lg = P.sbuf("lg", [128, NH_H, DEPTH], F32)
        ex = P.sbuf("ex", [128, NH_H, DEPTH], F32)
        sm = P.sbuf("sm", [128, NH_H], F32)
        lbt = P.sbuf("lbt", [128, NH_H, DEPTH], F32)
        b_lg, b_ex, b_sm, b_lb = Buf(), Buf(), Buf(), Buf()
        P.dma("sp", lg[:], self.lbT, b_lg, writes=[b_lg])
        P.op("act", lambda e: e.activation(out=ex[:], in_=lg[:], func=AF.Exp), reads=[b_lg], writes=[b_ex])
        P.op("dve", lambda e: e.tensor_reduce(out=sm[:], in_=ex[:], axis=AX.X, op=ALU.add), reads=[b_ex], writes=[b_sm])
        P.op("dve", lambda e: e.reciprocal(out=sm[:], in_=sm[:]), reads=[b_sm], writes=[b_sm])
        for h in range(NH_H):
            P.op("dve", lambda e, h=h: e.tensor_scalar(out=ex[:, h, :], in0=ex[:, h, :], scalar1=sm[:, h:h + 1],
                                                        scalar2=None, op0=ALU.mult),
                 reads=[b_sm, b_ex], writes=[b_ex])
        P.op("dve", lambda e: e.memset(lbt[:, :, 0:1], 0.0), writes=[b_lb])
        for l in range(1, DEPTH):
            P.op("dve", lambda e, l=l: e.tensor_tensor(out=lbt[:, :, l:l + 1], in0=lbt[:, :, l - 1:l],
                                                        in1=ex[:, :, l:l + 1], op=ALU.add),
                 reads=[b_ex, b_lb], writes=[b_lb])
        for l in range(DEPTH):
            P.op("dve", lambda e, l=l: e.tensor_scalar(out=self.lbA[:, l * 4:(l + 1) * 4], in0=lbt[:, :, l],
                                                        scalar1=-0.5, scalar2=0.5, op0=ALU.mult, op1=ALU.add),
                 reads=[b_lb], writes=[bp])
            P.op("dve", lambda e, l=l: e.tensor_scalar(out=self.lbB[:, l * 4:(l + 1) * 4], in0=lbt[:, :, l],
                                                        scalar1=0.5, scalar2=0.5, op0=ALU.mult, op1=ALU.add),
                 reads=[b_lb], writes=[bp])

        ct = P.sbuf("ct", [128, 8, ns], F32)
        sct = P.sbuf("sct", [128, 8, ns], F32)
        b_ct, b_sct = Buf(), Buf()
        P.dma("sp", ct[:], self.cT, b_ct, writes=[b_ct])
        P.op("act", lambda e: e.activation(out=sct[:], in_=ct[:], func=AF.Exp, scale=-1.0), reads=[b_ct], writes=[b_sct])
        P.op("dve", lambda e: e.tensor_scalar(out=sct[:], in0=sct[:], scalar1=1.0, scalar2=None, op0=ALU.add),
             reads=[b_sct], writes=[b_sct])
        P.op("dve", lambda e: e.reciprocal(out=sct[:], in_=sct[:]), reads=[b_sct], writes=[b_sct])
        P.op("dve", lambda e: e.tensor_tensor(out=sct[:], in0=sct[:], in1=ct[:], op=ALU.mult),
             reads=[b_sct, b_ct], writes=[b_sct])
        wbuf = [P.sbuf(f"wa{i}", [128, 8, 512], F32) for i in range(2)]
        b_w = [Buf(), Buf()]
        bias = [P.sbuf(f"bias{i}", [ns, 3 * D], F32) for i in range(2)]
        b_bias = [Buf(), Buf()]
        mt = [P.sbuf(f"mt{i}", [ns, 3 * D], F32) for i in range(2)]
        b_mt = [Buf(), Buf()]
        ps = [P.psum(f"ps{i}", [128, 512]) for i in range(2)]
        b_ps = [Buf(psum=True), Buf(psum=True)]
        k = 0
        for ls in range(8):
            l, s = divmod(ls, 2)
            if l not in cfg.layers:
                continue
            bi = ls % 2
            P.dma("sp", bias[bi][:], bcast_rows(self.b_ada[l, s, :], ns), b_bias[bi], writes=[b_bias[bi]])
            for n in range(6):
                wi = k % 2
                k += 1
                src = self.w_ada[l, s, :, n * 512:(n + 1) * 512].rearrange("(c p) f -> p c f", p=128)
                P.dma("sp", wbuf[wi][:], src, b_w[wi], writes=[b_w[wi]])
                for dc in range(8):
                    P.op("pe", lambda e, wi=wi, dc=dc: e.matmul(ps[wi][0:ns, :], sct[:, dc, :], wbuf[wi][:, dc, :],
                                                                  start=(dc == 0), stop=(dc == 7)),
                         reads=[b_sct, b_w[wi]], writes=[b_ps[wi]], signal=(dc == 7))
                P.op("dve", lambda e, wi=wi, bi=bi, n=n: e.tensor_tensor(
                    out=mt[bi][:, n * 512:(n + 1) * 512], in0=ps[wi][0:ns, :], in1=bias[bi][:, n * 512:(n + 1) * 512],
                    op=ALU.add), reads=[b_ps[wi], b_bias[bi]], writes=[b_mt[bi]])
            P.op("dve", lambda e, bi=bi: e.tensor_scalar(out=mt[bi][:, D:2 * D], in0=mt[bi][:, D:2 * D], scalar1=1.0,
                                                          scalar2=None, op0=ALU.add), reads=[b_mt[bi]], writes=[b_mt[bi]])
            P.dma("sp", self.mod[:, ls, :], mt[bi][:], b_mt[bi], reads=[b_mt[bi]])
        P.run()

    def load_bcast(self, P, tile, buf, src1d):
        P.dma("sp", tile[:], bcast_rows(src1d, 128), buf, writes=[buf])

    def phase_ffn(self, l, x_src, x_dst):
        nc, cfg = self.nc, self.cfg
        moe = (l % 2 == 1)
        j = l // 2
        P = Prog(nc, f"f{l}")
        TB = 1024
        NTI = TB // 128
        nblk = cfg.ntok // TB
        blk_per_seq = cfg.S // TB
        FF = FF_MOE if moe else FF_DENSE
        nfc = FF // 128
        groups = [(g0, min(4, nfc - g0)) for g0 in range(0, nfc, 4)]
        nexp = NE if moe else 1
        nexp = DBG.get('nexp', nexp)

        sc1 = P.sbuf("sc1", [128, D], F32); sh = P.sbuf("sh", [128, D], F32); gt = P.sbuf("gt", [128, D], F32)
        lng = P.sbuf("lng", [128, D], F32); lnb = P.sbuf("lnb", [128, D], F32)
        b_mod, b_ln = Buf("mod"), Buf("ln")
        xt = [P.sbuf(f"xt{i}", [128, D], F32) for i in range(2)]
        b_xt = [Buf(), Buf()]
        hf = [P.sbuf(f"hf{i}", [128, D], F32) for i in range(2)]
        b_hf = [Buf(), Buf()]
        hT = P.sbuf("hT", [128, 8, TB], BF16)
        b_hT = [Buf() for _ in range(NTI)]
        yacc = P.sbuf("yacc", [128, NTI, D], F32)
        b_y = [[Buf(), Buf()] for _ in range(NTI)]
        wg = [P.sbuf(f"wg{i}", [128, 8, 512], BF16) for i in range(2)]
        wu = [P.sbuf(f"wu{i}", [128, 8, 512], BF16) for i in range(2)]
        wd = [P.sbuf(f"wd{i}", [128, 4, D], BF16) for i in range(2)]
        b_wgu = [Buf(), Buf()]
        b_wd = [Buf(), Buf()]
        aT = [P.sbuf(f"aT{i}", [128, 4, TB], BF16) for i in range(2)]
        b_aT = [[Buf(), Buf()] for _ in range(2)]
        sg = [P.sbuf(f"sg{i}", [128, 512], F32) for i in range(2)]
        b_sg = [Buf(), Buf()]
        ot = [P.sbuf(f"ot{i}", [128, D], F32) for i in range(2)]
        b_ot = [Buf(), Buf()]
        stats = P.sbuf("stats", [128, NTI, 2, 6], F32)
        mv = P.sbuf("mv", [128, NTI, 2], F32)
        rstd = P.sbuf("rstd", [128, NTI], F32)
        b_stats = [Buf() for _ in range(NTI)]
        b_mv = Buf(); b_rstd = Buf()
        if moe:
            hTf = P.sbuf("hTf", [128, 8, 128], F32); b_hTf = Buf()
            wr = P.sbuf("wr", [128, 8, NE], F32); b_wr = Buf()
            comb = P.sbuf("comb", [128, NTI, NE], F32); b_comb = [Buf() for _ in range(NTI)]
            rt = P.sbuf("rt", [128, 8, NE], F32); b_rt = Buf()
            if DBG.get("router", 1):
                P.dma("sp", wr[:], self.w_router[j], b_wr, writes=[b_wr])
        psT = [P.psum(f"psT{i}", [128, 512]) for i in range(2)]; b_psT = [Buf(psum=True), Buf(psum=True)]
        gps = [P.psum(f"gps{i}", [128, 512]) for i in range(2)]; b_gps = [Buf(psum=True), Buf(psum=True)]
        ups = [P.psum(f"ups{i}", [128, 512]) for i in range(2)]; b_ups = [Buf(psum=True), Buf(psum=True)]
        yps = [P.psum(f"yps{i}", [128, 512]) for i in range(2)]; b_yps = [Buf(psum=True), Buf(psum=True)]

        P.dma("sp", lng[:], bcast_rows(self.ln_gain[l, 1, :], 128), b_ln, writes=[b_ln])
        P.dma("sp", lnb[:], bcast_rows(self.ln_bias[l, 1, :], 128), b_ln, writes=[b_ln])

        if moe:
            WG, WU, WD = self.wmg[j], self.wmu[j], self.wmd[j]
        else:
            WG, WU, WD = self.wdg[j:j + 1], self.wdu[j:j + 1], self.wdd[j:j + 1]

        gcount = 0
        xk = 0
        for blk in range(nblk):
            t0 = blk * TB
            bseq = blk // blk_per_seq
            if blk % blk_per_seq == 0:
                ls = l * 2 + 1
                P.dma("sp", sh[:], bcast_rows(self.mod[bseq, ls, 0:D], 128), b_mod, writes=[b_mod])
                P.dma("sp", sc1[:], bcast_rows(self.mod[bseq, ls, D:2 * D], 128), b_mod, writes=[b_mod])
                P.dma("sp", gt[:], bcast_rows(self.mod[bseq, ls, 2 * D:3 * D], 128), b_mod, writes=[b_mod])
            for i in range(NTI):
                xi = xk % 2
                xk += 1
                r0 = t0 + i * 128
                P.dma("sp", xt[xi][:], x_src[r0:r0 + 128, :], b_xt[xi], writes=[b_xt[xi]])
                P.op("dve", lambda e, xi=xi: e.tensor_tensor(out=hf[xi][:], in0=xt[xi][:], in1=sc1[:], op=ALU.mult),
                     reads=[b_xt[xi], b_mod], writes=[b_hf[xi]])
                P.op("pool", lambda e, xi=xi: e.tensor_tensor(out=hf[xi][:], in0=hf[xi][:], in1=sh[:], op=ALU.add),
                     reads=[b_hf[xi], b_mod], writes=[b_hf[xi]])
                for hb in range(2):
                    for q in range(4):
                        dc = hb * 4 + q
                        P.op("pe", lambda e, xi=xi, hb=hb, q=q, dc=dc: e.transpose(
                            psT[hb][:, q * 128:(q + 1) * 128], hf[xi][:, dc * 128:(dc + 1) * 128], self.identf[:]),
                            reads=[b_hf[xi]], writes=[b_psT[hb]], signal=(q == 3))
                    if moe:
                        P.op("act", lambda e, hb=hb: e.activation(
                            out=hTf[:, hb * 4:(hb + 1) * 4, :], in_=psT[hb][:].rearrange("p (c t) -> p c t", t=128),
                            func=AF.Copy), reads=[b_psT[hb]], writes=[b_hTf])
                    P.op("act" if not moe else "dve", lambda e, hb=hb, i=i: e.tensor_copy(
                        out=hT[:, hb * 4:(hb + 1) * 4, i * 128:(i + 1) * 128],
                        in_=psT[hb][:].rearrange("p (c t) -> p c t", t=128)) if moe else e.activation(
                        out=hT[:, hb * 4:(hb + 1) * 4, i * 128:(i + 1) * 128],
                        in_=psT[hb][:].rearrange("p (c t) -> p c t", t=128), func=AF.Copy),
                        reads=[b_psT[hb]], writes=[b_hT[i]])
                if moe and DBG.get("router", 1) == 0:
                    P.op("dve", lambda e, i=i: e.memset(comb[:, i, :], 0.125), writes=[b_comb[i]])
                if moe and DBG.get("router", 1):
                    for dc in range(8):
                        P.op("pe", lambda e, dc=dc: e.matmul(yps[1][:, 0:NE], hTf[:, dc, :], wr[:, dc, :],
                                                              start=(dc == 0), stop=(dc == 7)),
                             reads=[b_hTf, b_wr], writes=[b_yps[1]], signal=(dc == 7))
                    self.router_top2(P, yps[1], b_yps[1], rt, b_rt, comb, b_comb[i], i)
            for ex in range(nexp):
                for (g0, ng) in groups:
                    slot = gcount % 2
                    gcount += 1
                    f0 = g0 * 128
                    fw = ng * 128
                    P.dma("pool", wg[slot][:, :, 0:fw], WG[ex, :, f0:f0 + fw].rearrange("(c p) f -> p c f", p=128),
                          b_wgu[slot], writes=[b_wgu[slot]])
                    P.dma("pool", wu[slot][:, :, 0:fw], WU[ex, :, f0:f0 + fw].rearrange("(c p) f -> p c f", p=128),
                          b_wgu[slot], writes=[b_wgu[slot]])
                    P.dma("pool", wd[slot][:, 0:ng, :], WD[ex, f0:f0 + fw, :].rearrange("(c p) d -> p c d", p=128),
                          b_wd[slot], writes=[b_wd[slot]])
                    for fc in range(ng):
                        for half in range(2):
                            pb = (fc * 2 + half) % 2
                            for dc in range(8):
                                P.op("pe", lambda e, slot=slot, fc=fc, half=half, dc=dc, pb=pb: e.matmul(
                                    gps[pb][:], wg[slot][:, dc, fc * 128:(fc + 1) * 128],
                                    hT[:, dc, half * 512:(half + 1) * 512], start=(dc == 0), stop=(dc == 7)),
                                    reads=[b_wgu[slot]] + b_hT[half * 4:(half + 1) * 4], writes=[b_gps[pb]],
                                    signal=(dc == 7))
                            for dc in range(8):
                                P.op("pe", lambda e, slot=slot, fc=fc, half=half, dc=dc, pb=pb: e.matmul(
                                    ups[pb][:], wu[slot][:, dc, fc * 128:(fc + 1) * 128],
                                    hT[:, dc, half * 512:(half + 1) * 512], start=(dc == 0), stop=(dc == 7)),
                                    reads=[b_wgu[slot]] + b_hT[half * 4:(half + 1) * 4], writes=[b_ups[pb]],
                                    signal=(dc == 7))
                            P.op("act", lambda e, pb=pb: e.activation(out=sg[pb][:], in_=gps[pb][:], func=AF.Silu),
                                 reads=[b_gps[pb]], writes=[b_sg[pb]])
                            P.op("dve", lambda e, pb=pb, slot=slot, fc=fc, half=half: e.tensor_tensor(
                                out=aT[slot][:, fc, half * 512:(half + 1) * 512], in0=ups[pb][:], in1=sg[pb][:],
                                op=ALU.mult), reads=[b_ups[pb], b_sg[pb]], writes=[b_aT[slot][half]])
                    first = (ex == 0 and g0 == 0)
                    for i in range(NTI):
                        for h2 in range(2):
                            pb = (i * 2 + h2) % 2
                            for fc in range(ng):
                                P.op("pe", lambda e, slot=slot, fc=fc, i=i, h2=h2, pb=pb, ng=ng: e.matmul(
                                    yps[pb][:], aT[slot][:, fc, i * 128:(i + 1) * 128],
                                    wd[slot][:, fc, h2 * 512:(h2 + 1) * 512], start=(fc == 0), stop=(fc == ng - 1)),
                                    reads=[b_aT[slot][i // 4], b_wd[slot]], writes=[b_yps[pb]], signal=(fc == ng - 1))
                            ysl = yacc[:, i, h2 * 512:(h2 + 1) * 512]
                            if moe:
                                csc = comb[:, i, ex:ex + 1]
                                if first:
                                    P.op("dve", lambda e, pb=pb, ysl=ysl, csc=csc: e.tensor_scalar(
                                        out=ysl, in0=yps[pb][:], scalar1=csc, scalar2=None, op0=ALU.mult),
                                        reads=[b_yps[pb], b_comb[i]], writes=[b_y[i][h2]])
                                else:
                                    P.op("dve", lambda e, pb=pb, ysl=ysl, csc=csc: e.scalar_tensor_tensor(
                                        out=ysl, in0=yps[pb][:], scalar=csc, in1=ysl, op0=ALU.mult, op1=ALU.add),
                                        reads=[b_yps[pb], b_comb[i]], writes=[b_y[i][h2]])
                            else:
                                if first:
                                    P.op("act", lambda e, pb=pb, ysl=ysl: e.activation(out=ysl, in_=yps[pb][:], func=AF.Copy),
                                         reads=[b_yps[pb]], writes=[b_y[i][h2]])
                                else:
                                    P.op("dve", lambda e, pb=pb, ysl=ysl: e.tensor_tensor(
                                        out=ysl, in0=yps[pb][:], in1=ysl, op=ALU.add),
                                        reads=[b_yps[pb]], writes=[b_y[i][h2]])
            for i in range(NTI):
                xi = xk % 2
                xk += 1
                r0 = t0 + i * 128
                P.dma("sp", xt[xi][:], x_src[r0:r0 + 128, :], b_xt[xi], writes=[b_xt[xi]])
                P.op("pool", lambda e, i=i: e.tensor_tensor(out=yacc[:, i, :], in0=yacc[:, i, :], in1=gt[:], op=ALU.mult),
                     reads=[b_mod], writes=b_y[i])
                P.op("dve", lambda e, i=i, xi=xi: e.scalar_tensor_tensor(
                    out=yacc[:, i, :], in0=xt[xi][:], scalar=ALPHA, in1=yacc[:, i, :], op0=ALU.mult, op1=ALU.add),
                    reads=[b_xt[xi]], writes=b_y[i])
                for h2 in range(2):
                    P.op("dve", lambda e, i=i, h2=h2: e.bn_stats(out=stats[:, i, h2, :], in_=yacc[:, i, h2 * 512:(h2 + 1) * 512]),
                         reads=b_y[i], writes=[b_stats[i]])
                P.op("dve", lambda e, i=i: e.bn_aggr(out=mv[:, i, :], in_=stats[:, i, :, :].rearrange("p a b -> p (a b)")),
                     reads=[b_stats[i]], writes=[b_mv])
            self.rstd_from_var(P, mv, b_mv, rstd, b_rstd, NTI)
            for i in range(NTI):
                oi = i % 2
                r0 = t0 + i * 128
                P.op("dve", lambda e, i=i, oi=oi: e.tensor_scalar(
                    out=ot[oi][:], in0=yacc[:, i, :], scalar1=mv[:, i, 0:1], scalar2=rstd[:, i:i + 1],
                    op0=ALU.subtract, op1=ALU.mult), reads=b_y[i] + [b_mv, b_rstd], writes=[b_ot[oi]])
                P.op("pool", lambda e, oi=oi: e.tensor_tensor(out=ot[oi][:], in0=ot[oi][:], in1=lng[:], op=ALU.mult),
                     reads=[b_ln], writes=[b_ot[oi]])
                P.op("pool", lambda e, oi=oi: e.tensor_tensor(out=ot[oi][:], in0=ot[oi][:], in1=lnb[:], op=ALU.add),
                     reads=[b_ln], writes=[b_ot[oi]])
                P.dma("sp", x_dst[r0:r0 + 128, :], ot[oi][:], b_ot[oi], reads=[b_ot[oi]])
        P.run()

    def phase_inproj(self, l, x_src):
        nc, cfg = self.nc, self.cfg
        P = Prog(nc, f"i{l}")
        TB = 512
        nblk = cfg.ntok // TB
        blk_per_seq = cfg.S // TB
        w = P.sbuf("w", [128, 8, DIN], BF16); b_w = Buf()
        for n in range(7):
            P.dma("pool", w[:, :, n * 512:(n + 1) * 512],
                  self.w_in[l, :, n * 512:(n + 1) * 512].rearrange("(c p) f -> p c f", p=128), b_w, writes=[b_w])
        sc1 = P.sbuf("sc1", [128, D], F32); sh = P.sbuf("sh", [128, D], F32); b_mod = Buf()
        xt = [P.sbuf(f"xt{i}", [128, D], F32) for i in range(2)]; b_xt = [Buf(), Buf()]
        hf = [P.sbuf(f"hf{i}", [128, D], F32) for i in range(2)]; b_hf = [Buf(), Buf()]
        hT = [P.sbuf(f"hT{i}", [128, 8, TB], BF16) for i in range(2)]
        b_hT = [[Buf() for _ in range(4)] for _ in range(2)]
        NST = 4
        stf = [P.sbuf(f"stf{i}", [128, 512], F32) for i in range(NST)]; b_stf = [Buf() for _ in range(NST)]
        stb = [P.sbuf(f"stb{i}", [128, 512], BF16) for i in range(NST)]; b_stb = [Buf() for _ in range(NST)]
        psT = [P.psum(f"psT{i}", [128, 512]) for i in range(2)]; b_psT = [Buf(psum=True), Buf(psum=True)]
        NPS = 6
        mm = [P.psum(f"mm{i}", [128, 512]) for i in range(NPS)]; b_mm = [Buf(psum=True) for _ in range(NPS)]
        xk = 0; kf = 0; kb_ = 0; km = 0
        for blk in range(nblk):
            t0 = blk * TB
            bseq = blk // blk_per_seq
            hs = blk % 2
            if blk % blk_per_seq == 0:
                ls = l * 2
                P.dma("sp", sh[:], bcast_rows(self.mod[bseq, ls, 0:D], 128), b_mod, writes=[b_mod])
                P.dma("sp", sc1[:], bcast_rows(self.mod[bseq, ls, D:2 * D], 128), b_mod, writes=[b_mod])
            for i in range(4):
                xi = xk % 2; xk += 1
                r0 = t0 + i * 128
                P.dma("sp", xt[xi][:], x_src[r0:r0 + 128, :], b_xt[xi], writes=[b_xt[xi]])
                P.op("dve", lambda e, xi=xi: e.tensor_tensor(out=hf[xi][:], in0=xt[xi][:], in1=sc1[:], op=ALU.mult),
                     reads=[b_xt[xi], b_mod], writes=[b_hf[xi]])
                P.op("pool", lambda e, xi=xi: e.tensor_tensor(out=hf[xi][:], in0=hf[xi][:], in1=sh[:], op=ALU.add),
                     reads=[b_hf[xi], b_mod], writes=[b_hf[xi]])
                for hb in range(2):
                    for q in range(4):
                        dc = hb * 4 + q
                        P.op("pe", lambda e, xi=xi, hb=hb, q=q, dc=dc: e.transpose(
                            psT[hb][:, q * 128:(q + 1) * 128], hf[xi][:, dc * 128:(dc + 1) * 128], self.identf[:]),
                            reads=[b_hf[xi]], writes=[b_psT[hb]], signal=(q == 3))
                    P.op("act", lambda e, hb=hb, i=i, hs=hs: e.activation(
                        out=hT[hs][:, hb * 4:(hb + 1) * 4, i * 128:(i + 1) * 128],
                        in_=psT[hb][:].rearrange("p (c t) -> p c t", t=128), func=AF.Copy),
                        reads=[b_psT[hb]], writes=[b_hT[hs][i]])
            for cc in list(range(0, 8)) + list(range(16, 24)):
                pb = km % NPS; km += 1
                for dc in range(8):
                    P.op("pe", lambda e, cc=cc, dc=dc, pb=pb, hs=hs: e.matmul(
                        mm[pb][:], w[:, dc, cc * 128:(cc + 1) * 128], hT[hs][:, dc, :], start=(dc == 0), stop=(dc == 7)),
                        reads=[b_w] + b_hT[hs], writes=[b_mm[pb]], signal=(dc == 7))
                if cc < 8:
                    si = kf % NST; kf += 1
                    if cc < 4:
                        P.op("act", lambda e, pb=pb, si=si: e.activation(out=stf[si][:], in_=mm[pb][:], func=AF.Silu),
                             reads=[b_mm[pb]], writes=[b_stf[si]])
                        dst = self.hqT[cc * 128:(cc + 1) * 128, t0:t0 + TB]
                    else:
                        P.op("act", lambda e, pb=pb, si=si: e.activation(out=stf[si][:], in_=mm[pb][:], func=AF.Tanh, scale=0.5),
                             reads=[b_mm[pb]], writes=[b_stf[si]])
                        dst = self.hfT[(cc - 4) * 128:(cc - 3) * 128, t0:t0 + TB]
                    P.dma("sp", dst, stf[si][:], b_stf[si], reads=[b_stf[si]])
                else:
                    si = kb_ % NST; kb_ += 1
                    sc = 0.125 if cc < 20 else 1.0
                    P.op("dve", lambda e, pb=pb, si=si, sc=sc: e.tensor_scalar(
                        out=stb[si][:], in0=mm[pb][:], scalar1=sc, scalar2=None, op0=ALU.mult),
                        reads=[b_mm[pb]], writes=[b_stb[si]])
                    if cc < 20:
                        dst = self.sqT[(cc - 16) * 128:(cc - 15) * 128, t0:t0 + TB]
                    else:
                        dst = self.skT[(cc - 20) * 128:(cc - 19) * 128, t0:t0 + TB]
                    P.dma("sp", dst, stb[si][:], b_stb[si], reads=[b_stb[si]])
            for i in range(4):
                r0 = t0 + i * 128
                for seg, c0 in (("hi", 1024), ("hg", 1536), ("sv", 3072)):
                    pb = km % NPS; km += 1
                    for dc in range(8):
                        P.op("pe", lambda e, dc=dc, pb=pb, hs=hs, i=i, c0=c0: e.matmul(
                            mm[pb][:], hT[hs][:, dc, i * 128:(i + 1) * 128], w[:, dc, c0:c0 + 512],
                            start=(dc == 0), stop=(dc == 7)),
                            reads=[b_w, b_hT[hs][i]], writes=[b_mm[pb]], signal=(dc == 7))
                    if seg == "hg":
                        si = kf % NST; kf += 1
                        P.op("act", lambda e, pb=pb, si=si: e.activation(out=stf[si][:], in_=mm[pb][:], func=AF.Silu),
                             reads=[b_mm[pb]], writes=[b_stf[si]])
                        P.dma("sp", self.hg_tm[r0:r0 + 128, :], stf[si][:], b_stf[si], reads=[b_stf[si]])
                    else:
                        si = kb_ % NST; kb_ += 1
                        P.op("dve", lambda e, pb=pb, si=si: e.tensor_copy(out=stb[si][:], in_=mm[pb][:]),
                             reads=[b_mm[pb]], writes=[b_stb[si]])
                        dst = self.hi_tm if seg == "hi" else self.sv_tm
                        P.dma("sp", dst[r0:r0 + 128, :], stb[si][:], b_stb[si], reads=[b_stb[si]])
        P.run()

    def phase_outproj(self, l, x_src, x_dst):
        nc, cfg = self.nc, self.cfg
        P = Prog(nc, f"o{l}")
        TB = 512
        nblk = cfg.ntok // TB
        blk_per_seq = cfg.S // TB
        w = P.sbuf("w", [128, 8, D], BF16); b_w = Buf()
        P.dma("pool", w[:], self.w_out[l].rearrange("(c p) f -> p c f", p=128), b_w, writes=[b_w])
        gt = P.sbuf("gt", [128, D], F32); b_mod = Buf()
        lng = P.sbuf("lng", [128, D], F32); lnb = P.sbuf("lnb", [128, D], F32); b_ln = Buf()
        P.dma("sp", lng[:], bcast_rows(self.ln_gain[l, 0, :], 128), b_ln, writes=[b_ln])
        P.dma("sp", lnb[:], bcast_rows(self.ln_bias[l, 0, :], 128), b_ln, writes=[b_ln])
        oTs = [P.sbuf(f"oTs{i}", [128, 8, TB], BF16) for i in range(2)]; b_oTs = [Buf(), Buf()]
        xt = [P.sbuf(f"xt{i}", [128, D], F32) for i in range(2)]; b_xt = [Buf(), Buf()]
        zt = P.sbuf("zt", [128, 4, D], F32); b_z = [Buf() for _ in range(4)]
        ot = [P.sbuf(f"ot{i}", [128, D], F32) for i in range(2)]; b_ot = [Buf(), Buf()]
        stats = P.sbuf("stats", [128, 4, 2, 6], F32); b_stats = [Buf() for _ in range(4)]
        mv = P.sbuf("mv", [128, 4, 2], F32); b_mv = Buf()
        rstd = P.sbuf("rstd", [128, 4], F32); b_rstd = Buf()
        yps = [P.psum(f"yps{i}", [128, 512]) for i in range(4)]; b_yps = [Buf(psum=True) for _ in range(4)]
        xk = 0; kp = 0
        for blk in range(nblk):
            t0 = blk * TB
            bseq = blk // blk_per_seq
            os_ = blk % 2
            if blk % blk_per_seq == 0:
                P.dma("sp", gt[:], bcast_rows(self.mod[bseq, l * 2, 2 * D:3 * D], 128), b_mod, writes=[b_mod])
            P.dma("sp", oTs[os_][:], self.oT[:, t0:t0 + TB].rearrange("(c p) t -> p c t", p=128), b_oTs[os_],
                  writes=[b_oTs[os_]])
            for i in range(4):
                xi = xk % 2; xk += 1
                r0 = t0 + i * 128
                P.dma("sp", xt[xi][:], x_src[r0:r0 + 128, :], b_xt[xi], writes=[b_xt[xi]])
                for h2 in range(2):
                    pb = kp % 4; kp += 1
                    for cc in range(8):
                        P.op("pe", lambda e, cc=cc, pb=pb, os_=os_, i=i, h2=h2: e.matmul(
                            yps[pb][:], oTs[os_][:, cc, i * 128:(i + 1) * 128], w[:, cc, h2 * 512:(h2 + 1) * 512],
                            start=(cc == 0), stop=(cc == 7)), reads=[b_w, b_oTs[os_]], writes=[b_yps[pb]], signal=(cc == 7))
                    P.op("dve", lambda e, pb=pb, i=i, h2=h2: e.tensor_tensor(
                        out=zt[:, i, h2 * 512:(h2 + 1) * 512], in0=yps[pb][:], in1=gt[:, h2 * 512:(h2 + 1) * 512], op=ALU.mult),
                        reads=[b_yps[pb], b_mod], writes=[b_z[i]])
                P.op("dve", lambda e, i=i, xi=xi: e.scalar_tensor_tensor(
                    out=zt[:, i, :], in0=xt[xi][:], scalar=ALPHA, in1=zt[:, i, :], op0=ALU.mult, op1=ALU.add),
                    reads=[b_xt[xi]], writes=[b_z[i]])
                for h2 in range(2):
                    P.op("dve", lambda e, i=i, h2=h2: e.bn_stats(out=stats[:, i, h2, :], in_=zt[:, i, h2 * 512:(h2 + 1) * 512]),
                         reads=[b_z[i]], writes=[b_stats[i]])
                P.op("dve", lambda e, i=i: e.bn_aggr(out=mv[:, i, :], in_=stats[:, i, :, :].rearrange("p a b -> p (a b)")),
                     reads=[b_stats[i]], writes=[b_mv])
            self.rstd_from_var(P, mv, b_mv, rstd, b_rstd, 4)
            for i in range(4):
                oi = i % 2
                r0 = t0 + i * 128
                P.op("dve", lambda e, i=i, oi=oi: e.tensor_scalar(
                    out=ot[oi][:], in0=zt[:, i, :], scalar1=mv[:, i, 0:1], scalar2=rstd[:, i:i + 1],
                    op0=ALU.subtract, op1=ALU.mult), reads=[b_z[i], b_mv, b_rstd], writes=[b_ot[oi]])
                P.op("pool", lambda e, oi=oi: e.tensor_tensor(out=ot[oi][:], in0=ot[oi][:], in1=lng[:], op=ALU.mult),
                     reads=[b_ln], writes=[b_ot[oi]])
                P.op("pool", lambda e, oi=oi: e.tensor_tensor(out=ot[oi][:], in0=ot[oi][:], in1=lnb[:], op=ALU.add),
                     reads=[b_ln], writes=[b_ot[oi]])
                P.dma("sp", x_dst[r0:r0 + 128, :], ot[oi][:], b_ot[oi], reads=[b_ot[oi]])
        P.run()

    def phase_sb(self, l):
        nc, cfg = self.nc, self.cfg
        P = Prog(nc, f"s{l}")
        S = cfg.S
        NB = S // 128
        NG = S // 512
        ka = [P.sbuf(f"ka{i}", [128, S], BF16) for i in range(2)]; b_ka = [Buf(), Buf()]
        qz = [P.sbuf(f"qz{i}", [128, S], BF16) for i in range(2)]; b_qz = [Buf(), Buf()]
        NQA = 3
        qa = [[P.sbuf(f"qa{i}{j}", [128, S], BF16) for j in range(NQA)] for i in range(2)]
        b_qa = [[Buf() for _ in range(NQA)] for _ in range(2)]
        vall = P.sbuf("vall", [128, NB, DSB], BF16); b_v = Buf()
        e1 = [P.sbuf(f"e1{i}", [128, 512], F32) for i in range(2)]; b_e1 = [Buf(), Buf()]
        NSP = 4
        sp = [P.sbuf(f"sp{i}", [128, 512], BF16) for i in range(NSP)]; b_sp = [Buf() for _ in range(NSP)]
        At = [P.sbuf(f"A{i}", [128, 512], BF16) for i in range(NSP)]; b_A = [Buf() for _ in range(NSP)]
        ost = [P.sbuf(f"ost{i}", [64, 512], BF16) for i in range(2)]; b_ost = [Buf(), Buf()]
        zps = [P.psum(f"z{i}", [128, 512]) for i in range(2)]; b_z = [Buf(psum=True), Buf(psum=True)]
        gps = [P.psum(f"g{i}", [128, 512]) for i in range(2)]; b_g = [Buf(psum=True), Buf(psum=True)]
        cps = [P.psum(f"c{i}", [128, 512]) for i in range(2)]; b_c = [Buf(psum=True), Buf(psum=True)]
        ops_ = [P.psum(f"o{i}", [128, 512]) for i in range(2)]; b_o = [Buf(psum=True), Buf(psum=True)]
        for i in range(2):
            P.op("pool", lambda e, i=i: e.memset(ka[i][64:128, :], 0.0), writes=[b_ka[i]])
            P.op("pool", lambda e, i=i: e.memset(ka[i][64:65, :], 1.0), writes=[b_ka[i]])
            P.op("pool", lambda e, i=i: e.memset(qz[i][64:128, :], 0.0), writes=[b_qz[i]])
            for j in range(NQA):
                P.op("pool", lambda e, i=i, j=j: e.memset(qa[i][j][64:128, :], 0.0), writes=[b_qa[i][j]])

        steps = []
        hk = 0
        for b in range(cfg.nseq):
            for h in range(NH_S):
                hp = hk % 2
                for g in range(NG):
                    kmax = 4 * g + 3
                    for kb in range(kmax, -1, -1):
                        i = kb - 4 * g
                        n0 = max(i, 0) * 128
                        steps.append(dict(b=b, h=h, hp=hp, g=g, kb=kb, n0=n0, diag=(i >= 0),
                                          first=(kb == kmax), last=(kb == 0), newhead=(g == 0 and kb == kmax),
                                          gk=None))
                hk += 1
        gk = -1
        for st in steps:
            if st["first"]:
                gk += 1
            st["gp"] = gk % 2
        ost_k = [0]

        def load_head(st):
            b, h, hp = st["b"], st["h"], st["hp"]
            c0 = b * S
            P.dma("pool", ka[hp][0:64, :], self.skT[h * 64:(h + 1) * 64, c0:c0 + S], b_ka[hp], writes=[b_ka[hp]])
            P.dma("pool", qz[hp][0:64, :], self.sqT[h * 64:(h + 1) * 64, c0:c0 + S], b_qz[hp], writes=[b_qz[hp]])
            for j in range(NQA):
                P.dma("pool", qa[hp][j][0:64, :], self.sqT[h * 64:(h + 1) * 64, c0:c0 + S], b_qa[hp][j], writes=[b_qa[hp][j]])

        def emit_p1(j):
            st = steps[j]
            hp, kb, n0 = st["hp"], st["kb"], st["n0"]
            c0 = st["g"] * 512
            zb = j % 2
            P.op("pe", lambda e: e.matmul(zps[zb][:, n0:512], ka[hp][:, kb * 128:(kb + 1) * 128],
                                           qz[hp][:, c0 + n0:c0 + 512], start=True, stop=True),
                 reads=[b_ka[hp], b_qz[hp]], writes=[b_z[zb]])

        def emit_a12(j):
            st = steps[j]
            n0 = st["n0"]
            zb = j % 2; eb = j % 2; sb = j % NSP
            P.op("act", lambda e: e.activation(out=e1[eb][:, n0:512], in_=zps[zb][:, n0:512], func=AF.Exp),
                 reads=[b_z[zb]], writes=[b_e1[eb]])
            P.op("act", lambda e: e.activation(out=sp[sb][:, n0:512], in_=e1[eb][:, n0:512], func=AF.Ln, bias=1.0, scale=1.0),
                 reads=[b_e1[eb]], writes=[b_sp[sb]])
            if st["diag"]:
                P.op("dve", lambda e: e.tensor_tensor(out=sp[sb][:, n0:n0 + 128], in0=sp[sb][:, n0:n0 + 128],
                                                       in1=self.mask_st[:], op=ALU.mult), writes=[b_sp[sb]])

        def emit_carry(j):
            st = steps[j]
            n0 = st["n0"]
            sb = j % NSP
            hp, gp = st["hp"], st["gp"]
            c0 = st["g"] * 512
            par = j % NQA
            if st["first"]:
                P.op("pe", lambda e: e.matmul(cps[gp][:, :], self.zeros_b[:, 0:128], self.zeros_b[:, 0:512], start=True, stop=True),
                     writes=[b_c[gp]])
            P.op("dve", lambda e: e.tensor_copy(out=qa[hp][par][64:65, c0 + n0:c0 + 512], in_=cps[gp][64:65, n0:512]),
                 reads=[b_c[gp]], writes=[b_qa[hp][par]])
            P.op("pe", lambda e: e.matmul(cps[gp][:, n0:512], self.neg_col[:, :], sp[sb][:, n0:512], start=False, stop=True,
                                           skip_group_check=True),
                 reads=[b_sp[sb]], writes=[b_c[gp]])

        def emit_main(j):
            st = steps[j]
            hp, kb, n0, gp, h = st["hp"], st["kb"], st["n0"], st["gp"], st["h"]
            c0 = st["g"] * 512
            par = j % NQA; sb = j % NSP; gb = j % 2; ab = j % NSP
            if st["first"]:
                P.op("pe", lambda e: e.matmul(ops_[gp][:, :], self.zeros_b[:, 0:128], self.zeros_b[:, 0:512], start=True, stop=True),
                     writes=[b_o[gp]])
            P.op("pe", lambda e: e.matmul(gps[gb][:, n0:512], ka[hp][:, kb * 128:(kb + 1) * 128],
                                           qa[hp][par][:, c0 + n0:c0 + 512], start=True, stop=False),
                 reads=[b_ka[hp], b_qa[hp][par]], writes=[b_g[gb]], signal=False)
            P.op("pe", lambda e: e.matmul(gps[gb][:, n0:512], self.tri_neg[:], sp[sb][:, n0:512], start=False, stop=True),
                 reads=[b_sp[sb]], writes=[b_g[gb]])

        def emit_a3(j):
            st = steps[j]
            n0 = st["n0"]
            gb = j % 2; ab = j % NSP
            P.op("act", lambda e: e.activation(out=At[ab][:, n0:512], in_=gps[gb][:, n0:512], func=AF.Exp),
                 reads=[b_g[gb]], writes=[b_A[ab]])
            if st["diag"]:
                P.op("dve", lambda e: e.tensor_tensor(out=At[ab][:, n0:n0 + 128], in0=At[ab][:, n0:n0 + 128],
                                                       in1=self.mask_st[:], op=ALU.mult), writes=[b_A[ab]])

        def emit_p4(j):
            st = steps[j]
            kb, n0, gp, h, b = st["kb"], st["n0"], st["gp"], st["h"], st["b"]
            ab = j % NSP
            if st["newhead"] and h == 0:
                for q0 in range(0, NB, 8):
                    P.dma("sp", vall[:, q0:q0 + 8, :],
                          self.sv_tm[b * S + q0 * 128:b * S + (q0 + 8) * 128, :].rearrange("(n p) d -> p n d", p=128),
                          b_v, writes=[b_v])
            P.op("pe", lambda e: e.matmul(ops_[gp][:, n0:512], vall[:, kb, (h // 2) * 128:(h // 2 + 1) * 128], At[ab][:, n0:512],
                                           start=False, stop=True, skip_group_check=True),
                 reads=[b_v, b_A[ab]], writes=[b_o[gp]])
            if st["last"]:
                k = ost_k[0] % 2; ost_k[0] += 1
                c0 = b * S + st["g"] * 512
                P.op("dve", lambda e: e.tensor_copy(out=ost[k][:], in_=ops_[gp][(h % 2) * 64:(h % 2) * 64 + 64, :]), reads=[b_o[gp]], writes=[b_ost[k]])
                P.dma("sp", self.oT[DH + h * 64:DH + (h + 1) * 64, c0:c0 + 512], ost[k][:], b_ost[k], reads=[b_ost[k]])

        n = len(steps)
        ok = lambda m: 0 <= m < n
        per_head = n // (cfg.nseq * NH_S)
        LOOKH = max(3, min(110, per_head - 8))
        for j in range(0, min(LOOKH, n)):
            if steps[j]["newhead"]:
                load_head(steps[j])
        for j in range(-3, n + 1):
            if ok(j + LOOKH) and steps[j + LOOKH]["newhead"] and j + LOOKH >= LOOKH:
                load_head(steps[j + LOOKH])
            if ok(j + 3):
                emit_p1(j + 3)
            if ok(j + 1):
                emit_carry(j + 1)
            if ok(j):
                emit_main(j)
            if ok(j - 1):
                emit_p4(j - 1)
            if ok(j + 2):
                emit_a12(j + 2)
            if ok(j):
                emit_a3(j)
        P.run()

    def phase_hgrn(self, l):
        nc, cfg = self.nc, self.cfg
        P = Prog(nc, f"h{l}")
        S = cfg.S
        ns = cfg.nseq
        ntile = S // 128
        R = 31
        NS_ = 4 * ns
        gbc = P.sbuf("gbc", [128, DH], F32); b_gbc = Buf()
        P.dma("sp", gbc[:], bcast_rows(self.hgain[l, :], 128), b_gbc, writes=[b_gbc])
        NT_ = 2 * ns
        def many(name, n, shape, dt):
            return [P.sbuf(f"{name}{i}", shape, dt) for i in range(n)], [Buf() for _ in range(n)]
        tf, b_tf = many("tf", NT_, [128, 4, 128], F32)
        qs, b_qs = many("qs", NT_, [128, 4, 128], F32)
        vt, b_vt = many("vt", NT_, [128, DH], BF16)
        gg, b_gg = many("gg", NT_, [128, DH], F32)
        ss, b_ss = many("ss", NT_, [128, 4], F32)
        on, b_on = many("on", NT_, [128, DH], F32)
        onb, b_onb = many("onb", NT_, [128, DH], BF16)
        oTst, b_oTst = many("oTst", NT_, [128, 4, 128], BF16)
        ff, b_ff = many("ff", NS_, [128, 128], F32)
        lf, b_lf = many("lf", NS_, [128, 128], F32)
        kk, b_kk = many("kk", NS_, [128, 128], F32)
        bb, b_bb = many("bb", NS_, [128, 128], F32)
        eq, b_eq = many("eq", NS_, [128, 128], F32)
        ek, b_ek = many("ek", NS_, [128, 128], F32)
        sc_, b_sc = many("sc", NS_, [128, 8], F32)
        qzp, b_qzp = many("qzp", NS_, [128, 384], BF16)
        kt, b_kt = many("kt", NS_, [128, 128], BF16)
        ktok, b_ktok = many("ktok", NS_, [128, 128], BF16)
        pT, b_pT = many("pT", NS_, [128, 128], BF16)
        sq_, b_sq = many("sqj", NS_, [128, 128], F32)
        tmp, b_tmp = many("tmp", NS_, [128, 128], F32)
        St, b_St = many("St", NS_, [128, 128], F32)
        Sb, b_Sb = many("Sb", NS_, [128, 128], BF16)
        osb, b_osb = many("osb", NS_, [128, 128], F32)
        ones = self.ones_f
        _pkt = P.psum("pkt", [128, 1024], BF16); b_pkt = Buf(psum=True)
        p_kt = _pkt[:, 0:128]
        _psc = [P.psum(f"psc{i}", [128, 512]) for i in range(2)]; b_psc = [Buf(psum=True), Buf(psum=True)]
        p_sc = [t[:, 0:128] for t in _psc]
        _po = [P.psum(f"po{i}", [128, 512]) for i in range(2)]; b_po = [Buf(psum=True), Buf(psum=True)]
        p_o = [t[:, 0:128] for t in _po]
        _pkv = [P.psum(f"pkv{i}", [128, 512]) for i in range(2)]; b_pkv = [Buf(psum=True), Buf(psum=True)]
        p_kv = [t[:, 0:128] for t in _pkv]
        _pot = P.psum("pot", [128, 1024], BF16); b_pot = Buf(psum=True)
        p_ot = _pot[:, 0:512].rearrange("p (h t) -> p h t", t=128)
        for i in range(NS_):
            P.op("pool", lambda e, i=i: e.memset(qzp[i][:], 0.0), writes=[b_qzp[i]])
            P.op("pool", lambda e, i=i: e.memset(St[i][:], 0.0), writes=[b_St[i]])

        def head_stream(b, h, ti, tk):
            sk = b * 4 + h
            pp = sk % 2
            A_ = self.lbA[:, l * 4 + h:l * 4 + h + 1]
            B_ = self.lbB[:, l * 4 + h:l * 4 + h + 1]
            P.op("dve", lambda e: e.tensor_scalar(out=ff[sk][:], in0=tf[tk][:, h, :], scalar1=A_, scalar2=B_,
                                                   op0=ALU.mult, op1=ALU.add), reads=[b_tf[tk]], writes=[b_ff[sk]])
            yield
            P.op("act", lambda e: e.activation(out=lf[sk][:], in_=ff[sk][:], func=AF.Ln), reads=[b_ff[sk]], writes=[b_lf[sk]])
            P.op("pool", lambda e: e.tensor_scalar(out=kk[sk][:], in0=ff[sk][:], scalar1=-1.0, scalar2=1.0,
                                                    op0=ALU.mult, op1=ALU.add), reads=[b_ff[sk]], writes=[b_kk[sk]])
            yield
            for c in range(2):
                P.op("dve", lambda e, c=c: e.tensor_tensor_scan(
                    out=bb[sk][:, c * 64:(c + 1) * 64], data0=ones[:, 0:64], data1=lf[sk][:, c * 64:(c + 1) * 64],
                    initial=0.0, op0=ALU.mult, op1=ALU.add), reads=[b_lf[sk]], writes=[b_bb[sk]])
            yield
            for c in range(2):
                P.op("dve", lambda e, c=c: e.tensor_scalar(
                    out=sc_[sk][:, 3 * c:3 * c + 1], in0=bb[sk][:, c * 64 + R:c * 64 + R + 1], scalar1=-1.0, scalar2=None,
                    op0=ALU.mult), reads=[b_bb[sk]], writes=[b_sc[sk]])
            yield
            for c in range(2):
                br = bb[sk][:, c * 64 + R:c * 64 + R + 1]
                nbr = sc_[sk][:, 3 * c:3 * c + 1]
                P.op("act", lambda e, c=c, nbr=nbr: e.activation(
                    out=eq[sk][:, c * 64:(c + 1) * 64], in_=bb[sk][:, c * 64:(c + 1) * 64], func=AF.Exp, bias=nbr, scale=1.0),
                    reads=[b_bb[sk], b_sc[sk]], writes=[b_eq[sk]])
                P.op("act", lambda e, c=c, br=br: e.activation(
                    out=ek[sk][:, c * 64:(c + 1) * 64], in_=bb[sk][:, c * 64:(c + 1) * 64], func=AF.Exp, bias=br, scale=-1.0),
                    reads=[b_bb[sk]], writes=[b_ek[sk]])
                P.op("act", lambda e, c=c, br=br: e.activation(
                    out=sc_[sk][:, 3 * c + 1:3 * c + 2], in_=br, func=AF.Exp), reads=[b_bb[sk]], writes=[b_sc[sk]])
                P.op("act", lambda e, c=c: e.activation(
                    out=sc_[sk][:, 3 * c + 2:3 * c + 3], in_=bb[sk][:, c * 64 + 63:c * 64 + 64], func=AF.Exp),
                    reads=[b_bb[sk]], writes=[b_sc[sk]])
            yield
            P.op("dve", lambda e: e.tensor_tensor(
                out=qzp[sk][:].rearrange("p (c x) -> p c x", x=192)[:, :, 0:64],
                in0=qs[tk][:, h, :].rearrange("p (c x) -> p c x", x=64),
                in1=eq[sk][:].rearrange("p (c x) -> p c x", x=64), op=ALU.mult),
                reads=[b_qs[tk], b_eq[sk]], writes=[b_qzp[sk]])
            P.op("pool", lambda e: e.tensor_tensor(out=kt[sk][:], in0=kk[sk][:], in1=ek[sk][:], op=ALU.mult),
                 reads=[b_kk[sk], b_ek[sk]], writes=[b_kt[sk]])
            yield
            P.op("pe", lambda e: e.transpose(p_kt, kt[sk][:], self.identb[:]), reads=[b_kt[sk]], writes=[b_pkt])
            P.op("act", lambda e: e.activation(out=ktok[sk][:], in_=p_kt, func=AF.Copy), reads=[b_pkt], writes=[b_ktok[sk]])
            P.op("pe", lambda e: e.matmul(
                p_sc[pp].rearrange("p (c x) -> p c x", x=64), kt[sk][:],
                qzp[sk][:].rearrange("p (c x) -> p c x", x=192)[:, :, 0:64], start=True, stop=True),
                reads=[b_kt[sk], b_qzp[sk]], writes=[b_psc[pp]])
            P.op("dve", lambda e: e.tensor_scalar(out=sq_[sk][:], in0=p_sc[pp], scalar1=1e30, scalar2=-1e30,
                                                   op0=ALU.min, op1=ALU.max), reads=[b_psc[pp]], writes=[b_sq[sk]])
            yield
            P.op("dve", lambda e: e.tensor_tensor(out=pT[sk][:], in0=sq_[sk][:], in1=self.mask_bd[:], op=ALU.mult),
                 reads=[b_sq[sk]], writes=[b_pT[sk]])
            P.op("dve", lambda e: e.tensor_scalar(out=Sb[sk][:], in0=St[sk][:], scalar1=sc_[sk][:, 1:2], scalar2=None,
                                                   op0=ALU.mult), reads=[b_St[sk], b_sc[sk]], writes=[b_Sb[sk]])
            yield
            for c in range(2):
                if c == 0:
                    P.op("pe", lambda e: e.matmul(p_o[pp], pT[sk][:], vt[tk][:, h * 128:(h + 1) * 128], start=True, stop=False),
                         reads=[b_pT[sk], b_vt[tk]], writes=[b_po[pp]], signal=False)
                P.op("pe", lambda e, c=c: e.matmul(
                    p_o[pp], qzp[sk][:, c * 128:(c + 1) * 128], Sb[sk][:], start=(c == 1), stop=True),
                    reads=[b_qzp[sk], b_Sb[sk]], writes=[b_po[pp]])
                P.op("pe", lambda e, c=c: e.matmul(
                    p_kv[pp], ktok[sk][c * 64:(c + 1) * 64, :], vt[tk][c * 64:(c + 1) * 64, h * 128:(h + 1) * 128],
                    start=True, stop=True), reads=[b_ktok[sk], b_vt[tk]], writes=[b_pkv[pp]])
                if c == 0:
                    P.op("act", lambda e: e.activation(out=osb[sk][:], in_=p_o[pp], func=AF.Copy),
                         reads=[b_po[pp]], writes=[b_osb[sk]])
                else:
                    P.op("dve", lambda e: e.tensor_tensor(out=osb[sk][:], in0=p_o[pp], in1=osb[sk][:], op=ALU.add),
                         reads=[b_po[pp]], writes=[b_osb[sk]])
                P.op("dve", lambda e, c=c: e.tensor_scalar(
                    out=tmp[sk][:], in0=p_kv[pp], scalar1=eq[sk][:, c * 64 + 63:c * 64 + 64], scalar2=None, op0=ALU.mult),
                    reads=[b_pkv[pp], b_eq[sk]], writes=[b_tmp[sk]])
                yield
                P.op("dve", lambda e, c=c: e.scalar_tensor_tensor(
                    out=St[sk][:], in0=St[sk][:], scalar=sc_[sk][:, 3 * c + 2:3 * c + 3], in1=tmp[sk][:],
                    op0=ALU.mult, op1=ALU.add), reads=[b_tmp[sk], b_sc[sk]], writes=[b_St[sk]])
                yield
                if c == 0:
                    P.op("dve", lambda e: e.tensor_scalar(out=Sb[sk][:], in0=St[sk][:], scalar1=sc_[sk][:, 4:5], scalar2=None,
                                                           op0=ALU.mult), reads=[b_St[sk], b_sc[sk]], writes=[b_Sb[sk]])
                    yield
            P.op("act", lambda e: e.activation(out=sq_[sk][:], in_=osb[sk][:], func=AF.Square, accum_out=ss[tk][:, h:h + 1]),
                 reads=[b_osb[sk]], writes=[b_sq[sk], b_ss[tk]])
            P.op("pool", lambda e: e.tensor_tensor(out=on[tk][:, h * 128:(h + 1) * 128], in0=osb[sk][:],
                                                    in1=gg[tk][:, h * 128:(h + 1) * 128], op=ALU.mult),
                 reads=[b_osb[sk], b_gg[tk]], writes=[b_on[tk]])
            yield

        def tile_tail(b, ti, tk):
            r0 = b * S + ti * 128
            P.op("dve", lambda e: e.tensor_scalar(out=ss[tk][:], in0=ss[tk][:], scalar1=1.0 / 128.0, scalar2=RMS_EPS,
                                                   op0=ALU.mult, op1=ALU.add), reads=[b_ss[tk]], writes=[b_ss[tk]])
            P.op("act", lambda e: e.activation(out=ss[tk][:], in_=ss[tk][:], func=AF.Ln), reads=[b_ss[tk]], writes=[b_ss[tk]])
            P.op("act", lambda e: e.activation(out=ss[tk][:], in_=ss[tk][:], func=AF.Exp, scale=-0.5),
                 reads=[b_ss[tk]], writes=[b_ss[tk]])
            for h in range(4):
                P.op("dve", lambda e, h=h: e.tensor_scalar(
                    out=onb[tk][:, h * 128:(h + 1) * 128], in0=on[tk][:, h * 128:(h + 1) * 128], scalar1=ss[tk][:, h:h + 1],
                    scalar2=None, op0=ALU.mult), reads=[b_ss[tk], b_on[tk]], writes=[b_onb[tk]])
            for h in range(4):
                P.op("pe", lambda e, h=h: e.transpose(p_ot[:, h, :], onb[tk][:, h * 128:(h + 1) * 128], self.identb[:]),
                     reads=[b_onb[tk]], writes=[b_pot], signal=(h == 3))
            P.op("act", lambda e: e.activation(out=oTst[tk][:], in_=p_ot, func=AF.Copy), reads=[b_pot], writes=[b_oTst[tk]])
            P.dma("sp", self.oT[0:DH, r0:r0 + 128].rearrange("(h p) t -> p h t", p=128), oTst[tk][:], b_oTst[tk],
                  reads=[b_oTst[tk]])

        def tile_loads(b, ti, tk):
            r0 = b * S + ti * 128
            P.dma("sp", tf[tk][:], self.hfT[:, r0:r0 + 128].rearrange("(h p) t -> p h t", p=128), b_tf[tk], writes=[b_tf[tk]])
            P.dma("sp", qs[tk][:], self.hqT[:, r0:r0 + 128].rearrange("(h p) t -> p h t", p=128), b_qs[tk], writes=[b_qs[tk]])
            P.dma("sp", vt[tk][:], self.hi_tm[r0:r0 + 128, :], b_vt[tk], writes=[b_vt[tk]])
            P.dma("sp", gg[tk][:], self.hg_tm[r0:r0 + 128, :], b_gg[tk], writes=[b_gg[tk]])
            P.op("pool", lambda e: e.tensor_tensor(out=gg[tk][:], in0=gg[tk][:], in1=gbc[:], op=ALU.mult),
                 reads=[b_gbc], writes=[b_gg[tk]])

        for b in range(ns):
            tile_loads(b, 0, b)
        pending = []
        for ti in range(ntile):
            tks = [(ti % 2) * ns + b for b in range(ns)]
            if ti + 1 < ntile:
                for b in range(ns):
                    tile_loads(b, ti + 1, ((ti + 1) % 2) * ns + b)
            gens = [head_stream(b, h, ti, tks[b]) for h in range(4) for b in range(ns)]
            alive = list(gens)
            rnd = 0
            while alive:
                nxt = []
                for g in alive:
                    try:
                        next(g)
                        nxt.append(g)
                    except StopIteration:
                        pass
                alive = nxt
                rnd += 1
                if rnd == 4 and pending:
                    for args in pending:
                        tile_tail(*args)
                    pending = []
            for args in pending:
                tile_tail(*args)
            pending = [(b, ti, tks[b]) for b in range(ns)]
        for args in pending:
            tile_tail(*args)
        P.run()

    def rstd_from_var(self, P, mv, b_mv, rstd, b_rstd, n):
        P.op("dve", lambda e: e.tensor_scalar(out=rstd[:, 0:n], in0=mv[:, 0:n, 1], scalar1=LN_EPS, scalar2=None,
                                               op0=ALU.add), reads=[b_mv], writes=[b_rstd])
        P.op("act", lambda e: e.activation(out=rstd[:, 0:n], in_=rstd[:, 0:n], func=AF.Ln), reads=[b_rstd], writes=[b_rstd])
        P.op("act", lambda e: e.activation(out=rstd[:, 0:n], in_=rstd[:, 0:n], func=AF.Exp, scale=-0.5),
             reads=[b_rstd], writes=[b_rstd])

    def router_top2(self, P, lps, b_lps, rt, b_rt, comb, b_c, i):
        L, m1, k1, L2, m2, k2, dd, p1 = (rt[:, q, :] for q in range(8))
        P.op("dve", lambda e: e.tensor_copy(out=L, in_=lps[:, 0:NE]), reads=[b_lps], writes=[b_rt])
        P.op("dve", lambda e: e.tensor_reduce(out=m1[:, 0:1], in_=L, axis=AX.X, op=ALU.max), reads=[b_rt], writes=[b_rt])
        P.op("dve", lambda e: e.tensor_scalar(out=k1, in0=L, scalar1=m1[:, 0:1], scalar2=None, op0=ALU.is_equal),
             reads=[b_rt], writes=[b_rt])
        P.op("dve", lambda e: e.scalar_tensor_tensor(out=L2, in0=k1, scalar=-1e30, in1=L, op0=ALU.mult, op1=ALU.add),
             reads=[b_rt], writes=[b_rt])
        P.op("dve", lambda e: e.tensor_reduce(out=m2[:, 0:1], in_=L2, axis=AX.X, op=ALU.max), reads=[b_rt], writes=[b_rt])
        P.op("dve", lambda e: e.tensor_scalar(out=k2, in0=L2, scalar1=m2[:, 0:1], scalar2=None, op0=ALU.is_equal),
             reads=[b_rt], writes=[b_rt])
        P.op("dve", lambda e: e.tensor_tensor(out=dd[:, 0:1], in0=m2[:, 0:1], in1=m1[:, 0:1], op=ALU.subtract),
             reads=[b_rt], writes=[b_rt])
        P.op("act", lambda e: e.activation(out=dd[:, 1:2], in_=dd[:, 0:1], func=AF.Exp), reads=[b_rt], writes=[b_rt])
        P.op("dve", lambda e: e.tensor_scalar(out=dd[:, 2:3], in0=dd[:, 1:2], scalar1=1.0, scalar2=None, op0=ALU.add),
             reads=[b_rt], writes=[b_rt])
        P.op("dve", lambda e: e.reciprocal(out=p1[:, 0:1], in_=dd[:, 2:3]), reads=[b_rt], writes=[b_rt])
        P.op("dve", lambda e: e.tensor_scalar(out=p1[:, 1:2], in0=p1[:, 0:1], scalar1=-1.0, scalar2=1.0,
                                               op0=ALU.mult, op1=ALU.add), reads=[b_rt], writes=[b_rt])
        P.op("dve", lambda e: e.tensor_scalar(out=comb[:, i, :], in0=k1, scalar1=p1[:, 0:1], scalar2=None, op0=ALU.mult),
             reads=[b_rt], writes=[b_c])
        P.op("dve", lambda e: e.scalar_tensor_tensor(out=comb[:, i, :], in0=k2, scalar=p1[:, 1:2], in1=comb[:, i, :],
                                                      op0=ALU.mult, op1=ALU.add), reads=[b_rt], writes=[b_c])


def const_tables():
    s = np.arange(128)[:, None]
    t = np.arange(128)[None, :]
    c = np.zeros((128, 5, 128), np.float32)
    c[:, 0, :] = (s == t)
    c[:, 1, :] = ((s // CHUNK) == (t // CHUNK)) & (s <= t)
    c[:, 2, :] = (s < t)
    c[:, 3, :] = -1.0 * (s >= t)
    c[:, 4, :] = -1.0 * (t == 64)
    return c


def core_inputs(inp, b0, nseq, S):
    f = lambda a: np.ascontiguousarray(np.asarray(a, dtype=np.float32))
    x = f(inp["x"])[b0:b0 + nseq, :S].reshape(nseq * S, D)
    c = f(inp["c"])[b0:b0 + nseq]
    cT = np.ascontiguousarray(c.T.reshape(8, 128, nseq).transpose(1, 0, 2))
    lb = f(inp["hgrn_lb_logits"])
    lbT = np.ascontiguousarray(lb.T.reshape(NH_H, 128, DEPTH).transpose(1, 0, 2))
    m = {
        "x": np.ascontiguousarray(x), "cT": cT, "lbT": lbT, "consts": const_tables(),
        "w_ada": f(inp["w_ada"]), "b_ada": f(inp["b_ada"]), "w_in": f(inp["w_in"]), "w_out": f(inp["w_out"]),
        "hgain": f(inp["hgrn_norm_gain"]),
        "w_dense_gate": f(inp["w_dense_gate"]), "w_dense_up": f(inp["w_dense_up"]),
        "w_dense_down": f(inp["w_dense_down"]),
        "w_router": np.ascontiguousarray(f(inp["w_router"]).reshape(2, 8, 128, NE).transpose(0, 2, 1, 3)),
        "w_moe_gate": f(inp["w_moe_gate"]), "w_moe_up": f(inp["w_moe_up"]), "w_moe_down": f(inp["w_moe_down"]),
        "ln_gain": f(inp["ln_gain"]), "ln_bias": f(inp["ln_bias"]),
    }
    return m


def kernel(**inputs):
    cfg = Cfg()
    nc = Builder(cfg).build()
    in_maps = [core_inputs(inputs, 2 * c, 2, 4096) for c in range(NCORES)]
    res = run_bass_kernel_spmd(nc, in_maps, core_ids=list(range(NCORES)))
    outs = [np.asarray(r["out"], dtype=np.float32).reshape(2, 4096, D) for r in res.results]
    return np.concatenate(outs, axis=0)
```

```python
import contextlib
import numpy as np
import ml_dtypes
import concourse.bass as bass
import concourse.mybir as mybir
from concourse.bass_utils import run_bass_kernel_spmd

F32 = mybir.dt.float32
BF16 = mybir.dt.bfloat16
AF = mybir.ActivationFunctionType
ALU = mybir.AluOpType
AX = mybir.AxisListType

D = 1024
DEPTH = 4
DH = 512
NH_H = 4
DSB = 512
NH_S = 8
DIN = 3584
FF_DENSE = 2816
FF_MOE = 3584
NE = 8
ALPHA = float((2 * DEPTH) ** 0.25)
LN_EPS = 1e-5
RMS_EPS = 1e-6
CHUNK = 64
NCORES = 8
DBG = {}


class Buf:
    __slots__ = ("name", "w", "r", "dsem", "dcnt", "psum")

    def __init__(self, name="", psum=False):
        self.name = name
        self.psum = psum
        self.w = None
        self.r = []
        self.dsem = None
        self.dcnt = 0


class _Eng:
    def __init__(self, name, sem):
        self.name = name
        self.sem = sem
        self.count = 0
        self.ops = []
        self.waited = {}


class Prog:
    ENG = ("pe", "act", "dve", "pool", "sp")

    def __init__(self, nc, name):
        self.nc = nc
        self.name = name
        self.stack = contextlib.ExitStack()
        self.eng = {}
        self.sems = []
        for e in self.ENG:
            sem = nc.alloc_semaphore(name=f"{name}_{e}")
            self.sems.append(sem)
            self.eng[e] = _Eng(e, sem)
        self.dma_toks = []
        self.nsem = 5

    def sbuf(self, name, shape, dt):
        return self.stack.enter_context(self.nc.sbuf_tensor(f"{self.name}_{name}", list(shape), dt))

    def psum(self, name, shape, dt=F32):
        return self.stack.enter_context(self.nc.psum_tensor(f"{self.name}_{name}", list(shape), dt))

    def _wait(self, e, tok):
        if tok is None:
            return
        sem, val, owner = tok
        if owner == "pe" and e.name == "pe":
            return
        key = id(sem)
        if e.waited.get(key, 0) >= val:
            return
        e.waited[key] = val
        e.ops.append(lambda eng, s=sem, v=val: eng.wait_ge(s, v))

    def _deps(self, e, reads, writes, extra):
        for b in reads:
            self._wait(e, b.w)
            if b.psum:
                for t in b.r:
                    if t[2] != e.name:
                        self._wait(e, t)
        for b in writes:
            self._wait(e, b.w)
            for t in b.r:
                self._wait(e, t)
        for t in extra:
            self._wait(e, t)

    def op(self, engine, fn, reads=(), writes=(), extra=(), signal=True):
        e = self.eng[engine]
        self._deps(e, reads, writes, extra)
        if signal:
            e.count += 1
            tok = (e.sem, e.count, engine)
            e.ops.append(lambda eng, f=fn, s=e.sem: f(eng).then_inc(s, 1))
        else:
            tok = (e.sem, e.count + 1, engine)
            e.ops.append(lambda eng, f=fn: f(eng))
        for b in writes:
            b.w = tok
            b.r = []
        for b in reads:
            b.r.append(tok)
        return tok

    def dma(self, queue, out, in_, owner, reads=(), writes=(), extra=(), **kw):
        e = self.eng[queue]
        self._deps(e, reads, writes, extra)
        if owner.dsem is None:
            owner.dsem = self.nc.alloc_semaphore(name=f"{self.name}_d{self.nsem}")
            self.sems.append(owner.dsem)
            self.nsem += 1
            owner.dcnt = 0
        owner.dcnt += 16
        tok = (owner.dsem, owner.dcnt, "dma")
        e.ops.append(lambda eng, o=out, i=in_, s=owner.dsem, k=kw: eng.dma_start(out=o, in_=i, **k).then_inc(s, 16))
        for b in writes:
            b.w = tok
            b.r = []
        for b in reads:
            b.r.append(tok)
        self.dma_toks.append(tok)
        return tok

    def run(self):
        nc = self.nc
        sp = self.eng["sp"]
        for t in self.dma_toks:
            self._wait(sp, t)
        with nc.Block() as block:
            @block.tensor
            def _(t):
                for f in self.eng["pe"].ops:
                    f(t)

            @block.scalar
            def _(a):
                for f in self.eng["act"].ops:
                    f(a)

            @block.vector
            def _(v):
                for f in self.eng["dve"].ops:
                    f(v)

            @block.gpsimd
            def _(g):
                for f in self.eng["pool"].ops:
                    f(g)

            @block.sync
            def _(s):
                for f in self.eng["sp"].ops:
                    f(s)
        nc.clear_and_free_semaphores(self.sems)
        nc.all_engine_barrier()
        self.stack.close()


class Cfg:
    def __init__(self, nseq=2, S=4096, layers=(0, 1, 2, 3), debug=()):
        self.nseq = nseq
        self.S = S
        self.ntok = nseq * S
        self.layers = tuple(layers)
        self.debug = tuple(debug)


def bcast_rows(ap1d, nparts):
    return ap1d.partition_broadcast(nparts)


class Builder:
    def __init__(self, cfg):
        self.cfg = cfg
        nc = self.nc = bass.Bass("TRN2", target_bir_lowering=False)
        NT = cfg.ntok
        ns = cfg.nseq

        def din(name, shape, dt=F32):
            return nc.dram_tensor(name, list(shape), dt, kind="ExternalInput").ap()

        def scratch(name, shape, dt=F32):
            kind = "ExternalOutput" if name in cfg.debug else "Internal"
            return nc.dram_tensor(name, list(shape), dt, kind=kind).ap()

        self.x_in = din("x", [NT, D])
        self.cT = din("cT", [128, 8, ns])
        self.w_ada = din("w_ada", [DEPTH, 2, D, 3 * D])
        self.b_ada = din("b_ada", [DEPTH, 2, 3 * D])
        self.w_in = din("w_in", [DEPTH, D, DIN])
        self.w_out = din("w_out", [DEPTH, D, D])
        self.lbT = din("lbT", [128, NH_H, DEPTH])
        self.hgain = din("hgain", [DEPTH, DH])
        self.wdg = din("w_dense_gate", [2, D, FF_DENSE])
        self.wdu = din("w_dense_up", [2, D, FF_DENSE])
        self.wdd = din("w_dense_down", [2, FF_DENSE, D])
        self.w_router = din("w_router", [2, 128, 8, NE])
        self.wmg = din("w_moe_gate", [2, NE, D, FF_MOE])
        self.wmu = din("w_moe_up", [2, NE, D, FF_MOE])
        self.wmd = din("w_moe_down", [2, NE, FF_MOE, D])
        self.ln_gain = din("ln_gain", [DEPTH, 2, D])
        self.ln_bias = din("ln_bias", [DEPTH, 2, D])
        self.consts = din("consts", [128, 5, 128])

        self.out = nc.dram_tensor("out", [NT, D], F32, kind="ExternalOutput").ap()
        self.xa = scratch("xa", [NT, D])
        self.xb = scratch("xb", [NT, D])
        self.mod = scratch("mod", [ns, 8, 3 * D])
        self.hqT = scratch("hqT", [DH, NT])
        self.hfT = scratch("hfT", [DH, NT])
        self.hi_tm = scratch("hi_tm", [NT, DH], BF16)
        self.hg_tm = scratch("hg_tm", [NT, DH])
        self.sqT = scratch("sqT", [DSB, NT], BF16)
        self.skT = scratch("skT", [DSB, NT], BF16)
        self.sv_tm = scratch("sv_tm", [NT, DSB], BF16)
        self.oT = scratch("oT", [D, NT], BF16)

    def dump(self, P, name, ap, buf, shape, dt=F32):
        if name not in self.cfg.debug:
            return
        t = self.nc.dram_tensor(name, list(shape), dt, kind="ExternalOutput").ap()
        P.dma("sp", t, ap, buf, reads=[buf])

    def build(self):
        nc = self.nc
        cfg = self.cfg
        with contextlib.ExitStack() as top:
            def pt(name, shape, dt):
                return top.enter_context(nc.sbuf_tensor(name, list(shape), dt))
            self.identf = pt("identf", [128, 128], F32)
            self.identb = pt("identb", [128, 128], BF16)
            self.mask_bd = pt("mask_bd", [128, 128], F32)
            self.mask_st = pt("mask_st", [128, 128], F32)
            self.tri_neg = pt("tri_neg", [128, 128], BF16)
            self.neg_col = pt("neg_col", [128, 128], BF16)
            self.lbA = pt("lbA", [128, NH_H * DEPTH], F32)
            self.lbB = pt("lbB", [128, NH_H * DEPTH], F32)
            self.ones_f = pt("ones_f", [128, 128], F32)
            self.zeros_b = pt("zeros_b", [128, 512], BF16)
            self.ones_b = pt("ones_b", [128, 128], BF16)

            self.phase_setup()
            xcur = self.x_in
            for li, l in enumerate(cfg.layers):
                last = li == len(cfg.layers) - 1
                self.phase_inproj(l, xcur)
                self.phase_hgrn(l)
                self.phase_sb(l)
                self.phase_outproj(l, xcur, self.xa)
                self.phase_ffn(l, self.xa, self.out if last else self.xb)
                xcur = self.xb
        return nc

    def phase_setup(self):
        nc, cfg = self.nc, self.cfg
        P = Prog(nc, "p0")
        ns = cfg.nseq
        cst = P.sbuf("cst", [128, 5, 128], F32)
        b_cst = Buf("cst")
        P.dma("sp", cst[:], self.consts, b_cst, writes=[b_cst])
        bp = Buf("persist")
        P.op("dve", lambda e: e.tensor_copy(out=self.identf[:], in_=cst[:, 0, :]), reads=[b_cst], writes=[bp])
        P.op("dve", lambda e: e.tensor_copy(out=self.identb[:], in_=cst[:, 0, :]), reads=[b_cst], writes=[bp])
        P.op("dve", lambda e: e.tensor_copy(out=self.mask_bd[:], in_=cst[:, 1, :]), reads=[b_cst], writes=[bp])
        P.op("dve", lambda e: e.tensor_copy(out=self.mask_st[:], in_=cst[:, 2, :]), reads=[b_cst], writes=[bp])
        P.op("dve", lambda e: e.tensor_copy(out=self.tri_neg[:], in_=cst[:, 3, :]), reads=[b_cst], writes=[bp])
        P.op("dve", lambda e: e.tensor_copy(out=self.neg_col[:], in_=cst[:, 4, :]), reads=[b_cst], writes=[bp])
        P.op("dve", lambda e: e.memset(self.ones_f[:], 1.0), writes=[bp])
        P.op("dve", lambda e: e.memset(self.ones_b[:], 1.0), writes=[bp])
        P.op("dve", lambda e: e.memset(self.zeros_b[:], 0.0), writes=[bp])

        lg = P.sbuf("lg", [128, NH_H, DEPTH], F32)
        ex = P.sbuf("ex", [128, NH_H, DEPTH], F32)
        sm = P.sbuf("sm", [128, NH_H], F32)
        lbt = P.sbuf("lbt", [128, NH_H, DEPTH], F32)
        b_lg, b_ex, b_sm, b_lb = Buf(), Buf(), Buf(), Buf()
        P.dma("sp", lg[:], self.lbT, b_lg, writes=[b_lg])
        P.op("act", lambda e: e.activation(out=ex[:], in_=lg[:], func=AF.Exp), reads=[b_lg], writes=[b_ex])
        P.op("dve", lambda e: e.tensor_reduce(out=sm[:], in_=ex[:], axis=AX.X, op=ALU.add), reads=[b_ex], writes=[b_sm])
        P.op("dve", lambda e: e.reciprocal(out=sm[:], in_=sm[:]), reads=[b_sm], writes=[b_sm])
        for h in range(NH_H):
            P.op("dve", lambda e, h=h: e.tensor_scalar(out=ex[:, h, :], in0=ex[:, h, :], scalar1=sm[:, h:h + 1],
                                                        scalar2=None, op0=ALU.mult),
                 reads=[b_sm, b_ex], writes=[b_ex])
        P.op("dve", lambda e: e.memset(lbt[:, :, 0:1], 0.0), writes=[b_lb])
        for l in range(1, DEPTH):
            P.op("dve", lambda e, l=l: e.tensor_tensor(out=lbt[:, :, l:l + 1], in0=lbt[:, :, l - 1:l],
                                                        in1=ex[:, :, l:l + 1], op=ALU.add),
                 reads=[b_ex, b_lb], writes=[b_lb])
        for l in range(DEPTH):
            P.op("dve", lambda e, l=l: e.tensor_scalar(out=self.lbA[:, l * 4:(l + 1) * 4], in0=lbt[:, :, l],
                                                        scalar1=-0.5, scalar2=0.5, op0=ALU.mult, op1=ALU.add),
                 reads=[b_lb], writes=[bp])
            P.op("dve", lambda e, l=l: e.tensor_scalar(out=self.lbB[:, l * 4:(l + 1) * 4], in0=lbt[:, :, l],
                                                        scalar1=0.5, scalar2=0.5, op0=ALU.mult, op1=ALU.add),
                 reads=[b_lb], writes=[bp])

        ct = P.sbuf("ct", [128, 8, ns], F32)
        sct = P.sbuf("sct", [128, 8, ns], F32)
        b_ct, b_sct = Buf(), Buf()
        P.dma("sp", ct[:], self.cT, b_ct, writes=[b_ct])
        P.op("act", lambda e: e.activation(out=sct[:], in_=ct[:], func=AF.Exp, scale=-1.0), reads=[b_ct], writes=[b_sct])
        P.op("dve", lambda e: e.tensor_scalar(out=sct[:], in0=sct[:], scalar1=1.0, scalar2=None, op0=ALU.add),
             reads=[b_sct], writes=[b_sct])
        P.op("dve", lambda e: e.reciprocal(out=sct[:], in_=sct[:]), reads=[b_sct], writes=[b_sct])
        P.op("dve", lambda e: e.tensor_tensor(out=sct[:], in0=sct[:], in1=ct[:], op=ALU.mult),
             reads=[b_sct, b_ct], writes=[b_sct])
        wbuf = [P.sbuf(f"wa{i}", [128, 8, 512], F32) for i in range(2)]
        b_w = [Buf(), Buf()]
        bias = [P.sbuf(f"bias{i}", [ns, 3 * D], F32) for i in range(2)]
        b_bias = [Buf(), Buf()]
        mt = [P.sbuf(f"mt{i}", [ns, 3 * D], F32) for i in range(2)]
        b_mt = [Buf(), Buf()]
        ps = [P.psum(f"ps{i}", [128, 512]) for i in range(2)]
        b_ps = [Buf(psum=True), Buf(psum=True)]
        k = 0
        for ls in range(8):
            l, s = divmod(ls, 2)
            if l not in cfg.layers:
                continue
            bi = ls % 2
            P.dma("sp", bias[bi][:], bcast_rows(self.b_ada[l, s, :], ns), b_bias[bi], writes=[b_bias[bi]])
            for n in range(6):
                wi = k % 2
                k += 1
                src = self.w_ada[l, s, :, n * 512:(n + 1) * 512].rearrange("(c p) f -> p c f", p=128)
                P.dma("sp", wbuf[wi][:], src, b_w[wi], writes=[b_w[wi]])
                for dc in range(8):
                    P.op("pe", lambda e, wi=wi, dc=dc: e.matmul(ps[wi][0:ns, :], sct[:, dc, :], wbuf[wi][:, dc, :],
                                                                  start=(dc == 0), stop=(dc == 7)),
                         reads=[b_sct, b_w[wi]], writes=[b_ps[wi]], signal=(dc == 7))
                P.op("dve", lambda e, wi=wi, bi=bi, n=n: e.tensor_tensor(
                    out=mt[bi][:, n * 512:(n + 1) * 512], in0=ps[wi][0:ns, :], in1=bias[bi][:, n * 512:(n + 1) * 512],
                    op=ALU.add), reads=[b_ps[wi], b_bias[bi]], writes=[b_mt[bi]])
            P.op("dve", lambda e, bi=bi: e.tensor_scalar(out=mt[bi][:, D:2 * D], in0=mt[bi][:, D:2 * D], scalar1=1.0,
                                                          scalar2=None, op0=ALU.add), reads=[b_mt[bi]], writes=[b_mt[bi]])
            P.dma("sp", self.mod[:, ls, :], mt[bi][:], b_mt[bi], reads=[b_mt[bi]])
        P.run()

    def load_bcast(self, P, tile, buf, src1d):
        P.dma("sp", tile[:], bcast_rows(src1d, 128), buf, writes=[buf])

    def phase_ffn(self, l, x_src, x_dst):
        nc, cfg = self.nc, self.cfg
        moe = (l % 2 == 1)
        j = l // 2
        P = Prog(nc, f"f{l}")
        TB = 1024
        NTI = TB // 128
        nblk = cfg.ntok // TB
        blk_per_seq = cfg.S // TB
        FF = FF_MOE if moe else FF_DENSE
        nfc = FF // 128
        groups = [(g0, min(4, nfc - g0)) for g0 in range(0, nfc, 4)]
        nexp = NE if moe else 1
        nexp = DBG.get('nexp', nexp)

        sc1 = P.sbuf("sc1", [128, D], F32); sh = P.sbuf("sh", [128, D], F32); gt = P.sbuf("gt", [128, D], F32)
        lng = P.sbuf("lng", [128, D], F32); lnb = P.sbuf("lnb", [128, D], F32)
        b_mod, b_ln = Buf("mod"), Buf("ln")
        xt = [P.sbuf(f"xt{i}", [128, D], F32) for i in range(2)]
        b_xt = [Buf(), Buf()]
        hf = [P.sbuf(f"hf{i}", [128, D], F32) for i in range(2)]
        b_hf = [Buf(), Buf()]
        hT = P.sbuf("hT", [128, 8, TB], BF16)
        b_hT = [Buf() for _ in range(NTI)]
        yacc = P.sbuf("yacc", [128, NTI, D], F32)
        b_y = [[Buf(), Buf()] for _ in range(NTI)]
        wg = [P.sbuf(f"wg{i}", [128, 8, 512], BF16) for i in range(2)]
        wu = [P.sbuf(f"wu{i}", [128, 8, 512], BF16) for i in range(2)]
        wd = [P.sbuf(f"wd{i}", [128, 4, D], BF16) for i in range(2)]
        b_wgu = [Buf(), Buf()]
        b_wd = [Buf(), Buf()]
        aT = [P.sbuf(f"aT{i}", [128, 4, TB], BF16) for i in range(2)]
        b_aT = [[Buf(), Buf()] for _ in range(2)]
        sg = [P.sbuf(f"sg{i}", [128, 512], F32) for i in range(2)]
        b_sg = [Buf(), Buf()]
        ot = [P.sbuf(f"ot{i}", [128, D], F32) for i in range(2)]
        b_ot = [Buf(), Buf()]
        stats = P.sbuf("stats", [128, NTI, 2, 6], F32)
        mv = P.sbuf("mv", [128, NTI, 2], F32)
        rstd = P.sbuf("rstd", [128, NTI], F32)
        b_stats = [Buf() for _ in range(NTI)]
        b_mv = Buf(); b_rstd = Buf()
        if moe:
            hTf = P.sbuf("hTf", [128, 8, 128], F32); b_hTf = Buf()
            wr = P.sbuf("wr", [128, 8, NE], F32); b_wr = Buf()
            comb = P.sbuf("comb", [128, NTI, NE], F32); b_comb = [Buf() for _ in range(NTI)]
            rt = P.sbuf("rt", [128, 8, NE], F32); b_rt = Buf()
            if DBG.get("router", 1):
                P.dma("sp", wr[:], self.w_router[j], b_wr, writes=[b_wr])
        psT = [P.psum(f"psT{i}", [128, 512]) for i in range(2)]; b_psT = [Buf(psum=True), Buf(psum=True)]
        gps = [P.psum(f"gps{i}", [128, 512]) for i in range(2)]; b_gps = [Buf(psum=True), Buf(psum=True)]
        ups = [P.psum(f"ups{i}", [128, 512]) for i in range(2)]; b_ups = [Buf(psum=True), Buf(psum=True)]
        yps = [P.psum(f"yps{i}", [128, 512]) for i in range(2)]; b_yps = [Buf(psum=True), Buf(psum=True)]

        P.dma("sp", lng[:], bcast_rows(self.ln_gain[l, 1, :], 128), b_ln, writes=[b_ln])
        P.dma("sp", lnb[:], bcast_rows(self.ln_bias[l, 1, :], 128), b_ln, writes=[b_ln])

        if moe:
            WG, WU, WD = self.wmg[j], self.wmu[j], self.wmd[j]
        else:
            WG, WU, WD = self.wdg[j:j + 1], self.wdu[j:j + 1], self.wdd[j:j + 1]

        gcount = 0
        xk = 0
        for blk in range(nblk):
            t0 = blk * TB
            bseq = blk // blk_per_seq
            if blk % blk_per_seq == 0:
                ls = l * 2 + 1
                P.dma("sp", sh[:], bcast_rows(self.mod[bseq, ls, 0:D], 128), b_mod, writes=[b_mod])
                P.dma("sp", sc1[:], bcast_rows(self.mod[bseq, ls, D:2 * D], 128), b_mod, writes=[b_mod])
                P.dma("sp", gt[:], bcast_rows(self.mod[bseq, ls, 2 * D:3 * D], 128), b_mod, writes=[b_mod])
            for i in range(NTI):
                xi = xk % 2
                xk += 1
                r0 = t0 + i * 128
                P.dma("sp", xt[xi][:], x_src[r0:r0 + 128, :], b_xt[xi], writes=[b_xt[xi]])
                P.op("dve", lambda e, xi=xi: e.tensor_tensor(out=hf[xi][:], in0=xt[xi][:], in1=sc1[:], op=ALU.mult),
                     reads=[b_xt[xi], b_mod], writes=[b_hf[xi]])
                P.op("pool", lambda e, xi=xi: e.tensor_tensor(out=hf[xi][:], in0=hf[xi][:], in1=sh[:], op=ALU.add),
                     reads=[b_hf[xi], b_mod], writes=[b_hf[xi]])
                for hb in range(2):
                    for q in range(4):
                        dc = hb * 4 + q
                        P.op("pe", lambda e, xi=xi, hb=hb, q=q, dc=dc: e.transpose(
                            psT[hb][:, q * 128:(q + 1) * 128], hf[xi][:, dc * 128:(dc + 1) * 128], self.identf[:]),
                            reads=[b_hf[xi]], writes=[b_psT[hb]], signal=(q == 3))
                    if moe:
                        P.op("act", lambda e, hb=hb: e.activation(
                            out=hTf[:, hb * 4:(hb + 1) * 4, :], in_=psT[hb][:].rearrange("p (c t) -> p c t", t=128),
                            func=AF.Copy), reads=[b_psT[hb]], writes=[b_hTf])
                    P.op("act" if not moe else "dve", lambda e, hb=hb, i=i: e.tensor_copy(
                        out=hT[:, hb * 4:(hb + 1) * 4, i * 128:(i + 1) * 128],
                        in_=psT[hb][:].rearrange("p (c t) -> p c t", t=128)) if moe else e.activation(
                        out=hT[:, hb * 4:(hb + 1) * 4, i * 128:(i + 1) * 128],
                        in_=psT[hb][:].rearrange("p (c t) -> p c t", t=128), func=AF.Copy),
                        reads=[b_psT[hb]], writes=[b_hT[i]])
                if moe and DBG.get("router", 1) == 0:
                    P.op("dve", lambda e, i=i: e.memset(comb[:, i, :], 0.125), writes=[b_comb[i]])
                if moe and DBG.get("router", 1):
                    for dc in range(8):
                        P.op("pe", lambda e, dc=dc: e.matmul(yps[1][:, 0:NE], hTf[:, dc, :], wr[:, dc, :],
                                                              start=(dc == 0), stop=(dc == 7)),
                             reads=[b_hTf, b_wr], writes=[b_yps[1]], signal=(dc == 7))
                    self.router_top2(P, yps[1], b_yps[1], rt, b_rt, comb, b_comb[i], i)
            for ex in range(nexp):
                for (g0, ng) in groups:
                    slot = gcount % 2
                    gcount += 1
                    f0 = g0 * 128
                    fw = ng * 128
                    P.dma("pool", wg[slot][:, :, 0:fw], WG[ex, :, f0:f0 + fw].rearrange("(c p) f -> p c f", p=128),
                          b_wgu[slot], writes=[b_wgu[slot]])
                    P.dma("pool", wu[slot][:, :, 0:fw], WU[ex, :, f0:f0 + fw].rearrange("(c p) f -> p c f", p=128),
                          b_wgu[slot], writes=[b_wgu[slot]])
                    P.dma("pool", wd[slot][:, 0:ng, :], WD[ex, f0:f0 + fw, :].rearrange("(c p) d -> p c d", p=128),
                          b_wd[slot], writes=[b_wd[slot]])
                    for fc in range(ng):
                        for half in range(2):
                            pb = (fc * 2 + half) % 2
                            for dc in range(8):
                                P.op("pe", lambda e, slot=slot, fc=fc, half=half, dc=dc, pb=pb: e.matmul(
                                    gps[pb][:], wg[slot][:, dc, fc * 128:(fc + 1) * 128],
                                    hT[:, dc, half * 512:(half + 1) * 512], start=(dc == 0), stop=(dc == 7)),
                                    reads=[b_wgu[slot]] + b_hT[half * 4:(half + 1) * 4], writes=[b_gps[pb]],
                                    signal=(dc == 7))
                            for dc in range(8):
                                P.op("pe", lambda e, slot=slot, fc=fc, half=half, dc=dc, pb=pb: e.matmul(
                                    ups[pb][:], wu[slot][:, dc, fc * 128:(fc + 1) * 128],
                                    hT[:, dc, half * 512:(half + 1) * 512], start=(dc == 0), stop=(dc == 7)),
                                    reads=[b_wgu[slot]] + b_hT[half * 4:(half + 1) * 4], writes=[b_ups[pb]],
                                    signal=(dc == 7))
                            P.op("act", lambda e, pb=pb: e.activation(out=sg[pb][:], in_=gps[pb][:], func=AF.Silu),
                                 reads=[b_gps[pb]], writes=[b_sg[pb]])
                            P.op("dve", lambda e, pb=pb, slot=slot, fc=fc, half=half: e.tensor_tensor(
                                out=aT[slot][:, fc, half * 512:(half + 1) * 512], in0=ups[pb][:], in1=sg[pb][:],
                                op=ALU.mult), reads=[b_ups[pb], b_sg[pb]], writes=[b_aT[slot][half]])
                    first = (ex == 0 and g0 == 0)
                    for i in range(NTI):
                        for h2 in range(2):
                            pb = (i * 2 + h2) % 2
                            for fc in range(ng):
                                P.op("pe", lambda e, slot=slot, fc=fc, i=i, h2=h2, pb=pb, ng=ng: e.matmul(
                                    yps[pb][:], aT[slot][:, fc, i * 128:(i + 1) * 128],
                                    wd[slot][:, fc, h2 * 512:(h2 + 1) * 512], start=(fc == 0), stop=(fc == ng - 1)),
                                    reads=[b_aT[slot][i // 4], b_wd[slot]], writes=[b_yps[pb]], signal=(fc == ng - 1))
                            ysl = yacc[:, i, h2 * 512:(h2 + 1) * 512]
                            if moe:
                                csc = comb[:, i, ex:ex + 1]
                                if first:
                                    P.op("dve", lambda e, pb=pb, ysl=ysl, csc=csc: e.tensor_scalar(
                                        out=ysl, in0=yps[pb][:], scalar1=csc, scalar2=None, op0=ALU.mult),
                                        reads=[b_yps[pb], b_comb[i]], writes=[b_y[i][h2]])
                                else:
                                    P.op("dve", lambda e, pb=pb, ysl=ysl, csc=csc: e.scalar_tensor_tensor(
                                        out=ysl, in0=yps[pb][:], scalar=csc, in1=ysl, op0=ALU.mult, op1=ALU.add),
                                        reads=[b_yps[pb], b_comb[i]], writes=[b_y[i][h2]])
                            else:
                                if first:
                                    P.op("act", lambda e, pb=pb, ysl=ysl: e.activation(out=ysl, in_=yps[pb][:], func=AF.Copy),
                                         reads=[b_yps[pb]], writes=[b_y[i][h2]])
                                else:
                                    P.op("dve", lambda e, pb=pb, ysl=ysl: e.tensor_tensor(
                                        out=ysl, in0=yps[pb][:], in1=ysl, op=ALU.add),
                                        reads=[b_yps[pb]], writes=[b_y[i][h2]])
            for i in range(NTI):
                xi = xk % 2
                xk += 1
                r0 = t0 + i * 128
                P.dma("sp", xt[xi][:], x_src[r0:r0 + 128, :], b_xt[xi], writes=[b_xt[xi]])
                P.op("pool", lambda e, i=i: e.tensor_tensor(out=yacc[:, i, :], in0=yacc[:, i, :], in1=gt[:], op=ALU.mult),
                     reads=[b_mod], writes=b_y[i])
                P.op("dve", lambda e, i=i, xi=xi: e.scalar_tensor_tensor(
                    out=yacc[:, i, :], in0=xt[xi][:], scalar=ALPHA, in1=yacc[:, i, :], op0=ALU.mult, op1=ALU.add),
                    reads=[b_xt[xi]], writes=b_y[i])
                for h2 in range(2):
                    P.op("dve", lambda e, i=i, h2=h2: e.bn_stats(out=stats[:, i, h2, :], in_=yacc[:, i, h2 * 512:(h2 + 1) * 512]),
                         reads=b_y[i], writes=[b_stats[i]])
                P.op("dve", lambda e, i=i: e.bn_aggr(out=mv[:, i, :], in_=stats[:, i, :, :].rearrange("p a b -> p (a b)")),
                     reads=[b_stats[i]], writes=[b_mv])
            self.rstd_from_var(P, mv, b_mv, rstd, b_rstd, NTI)
            for i in range(NTI):
                oi = i % 2
                r0 = t0 + i * 128
                P.op("dve", lambda e, i=i, oi=oi: e.scalar_tensor_tensor(
                    out=ot[oi][:], in0=yacc[:, i, :], scalar=mv[:, i, 0:1], in1=lng[:],
                    op0=ALU.subtract, op1=ALU.mult), reads=b_y[i] + [b_mv, b_ln], writes=[b_ot[oi]])
                P.op("dve", lambda e, i=i, oi=oi: e.scalar_tensor_tensor(
                    out=ot[oi][:], in0=ot[oi][:], scalar=rstd[:, i:i + 1], in1=lnb[:],
                    op0=ALU.mult, op1=ALU.add), reads=[b_rstd, b_ln], writes=[b_ot[oi]])
                P.dma("sp", x_dst[r0:r0 + 128, :], ot[oi][:], b_ot[oi], reads=[b_ot[oi]])
        P.run()

    def phase_inproj(self, l, x_src):
        nc, cfg = self.nc, self.cfg
        P = Prog(nc, f"i{l}")
        TB = 512
        nblk = cfg.ntok // TB
        blk_per_seq = cfg.S // TB
        w = P.sbuf("w", [128, 8, DIN], BF16); b_w = Buf()
        for n in range(7):
            P.dma("pool", w[:, :, n * 512:(n + 1) * 512],
                  self.w_in[l, :, n * 512:(n + 1) * 512].rearrange("(c p) f -> p c f", p=128), b_w, writes=[b_w])
        sc1 = P.sbuf("sc1", [128, D], F32); sh = P.sbuf("sh", [128, D], F32); b_mod = Buf()
        xt = [P.sbuf(f"xt{i}", [128, D], F32) for i in range(2)]; b_xt = [Buf(), Buf()]
        hf = [P.sbuf(f"hf{i}", [128, D], F32) for i in range(2)]; b_hf = [Buf(), Buf()]
        hT = [P.sbuf(f"hT{i}", [128, 8, TB], BF16) for i in range(2)]
        b_hT = [[Buf() for _ in range(4)] for _ in range(2)]
        NST = 4
        stf = [P.sbuf(f"stf{i}", [128, 512], F32) for i in range(NST)]; b_stf = [Buf() for _ in range(NST)]
        stb = [P.sbuf(f"stb{i}", [128, 512], BF16) for i in range(NST)]; b_stb = [Buf() for _ in range(NST)]
        psT = [P.psum(f"psT{i}", [128, 512]) for i in range(2)]; b_psT = [Buf(psum=True), Buf(psum=True)]
        NPS = 6
        mm = [P.psum(f"mm{i}", [128, 512]) for i in range(NPS)]; b_mm = [Buf(psum=True) for _ in range(NPS)]
        xk = 0; kf = 0; kb_ = 0; km = 0
        for blk in range(nblk):
            t0 = blk * TB
            bseq = blk // blk_per_seq
            hs = blk % 2
            if blk % blk_per_seq == 0:
                ls = l * 2
                P.dma("sp", sh[:], bcast_rows(self.mod[bseq, ls, 0:D], 128), b_mod, writes=[b_mod])
                P.dma("sp", sc1[:], bcast_rows(self.mod[bseq, ls, D:2 * D], 128), b_mod, writes=[b_mod])
            for i in range(4):
                xi = xk % 2; xk += 1
                r0 = t0 + i * 128
                P.dma("sp", xt[xi][:], x_src[r0:r0 + 128, :], b_xt[xi], writes=[b_xt[xi]])
                P.op("dve", lambda e, xi=xi: e.tensor_tensor(out=hf[xi][:], in0=xt[xi][:], in1=sc1[:], op=ALU.mult),
                     reads=[b_xt[xi], b_mod], writes=[b_hf[xi]])
                P.op("pool", lambda e, xi=xi: e.tensor_tensor(out=hf[xi][:], in0=hf[xi][:], in1=sh[:], op=ALU.add),
                     reads=[b_hf[xi], b_mod], writes=[b_hf[xi]])
                for hb in range(2):
                    for q in range(4):
                        dc = hb * 4 + q
                        P.op("pe", lambda e, xi=xi, hb=hb, q=q, dc=dc: e.transpose(
                            psT[hb][:, q * 128:(q + 1) * 128], hf[xi][:, dc * 128:(dc + 1) * 128], self.identf[:]),
                            reads=[b_hf[xi]], writes=[b_psT[hb]], signal=(q == 3))
                    P.op("act", lambda e, hb=hb, i=i, hs=hs: e.activation(
                        out=hT[hs][:, hb * 4:(hb + 1) * 4, i * 128:(i + 1) * 128],
                        in_=psT[hb][:].rearrange("p (c t) -> p c t", t=128), func=AF.Copy),
                        reads=[b_psT[hb]], writes=[b_hT[hs][i]])
            for cc in list(range(0, 8)) + list(range(16, 24)):
                pb = km % NPS; km += 1
                for dc in range(8):
                    P.op("pe", lambda e, cc=cc, dc=dc, pb=pb, hs=hs: e.matmul(
                        mm[pb][:], w[:, dc, cc * 128:(cc + 1) * 128], hT[hs][:, dc, :], start=(dc == 0), stop=(dc == 7)),
                        reads=[b_w] + b_hT[hs], writes=[b_mm[pb]], signal=(dc == 7))
                if cc < 8:
                    si = kf % NST; kf += 1
                    if cc < 4:
                        P.op("act", lambda e, pb=pb, si=si: e.activation(out=stf[si][:], in_=mm[pb][:], func=AF.Silu),
                             reads=[b_mm[pb]], writes=[b_stf[si]])
                        dst = self.hqT[cc * 128:(cc + 1) * 128, t0:t0 + TB]
                    else:
                        P.op("act", lambda e, pb=pb, si=si: e.activation(out=stf[si][:], in_=mm[pb][:], func=AF.Tanh, scale=0.5),
                             reads=[b_mm[pb]], writes=[b_stf[si]])
                        dst = self.hfT[(cc - 4) * 128:(cc - 3) * 128, t0:t0 + TB]
                    P.dma("sp", dst, stf[si][:], b_stf[si], reads=[b_stf[si]])
                else:
                    si = kb_ % NST; kb_ += 1
                    sc = 0.125 if cc < 20 else 1.0
                    P.op("dve", lambda e, pb=pb, si=si, sc=sc: e.tensor_scalar(
                        out=stb[si][:], in0=mm[pb][:], scalar1=sc, scalar2=None, op0=ALU.mult),
                        reads=[b_mm[pb]], writes=[b_stb[si]])
                    if cc < 20:
                        dst = self.sqT[(cc - 16) * 128:(cc - 15) * 128, t0:t0 + TB]
                    else:
                        dst = self.skT[(cc - 20) * 128:(cc - 19) * 128, t0:t0 + TB]
                    P.dma("sp", dst, stb[si][:], b_stb[si], reads=[b_stb[si]])
            for i in range(4):
                r0 = t0 + i * 128
                for seg, c0 in (("hi", 1024), ("hg", 1536), ("sv", 3072)):
                    pb = km % NPS; km += 1
                    for dc in range(8):
                        P.op("pe", lambda e, dc=dc, pb=pb, hs=hs, i=i, c0=c0: e.matmul(
                            mm[pb][:], hT[hs][:, dc, i * 128:(i + 1) * 128], w[:, dc, c0:c0 + 512],
                            start=(dc == 0), stop=(dc == 7)),
                            reads=[b_w, b_hT[hs][i]], writes=[b_mm[pb]], signal=(dc == 7))
                    if seg == "hg":
                        si = kf % NST; kf += 1
                        P.op("act", lambda e, pb=pb, si=si: e.activation(out=stf[si][:], in_=mm[pb][:], func=AF.Silu),
                             reads=[b_mm[pb]], writes=[b_stf[si]])
                        P.dma("sp", self.hg_tm[r0:r0 + 128, :], stf[si][:], b_stf[si], reads=[b_stf[si]])
                    else:
                        si = kb_ % NST; kb_ += 1
                        P.op("dve", lambda e, pb=pb, si=si: e.tensor_copy(out=stb[si][:], in_=mm[pb][:]),
                             reads=[b_mm[pb]], writes=[b_stb[si]])
                        dst = self.hi_tm if seg == "hi" else self.sv_tm
                        P.dma("sp", dst[r0:r0 + 128, :], stb[si][:], b_stb[si], reads=[b_stb[si]])
        P.run()

    def phase_outproj(self, l, x_src, x_dst):
        nc, cfg = self.nc, self.cfg
        P = Prog(nc, f"o{l}")
        TB = 512
        nblk = cfg.ntok // TB
        blk_per_seq = cfg.S // TB
        w = P.sbuf("w", [128, 8, D], BF16); b_w = Buf()
        P.dma("pool", w[:], self.w_out[l].rearrange("(c p) f -> p c f", p=128), b_w, writes=[b_w])
        gt = P.sbuf("gt", [128, D], F32); b_mod = Buf()
        lng = P.sbuf("lng", [128, D], F32); lnb = P.sbuf("lnb", [128, D], F32); b_ln = Buf()
        P.dma("sp", lng[:], bcast_rows(self.ln_gain[l, 0, :], 128), b_ln, writes=[b_ln])
        P.dma("sp", lnb[:], bcast_rows(self.ln_bias[l, 0, :], 128), b_ln, writes=[b_ln])
        oTs = [P.sbuf(f"oTs{i}", [128, 8, TB], BF16) for i in range(2)]; b_oTs = [Buf(), Buf()]
        xt = [P.sbuf(f"xt{i}", [128, D], F32) for i in range(2)]; b_xt = [Buf(), Buf()]
        zt = P.sbuf("zt", [128, 4, D], F32); b_z = [Buf() for _ in range(4)]
        ot = [P.sbuf(f"ot{i}", [128, D], F32) for i in range(2)]; b_ot = [Buf(), Buf()]
        stats = P.sbuf("stats", [128, 4, 2, 6], F32); b_stats = [Buf() for _ in range(4)]
        mv = P.sbuf("mv", [128, 4, 2], F32); b_mv = Buf()
        rstd = P.sbuf("rstd", [128, 4], F32); b_rstd = Buf()
        yps = [P.psum(f"yps{i}", [128, 512]) for i in range(4)]; b_yps = [Buf(psum=True) for _ in range(4)]
        xk = 0; kp = 0
        for blk in range(nblk):
            t0 = blk * TB
            bseq = blk // blk_per_seq
            os_ = blk % 2
            if blk % blk_per_seq == 0:
                P.dma("sp", gt[:], bcast_rows(self.mod[bseq, l * 2, 2 * D:3 * D], 128), b_mod, writes=[b_mod])
            P.dma("sp", oTs[os_][:], self.oT[:, t0:t0 + TB].rearrange("(c p) t -> p c t", p=128), b_oTs[os_],
                  writes=[b_oTs[os_]])
            for i in range(4):
                xi = xk % 2; xk += 1
                r0 = t0 + i * 128
                P.dma("sp", xt[xi][:], x_src[r0:r0 + 128, :], b_xt[xi], writes=[b_xt[xi]])
                for h2 in range(2):
                    pb = kp % 4; kp += 1
                    for cc in range(8):
                        P.op("pe", lambda e, cc=cc, pb=pb, os_=os_, i=i, h2=h2: e.matmul(
                            yps[pb][:], oTs[os_][:, cc, i * 128:(i + 1) * 128], w[:, cc, h2 * 512:(h2 + 1) * 512],
                            start=(cc == 0), stop=(cc == 7)), reads=[b_w, b_oTs[os_]], writes=[b_yps[pb]], signal=(cc == 7))
                    P.op("dve", lambda e, pb=pb, i=i, h2=h2: e.tensor_tensor(
                        out=zt[:, i, h2 * 512:(h2 + 1) * 512], in0=yps[pb][:], in1=gt[:, h2 * 512:(h2 + 1) * 512], op=ALU.mult),
                        reads=[b_yps[pb], b_mod], writes=[b_z[i]])
                P.op("dve", lambda e, i=i, xi=xi: e.scalar_tensor_tensor(
                    out=zt[:, i, :], in0=xt[xi][:], scalar=ALPHA, in1=zt[:, i, :], op0=ALU.mult, op1=ALU.add),
                    reads=[b_xt[xi]], writes=[b_z[i]])
                for h2 in range(2):
                    P.op("dve", lambda e, i=i, h2=h2: e.bn_stats(out=stats[:, i, h2, :], in_=zt[:, i, h2 * 512:(h2 + 1) * 512]),
                         reads=[b_z[i]], writes=[b_stats[i]])
                P.op("dve", lambda e, i=i: e.bn_aggr(out=mv[:, i, :], in_=stats[:, i, :, :].rearrange("p a b -> p (a b)")),
                     reads=[b_stats[i]], writes=[b_mv])
            self.rstd_from_var(P, mv, b_mv, rstd, b_rstd, 4)
            for i in range(4):
                oi = i % 2
                r0 = t0 + i * 128
                P.op("dve", lambda e, i=i, oi=oi: e.scalar_tensor_tensor(
                    out=ot[oi][:], in0=zt[:, i, :], scalar=mv[:, i, 0:1], in1=lng[:],
                    op0=ALU.subtract, op1=ALU.mult), reads=[b_z[i], b_mv, b_ln], writes=[b_ot[oi]])
                P.op("dve", lambda e, i=i, oi=oi: e.scalar_tensor_tensor(
                    out=ot[oi][:], in0=ot[oi][:], scalar=rstd[:, i:i + 1], in1=lnb[:],
                    op0=ALU.mult, op1=ALU.add), reads=[b_rstd, b_ln], writes=[b_ot[oi]])
                P.dma("sp", x_dst[r0:r0 + 128, :], ot[oi][:], b_ot[oi], reads=[b_ot[oi]])
        P.run()

    def phase_sb(self, l):
        nc, cfg = self.nc, self.cfg
        P = Prog(nc, f"s{l}")
        S = cfg.S
        NB = S // 128
        NG = S // 512
        ka = [P.sbuf(f"ka{i}", [128, S], BF16) for i in range(2)]; b_ka = [Buf(), Buf()]
        qz = [P.sbuf(f"qz{i}", [128, S], BF16) for i in range(2)]; b_qz = [Buf(), Buf()]
        NQA = 3
        qa = [[P.sbuf(f"qa{i}{j}", [128, S], BF16) for j in range(NQA)] for i in range(2)]
        b_qa = [[Buf() for _ in range(NQA)] for _ in range(2)]
        vall = P.sbuf("vall", [128, NB, DSB], BF16); b_v = Buf()
        e1 = [P.sbuf(f"e1{i}", [128, 512], F32) for i in range(2)]; b_e1 = [Buf(), Buf()]
        NSP = 4
        sp = [P.sbuf(f"sp{i}", [128, 512], BF16) for i in range(NSP)]; b_sp = [Buf() for _ in range(NSP)]
        At = [P.sbuf(f"A{i}", [128, 512], BF16) for i in range(NSP)]; b_A = [Buf() for _ in range(NSP)]
        ost = [P.sbuf(f"ost{i}", [64, 512], BF16) for i in range(2)]; b_ost = [Buf(), Buf()]
        zps = [P.psum(f"z{i}", [128, 512]) for i in range(2)]; b_z = [Buf(psum=True), Buf(psum=True)]
        gps = [P.psum(f"g{i}", [128, 512]) for i in range(2)]; b_g = [Buf(psum=True), Buf(psum=True)]
        cps = [P.psum(f"c{i}", [128, 512]) for i in range(2)]; b_c = [Buf(psum=True), Buf(psum=True)]
        ops_ = [P.psum(f"o{i}", [128, 512]) for i in range(2)]; b_o = [Buf(psum=True), Buf(psum=True)]
        for i in range(2):
            P.op("pool", lambda e, i=i: e.memset(ka[i][64:128, :], 0.0), writes=[b_ka[i]])
            P.op("pool", lambda e, i=i: e.memset(ka[i][64:65, :], 1.0), writes=[b_ka[i]])
            P.op("pool", lambda e, i=i: e.memset(qz[i][64:128, :], 0.0), writes=[b_qz[i]])
            for j in range(NQA):
                P.op("pool", lambda e, i=i, j=j: e.memset(qa[i][j][64:128, :], 0.0), writes=[b_qa[i][j]])

        steps = []
        hk = 0
        for b in range(cfg.nseq):
            for h in range(NH_S):
                hp = hk % 2
                for g in range(NG):
                    kmax = 4 * g + 3
                    for kb in range(kmax, -1, -1):
                        i = kb - 4 * g
                        n0 = max(i, 0) * 128
                        steps.append(dict(b=b, h=h, hp=hp, g=g, kb=kb, n0=n0, diag=(i >= 0),
                                          first=(kb == kmax), last=(kb == 0), newhead=(g == 0 and kb == kmax),
                                          gk=None))
                hk += 1
        gk = -1
        for st in steps:
            if st["first"]:
                gk += 1
            st["gp"] = gk % 2
        ost_k = [0]

        def load_head(st):
            b, h, hp = st["b"], st["h"], st["hp"]
            c0 = b * S
            P.dma("pool", ka[hp][0:64, :], self.skT[h * 64:(h + 1) * 64, c0:c0 + S], b_ka[hp], writes=[b_ka[hp]])
            P.dma("pool", qz[hp][0:64, :], self.sqT[h * 64:(h + 1) * 64, c0:c0 + S], b_qz[hp], writes=[b_qz[hp]])
            for j in range(NQA):
                P.dma("pool", qa[hp][j][0:64, :], self.sqT[h * 64:(h + 1) * 64, c0:c0 + S], b_qa[hp][j], writes=[b_qa[hp][j]])

        def emit_p1(j):
            st = steps[j]
            hp, kb, n0 = st["hp"], st["kb"], st["n0"]
            c0 = st["g"] * 512
            zb = j % 2
            P.op("pe", lambda e: e.matmul(zps[zb][:, n0:512], ka[hp][:, kb * 128:(kb + 1) * 128],
                                           qz[hp][:, c0 + n0:c0 + 512], start=True, stop=True),
                 reads=[b_ka[hp], b_qz[hp]], writes=[b_z[zb]])

        def emit_a12(j):
            st = steps[j]
            n0 = st["n0"]
            zb = j % 2; eb = j % 2; sb = j % NSP
            P.op("act", lambda e: e.activation(out=e1[eb][:, n0:512], in_=zps[zb][:, n0:512], func=AF.Exp),
                 reads=[b_z[zb]], writes=[b_e1[eb]])
            P.op("act", lambda e: e.activation(out=sp[sb][:, n0:512], in_=e1[eb][:, n0:512], func=AF.Ln, bias=1.0, scale=1.0),
                 reads=[b_e1[eb]], writes=[b_sp[sb]])
            if st["diag"]:
                P.op("dve", lambda e: e.tensor_tensor(out=sp[sb][:, n0:n0 + 128], in0=sp[sb][:, n0:n0 + 128],
                                                       in1=self.mask_st[:], op=ALU.mult), writes=[b_sp[sb]])

        def emit_carry(j):
            st = steps[j]
            n0 = st["n0"]
            sb = j % NSP
            hp, gp = st["hp"], st["gp"]
            c0 = st["g"] * 512
            par = j % NQA
            if st["first"]:
                P.op("pe", lambda e: e.matmul(cps[gp][:, :], self.zeros_b[:, 0:128], self.zeros_b[:, 0:512], start=True, stop=True),
                     writes=[b_c[gp]])
            P.op("dve", lambda e: e.tensor_copy(out=qa[hp][par][64:65, c0 + n0:c0 + 512], in_=cps[gp][64:65, n0:512]),
                 reads=[b_c[gp]], writes=[b_qa[hp][par]])
            P.op("pe", lambda e: e.matmul(cps[gp][:, n0:512], self.neg_col[:, :], sp[sb][:, n0:512], start=False, stop=True,
                                           skip_group_check=True),
                 reads=[b_sp[sb]], writes=[b_c[gp]])

        def emit_main(j):
            st = steps[j]
            hp, kb, n0, gp, h = st["hp"], st["kb"], st["n0"], st["gp"], st["h"]
            c0 = st["g"] * 512
            par = j % NQA; sb = j % NSP; gb = j % 2; ab = j % NSP
            if st["first"]:
                P.op("pe", lambda e: e.matmul(ops_[gp][:, :], self.zeros_b[:, 0:128], self.zeros_b[:, 0:512], start=True, stop=True),
                     writes=[b_o[gp]])
            P.op("pe", lambda e: e.matmul(gps[gb][:, n0:512], ka[hp][:, kb * 128:(kb + 1) * 128],
                                           qa[hp][par][:, c0 + n0:c0 + 512], start=True, stop=False),
                 reads=[b_ka[hp], b_qa[hp][par]], writes=[b_g[gb]], signal=False)
            P.op("pe", lambda e: e.matmul(gps[gb][:, n0:512], self.tri_neg[:], sp[sb][:, n0:512], start=False, stop=True),
                 reads=[b_sp[sb]], writes=[b_g[gb]])

        def emit_a3(j):
            st = steps[j]
            n0 = st["n0"]
            gb = j % 2; ab = j % NSP
            P.op("act", lambda e: e.activation(out=At[ab][:, n0:512], in_=gps[gb][:, n0:512], func=AF.Exp),
                 reads=[b_g[gb]], writes=[b_A[ab]])
            if st["diag"]:
                P.op("dve", lambda e: e.tensor_tensor(out=At[ab][:, n0:n0 + 128], in0=At[ab][:, n0:n0 + 128],
                                                       in1=self.mask_st[:], op=ALU.mult), writes=[b_A[ab]])

        def emit_p4(j):
            st = steps[j]
            kb, n0, gp, h, b = st["kb"], st["n0"], st["gp"], st["h"], st["b"]
            ab = j % NSP
            if st["newhead"] and h == 0:
                for q0 in range(0, NB, 8):
                    P.dma("sp", vall[:, q0:q0 + 8, :],
                          self.sv_tm[b * S + q0 * 128:b * S + (q0 + 8) * 128, :].rearrange("(n p) d -> p n d", p=128),
                          b_v, writes=[b_v])
            P.op("pe", lambda e: e.matmul(ops_[gp][:, n0:512], vall[:, kb, (h // 2) * 128:(h // 2 + 1) * 128], At[ab][:, n0:512],
                                           start=False, stop=True, skip_group_check=True),
                 reads=[b_v, b_A[ab]], writes=[b_o[gp]])
            if st["last"]:
                k = ost_k[0] % 2; ost_k[0] += 1
                c0 = b * S + st["g"] * 512
                P.op("dve", lambda e: e.tensor_copy(out=ost[k][:], in_=ops_[gp][(h % 2) * 64:(h % 2) * 64 + 64, :]), reads=[b_o[gp]], writes=[b_ost[k]])
                P.dma("sp", self.oT[DH + h * 64:DH + (h + 1) * 64, c0:c0 + 512], ost[k][:], b_ost[k], reads=[b_ost[k]])

        n = len(steps)
        ok = lambda m: 0 <= m < n
        per_head = n // (cfg.nseq * NH_S)
        LOOKH = max(3, min(110, per_head - 8))
        for j in range(0, min(LOOKH, n)):
            if steps[j]["newhead"]:
                load_head(steps[j])
        for j in range(-3, n + 1):
            if ok(j + LOOKH) and steps[j + LOOKH]["newhead"] and j + LOOKH >= LOOKH:
                load_head(steps[j + LOOKH])
            if ok(j + 3):
                emit_p1(j + 3)
            if ok(j + 1):
                emit_carry(j + 1)
            if ok(j):
                emit_main(j)
            if ok(j - 1):
                emit_p4(j - 1)
            if ok(j + 2):
                emit_a12(j + 2)
            if ok(j):
                emit_a3(j)
        P.run()

    def phase_hgrn(self, l):
        nc, cfg = self.nc, self.cfg
        P = Prog(nc, f"h{l}")
        S = cfg.S
        ns = cfg.nseq
        ntile = S // 128
        R = 31
        NS_ = 4 * ns
        gbc = P.sbuf("gbc", [128, DH], F32); b_gbc = Buf()
        P.dma("sp", gbc[:], bcast_rows(self.hgain[l, :], 128), b_gbc, writes=[b_gbc])
        NT_ = 2 * ns
        def many(name, n, shape, dt):
            return [P.sbuf(f"{name}{i}", shape, dt) for i in range(n)], [Buf() for _ in range(n)]
        tf, b_tf = many("tf", NT_, [128, 4, 128], F32)
        qs, b_qs = many("qs", NT_, [128, 4, 128], F32)
        vt, b_vt = many("vt", NT_, [128, DH], BF16)
        gg, b_gg = many("gg", NT_, [128, DH], F32)
        ss, b_ss = many("ss", NT_, [128, 4], F32)
        on, b_on = many("on", NT_, [128, DH], F32)
        onb, b_onb = many("onb", NT_, [128, DH], BF16)
        oTst, b_oTst = many("oTst", NT_, [128, 4, 128], BF16)
        ff, b_ff = many("ff", NS_, [128, 128], F32)
        lf, b_lf = many("lf", NS_, [128, 128], F32)
        kk, b_kk = many("kk", NS_, [128, 128], F32)
        bb, b_bb = many("bb", NS_, [128, 128], F32)
        eq, b_eq = many("eq", NS_, [128, 128], F32)
        ek, b_ek = many("ek", NS_, [128, 128], F32)
        sc_, b_sc = many("sc", NS_, [128, 8], F32)
        qzp, b_qzp = many("qzp", NS_, [128, 384], BF16)
        kt, b_kt = many("kt", NS_, [128, 128], BF16)
        ktok, b_ktok = many("ktok", NS_, [128, 128], BF16)
        pT, b_pT = many("pT", NS_, [128, 128], BF16)
        sq_, b_sq = many("sqj", NS_, [128, 128], F32)
        tmp, b_tmp = many("tmp", NS_, [128, 128], F32)
        St, b_St = many("St", NS_, [128, 128], F32)
        Sb, b_Sb = many("Sb", NS_, [128, 128], BF16)
        osb, b_osb = many("osb", NS_, [128, 128], F32)
        ones = self.ones_f
        _pkt = P.psum("pkt", [128, 1024], BF16); b_pkt = Buf(psum=True)
        p_kt = _pkt[:, 0:128]
        _psc = [P.psum(f"psc{i}", [128, 512]) for i in range(2)]; b_psc = [Buf(psum=True), Buf(psum=True)]
        p_sc = [t[:, 0:128] for t in _psc]
        _po = [P.psum(f"po{i}", [128, 512]) for i in range(2)]; b_po = [Buf(psum=True), Buf(psum=True)]
        p_o = [t[:, 0:128] for t in _po]
        _pkv = [P.psum(f"pkv{i}", [128, 512]) for i in range(2)]; b_pkv = [Buf(psum=True), Buf(psum=True)]
        p_kv = [t[:, 0:128] for t in _pkv]
        _pot = P.psum("pot", [128, 1024], BF16); b_pot = Buf(psum=True)
        p_ot = _pot[:, 0:512].rearrange("p (h t) -> p h t", t=128)
        for i in range(NS_):
            P.op("pool", lambda e, i=i: e.memset(qzp[i][:], 0.0), writes=[b_qzp[i]])
            P.op("pool", lambda e, i=i: e.memset(St[i][:], 0.0), writes=[b_St[i]])

        def head_stream(b, h, ti, tk):
            sk = b * 4 + h
            pp = sk % 2
            A_ = self.lbA[:, l * 4 + h:l * 4 + h + 1]
            B_ = self.lbB[:, l * 4 + h:l * 4 + h + 1]
            P.op("dve", lambda e: e.tensor_scalar(out=ff[sk][:], in0=tf[tk][:, h, :], scalar1=A_, scalar2=B_,
                                                   op0=ALU.mult, op1=ALU.add), reads=[b_tf[tk]], writes=[b_ff[sk]])
            yield
            P.op("act", lambda e: e.activation(out=lf[sk][:], in_=ff[sk][:], func=AF.Ln), reads=[b_ff[sk]], writes=[b_lf[sk]])
            P.op("pool", lambda e: e.tensor_scalar(out=kk[sk][:], in0=ff[sk][:], scalar1=-1.0, scalar2=1.0,
                                                    op0=ALU.mult, op1=ALU.add), reads=[b_ff[sk]], writes=[b_kk[sk]])
            yield
            for c in range(2):
                P.op("dve", lambda e, c=c: e.tensor_tensor_scan(
                    out=bb[sk][:, c * 64:(c + 1) * 64], data0=ones[:, 0:64], data1=lf[sk][:, c * 64:(c + 1) * 64],
                    initial=0.0, op0=ALU.mult, op1=ALU.add), reads=[b_lf[sk]], writes=[b_bb[sk]])
            yield
            for c in range(2):
                P.op("dve", lambda e, c=c: e.tensor_scalar(
                    out=sc_[sk][:, 3 * c:3 * c + 1], in0=bb[sk][:, c * 64 + R:c * 64 + R + 1], scalar1=-1.0, scalar2=None,
                    op0=ALU.mult), reads=[b_bb[sk]], writes=[b_sc[sk]])
            yield
            for c in range(2):
                br = bb[sk][:, c * 64 + R:c * 64 + R + 1]
                nbr = sc_[sk][:, 3 * c:3 * c + 1]
                P.op("act", lambda e, c=c, nbr=nbr: e.activation(
                    out=eq[sk][:, c * 64:(c + 1) * 64], in_=bb[sk][:, c * 64:(c + 1) * 64], func=AF.Exp, bias=nbr, scale=1.0),
                    reads=[b_bb[sk], b_sc[sk]], writes=[b_eq[sk]])
                P.op("act", lambda e, c=c, br=br: e.activation(
                    out=ek[sk][:, c * 64:(c + 1) * 64], in_=bb[sk][:, c * 64:(c + 1) * 64], func=AF.Exp, bias=br, scale=-1.0),
                    reads=[b_bb[sk]], writes=[b_ek[sk]])
                P.op("act", lambda e, c=c, br=br: e.activation(
                    out=sc_[sk][:, 3 * c + 1:3 * c + 2], in_=br, func=AF.Exp), reads=[b_bb[sk]], writes=[b_sc[sk]])
                P.op("act", lambda e, c=c: e.activation(
                    out=sc_[sk][:, 3 * c + 2:3 * c + 3], in_=bb[sk][:, c * 64 + 63:c * 64 + 64], func=AF.Exp),
                    reads=[b_bb[sk]], writes=[b_sc[sk]])
            yield
            P.op("dve", lambda e: e.tensor_tensor(
                out=qzp[sk][:].rearrange("p (c x) -> p c x", x=192)[:, :, 0:64],
                in0=qs[tk][:, h, :].rearrange("p (c x) -> p c x", x=64),
                in1=eq[sk][:].rearrange("p (c x) -> p c x", x=64), op=ALU.mult),
                reads=[b_qs[tk], b_eq[sk]], writes=[b_qzp[sk]])
            P.op("pool", lambda e: e.tensor_tensor(out=kt[sk][:], in0=kk[sk][:], in1=ek[sk][:], op=ALU.mult),
                 reads=[b_kk[sk], b_ek[sk]], writes=[b_kt[sk]])
            yield
            P.op("pe", lambda e: e.transpose(p_kt, kt[sk][:], self.identb[:]), reads=[b_kt[sk]], writes=[b_pkt])
            P.op("act", lambda e: e.activation(out=ktok[sk][:], in_=p_kt, func=AF.Copy), reads=[b_pkt], writes=[b_ktok[sk]])
            P.op("pe", lambda e: e.matmul(
                p_sc[pp].rearrange("p (c x) -> p c x", x=64), kt[sk][:],
                qzp[sk][:].rearrange("p (c x) -> p c x", x=192)[:, :, 0:64], start=True, stop=True),
                reads=[b_kt[sk], b_qzp[sk]], writes=[b_psc[pp]])
            P.op("dve", lambda e: e.tensor_scalar(out=sq_[sk][:], in0=p_sc[pp], scalar1=1e30, scalar2=-1e30,
                                                   op0=ALU.min, op1=ALU.max), reads=[b_psc[pp]], writes=[b_sq[sk]])
            yield
            P.op("dve", lambda e: e.tensor_tensor(out=pT[sk][:], in0=sq_[sk][:], in1=self.mask_bd[:], op=ALU.mult),
                 reads=[b_sq[sk]], writes=[b_pT[sk]])
            P.op("dve", lambda e: e.tensor_scalar(out=Sb[sk][:], in0=St[sk][:], scalar1=sc_[sk][:, 1:2], scalar2=None,
                                                   op0=ALU.mult), reads=[b_St[sk], b_sc[sk]], writes=[b_Sb[sk]])
            yield
            for c in range(2):
                if c == 0:
                    P.op("pe", lambda e: e.matmul(p_o[pp], pT[sk][:], vt[tk][:, h * 128:(h + 1) * 128], start=True, stop=False),
                         reads=[b_pT[sk], b_vt[tk]], writes=[b_po[pp]], signal=False)
                P.op("pe", lambda e, c=c: e.matmul(
                    p_o[pp], qzp[sk][:, c * 128:(c + 1) * 128], Sb[sk][:], start=(c == 1), stop=True),
                    reads=[b_qzp[sk], b_Sb[sk]], writes=[b_po[pp]])
                P.op("pe", lambda e, c=c: e.matmul(
                    p_kv[pp], ktok[sk][c * 64:(c + 1) * 64, :], vt[tk][c * 64:(c + 1) * 64, h * 128:(h + 1) * 128],
                    start=True, stop=True), reads=[b_ktok[sk], b_vt[tk]], writes=[b_pkv[pp]])
                if c == 0:
                    P.op("act", lambda e: e.activation(out=osb[sk][:], in_=p_o[pp], func=AF.Copy),
                         reads=[b_po[pp]], writes=[b_osb[sk]])
                else:
                    P.op("dve", lambda e: e.tensor_tensor(out=osb[sk][:], in0=p_o[pp], in1=osb[sk][:], op=ALU.add),
                         reads=[b_po[pp]], writes=[b_osb[sk]])
                P.op("dve", lambda e, c=c: e.tensor_scalar(
                    out=tmp[sk][:], in0=p_kv[pp], scalar1=eq[sk][:, c * 64 + 63:c * 64 + 64], scalar2=None, op0=ALU.mult),
                    reads=[b_pkv[pp], b_eq[sk]], writes=[b_tmp[sk]])
                yield
                P.op("dve", lambda e, c=c: e.scalar_tensor_tensor(
                    out=St[sk][:], in0=St[sk][:], scalar=sc_[sk][:, 3 * c + 2:3 * c + 3], in1=tmp[sk][:],
                    op0=ALU.mult, op1=ALU.add), reads=[b_tmp[sk], b_sc[sk]], writes=[b_St[sk]])
                yield
                if c == 0:
                    P.op("dve", lambda e: e.tensor_scalar(out=Sb[sk][:], in0=St[sk][:], scalar1=sc_[sk][:, 4:5], scalar2=None,
                                                           op0=ALU.mult), reads=[b_St[sk], b_sc[sk]], writes=[b_Sb[sk]])
                    yield
            P.op("act", lambda e: e.activation(out=sq_[sk][:], in_=osb[sk][:], func=AF.Square, accum_out=ss[tk][:, h:h + 1]),
                 reads=[b_osb[sk]], writes=[b_sq[sk], b_ss[tk]])
            P.op("pool", lambda e: e.tensor_tensor(out=on[tk][:, h * 128:(h + 1) * 128], in0=osb[sk][:],
                                                    in1=gg[tk][:, h * 128:(h + 1) * 128], op=ALU.mult),
                 reads=[b_osb[sk], b_gg[tk]], writes=[b_on[tk]])
            yield

        def tile_tail(b, ti, tk):
            r0 = b * S + ti * 128
            P.op("dve", lambda e: e.tensor_scalar(out=ss[tk][:], in0=ss[tk][:], scalar1=1.0 / 128.0, scalar2=RMS_EPS,
                                                   op0=ALU.mult, op1=ALU.add), reads=[b_ss[tk]], writes=[b_ss[tk]])
            P.op("act", lambda e: e.activation(out=ss[tk][:], in_=ss[tk][:], func=AF.Ln), reads=[b_ss[tk]], writes=[b_ss[tk]])
            P.op("act", lambda e: e.activation(out=ss[tk][:], in_=ss[tk][:], func=AF.Exp, scale=-0.5),
                 reads=[b_ss[tk]], writes=[b_ss[tk]])
            for h in range(4):
                P.op("dve", lambda e, h=h: e.tensor_scalar(
                    out=onb[tk][:, h * 128:(h + 1) * 128], in0=on[tk][:, h * 128:(h + 1) * 128], scalar1=ss[tk][:, h:h + 1],
                    scalar2=None, op0=ALU.mult), reads=[b_ss[tk], b_on[tk]], writes=[b_onb[tk]])
            for h in range(4):
                P.op("pe", lambda e, h=h: e.transpose(p_ot[:, h, :], onb[tk][:, h * 128:(h + 1) * 128], self.identb[:]),
                     reads=[b_onb[tk]], writes=[b_pot], signal=(h == 3))
            P.op("act", lambda e: e.activation(out=oTst[tk][:], in_=p_ot, func=AF.Copy), reads=[b_pot], writes=[b_oTst[tk]])
            P.dma("sp", self.oT[0:DH, r0:r0 + 128].rearrange("(h p) t -> p h t", p=128), oTst[tk][:], b_oTst[tk],
                  reads=[b_oTst[tk]])

        def tile_loads(b, ti, tk):
            r0 = b * S + ti * 128
            P.dma("sp", tf[tk][:], self.hfT[:, r0:r0 + 128].rearrange("(h p) t -> p h t", p=128), b_tf[tk], writes=[b_tf[tk]])
            P.dma("sp", qs[tk][:], self.hqT[:, r0:r0 + 128].rearrange("(h p) t -> p h t", p=128), b_qs[tk], writes=[b_qs[tk]])
            P.dma("sp", vt[tk][:], self.hi_tm[r0:r0 + 128, :], b_vt[tk], writes=[b_vt[tk]])
            P.dma("sp", gg[tk][:], self.hg_tm[r0:r0 + 128, :], b_gg[tk], writes=[b_gg[tk]])
            P.op("pool", lambda e: e.tensor_tensor(out=gg[tk][:], in0=gg[tk][:], in1=gbc[:], op=ALU.mult),
                 reads=[b_gbc], writes=[b_gg[tk]])

        for b in range(ns):
            tile_loads(b, 0, b)
        pending = []
        for ti in range(ntile):
            tks = [(ti % 2) * ns + b for b in range(ns)]
            if ti + 1 < ntile:
                for b in range(ns):
                    tile_loads(b, ti + 1, ((ti + 1) % 2) * ns + b)
            gens = [head_stream(b, h, ti, tks[b]) for h in range(4) for b in range(ns)]
            alive = list(gens)
            rnd = 0
            while alive:
                nxt = []
                for g in alive:
                    try:
                        next(g)
                        nxt.append(g)
                    except StopIteration:
                        pass
                alive = nxt
                rnd += 1
                if rnd == 4 and pending:
                    for args in pending:
                        tile_tail(*args)
                    pending = []
            for args in pending:
                tile_tail(*args)
            pending = [(b, ti, tks[b]) for b in range(ns)]
        for args in pending:
            tile_tail(*args)
        P.run()

    def rstd_from_var(self, P, mv, b_mv, rstd, b_rstd, n):
        P.op("dve", lambda e: e.tensor_scalar(out=rstd[:, 0:n], in0=mv[:, 0:n, 1], scalar1=LN_EPS, scalar2=None,
                                               op0=ALU.add), reads=[b_mv], writes=[b_rstd])
        P.op("act", lambda e: e.activation(out=rstd[:, 0:n], in_=rstd[:, 0:n], func=AF.Ln), reads=[b_rstd], writes=[b_rstd])
        P.op("act", lambda e: e.activation(out=rstd[:, 0:n], in_=rstd[:, 0:n], func=AF.Exp, scale=-0.5),
             reads=[b_rstd], writes=[b_rstd])

    def router_top2(self, P, lps, b_lps, rt, b_rt, comb, b_c, i):
        L, m1, k1, L2, m2, k2, dd, p1 = (rt[:, q, :] for q in range(8))
        P.op("dve", lambda e: e.tensor_copy(out=L, in_=lps[:, 0:NE]), reads=[b_lps], writes=[b_rt])
        P.op("dve", lambda e: e.tensor_reduce(out=m1[:, 0:1], in_=L, axis=AX.X, op=ALU.max), reads=[b_rt], writes=[b_rt])
        P.op("dve", lambda e: e.tensor_scalar(out=k1, in0=L, scalar1=m1[:, 0:1], scalar2=None, op0=ALU.is_equal),
             reads=[b_rt], writes=[b_rt])
        P.op("dve", lambda e: e.scalar_tensor_tensor(out=L2, in0=k1, scalar=-1e30, in1=L, op0=ALU.mult, op1=ALU.add),
             reads=[b_rt], writes=[b_rt])
        P.op("dve", lambda e: e.tensor_reduce(out=m2[:, 0:1], in_=L2, axis=AX.X, op=ALU.max), reads=[b_rt], writes=[b_rt])
        P.op("dve", lambda e: e.tensor_scalar(out=k2, in0=L2, scalar1=m2[:, 0:1], scalar2=None, op0=ALU.is_equal),
             reads=[b_rt], writes=[b_rt])
        P.op("dve", lambda e: e.tensor_tensor(out=dd[:, 0:1], in0=m2[:, 0:1], in1=m1[:, 0:1], op=ALU.subtract),
             reads=[b_rt], writes=[b_rt])
        P.op("act", lambda e: e.activation(out=dd[:, 1:2], in_=dd[:, 0:1], func=AF.Exp), reads=[b_rt], writes=[b_rt])
        P.op("dve", lambda e: e.tensor_scalar(out=dd[:, 2:3], in0=dd[:, 1:2], scalar1=1.0, scalar2=None, op0=ALU.add),
             reads=[b_rt], writes=[b_rt])
        P.op("dve", lambda e: e.reciprocal(out=p1[:, 0:1], in_=dd[:, 2:3]), reads=[b_rt], writes=[b_rt])
        P.op("dve", lambda e: e.tensor_scalar(out=p1[:, 1:2], in0=p1[:, 0:1], scalar1=-1.0, scalar2=1.0,
                                               op0=ALU.mult, op1=ALU.add), reads=[b_rt], writes=[b_rt])
        P.op("dve", lambda e: e.tensor_scalar(out=comb[:, i, :], in0=k1, scalar1=p1[:, 0:1], scalar2=None, op0=ALU.mult),
             reads=[b_rt], writes=[b_c])
        P.op("dve", lambda e: e.scalar_tensor_tensor(out=comb[:, i, :], in0=k2, scalar=p1[:, 1:2], in1=comb[:, i, :],
                                                      op0=ALU.mult, op1=ALU.add), reads=[b_rt], writes=[b_c])


def const_tables():
    s = np.arange(128)[:, None]
    t = np.arange(128)[None, :]
    c = np.zeros((128, 5, 128), np.float32)
    c[:, 0, :] = (s == t)
    c[:, 1, :] = ((s // CHUNK) == (t // CHUNK)) & (s <= t)
    c[:, 2, :] = (s < t)
    c[:, 3, :] = -1.0 * (s >= t)
    c[:, 4, :] = -1.0 * (t == 64)
    return c


def core_inputs(inp, b0, nseq, S):
    f = lambda a: np.ascontiguousarray(np.asarray(a, dtype=np.float32))
    x = f(inp["x"])[b0:b0 + nseq, :S].reshape(nseq * S, D)
    c = f(inp["c"])[b0:b0 + nseq]
    cT = np.ascontiguousarray(c.T.reshape(8, 128, nseq).transpose(1, 0, 2))
    lb = f(inp["hgrn_lb_logits"])
    lbT = np.ascontiguousarray(lb.T.reshape(NH_H, 128, DEPTH).transpose(1, 0, 2))
    m = {
        "x": np.ascontiguousarray(x), "cT": cT, "lbT": lbT, "consts": const_tables(),
        "w_ada": f(inp["w_ada"]), "b_ada": f(inp["b_ada"]), "w_in": f(inp["w_in"]), "w_out": f(inp["w_out"]),
        "hgain": f(inp["hgrn_norm_gain"]),
        "w_dense_gate": f(inp["w_dense_gate"]), "w_dense_up": f(inp["w_dense_up"]),
        "w_dense_down": f(inp["w_dense_down"]),
        "w_router": np.ascontiguousarray(f(inp["w_router"]).reshape(2, 8, 128, NE).transpose(0, 2, 1, 3)),
        "w_moe_gate": f(inp["w_moe_gate"]), "w_moe_up": f(inp["w_moe_up"]), "w_moe_down": f(inp["w_moe_down"]),
        "ln_gain": f(inp["ln_gain"]), "ln_bias": f(inp["ln_bias"]),
    }
    return m


def kernel(**inputs):
    cfg = Cfg()
    nc = Builder(cfg).build()
    in_maps = [core_inputs(inputs, 2 * c, 2, 4096) for c in range(NCORES)]
    res = run_bass_kernel_spmd(nc, in_maps, core_ids=list(range(NCORES)))
    outs = [np.asarray(r["out"], dtype=np.float32).reshape(2, 4096, D) for r in res.results]
    return np.concatenate(outs, axis=0)
```
